# Optimizing a Trainium2 kernel written in Bass

```python
import math
import jax, jax.numpy as jnp
from jax import lax
import numpy as np

D_MODEL = 1024
BATCH = 4
SEQ = 4096
DEPTH = 1

HEAD_DIM = 64
D_MIX = D_MODEL
D_ATTN = D_MIX // 2
D_SGU = D_MIX - D_ATTN
N_ATTN_HEADS = D_ATTN // HEAD_DIM
N_SGU_GROUPS = D_SGU // HEAD_DIM
SGU_GROUP_DIM = D_SGU // N_SGU_GROUPS
D_IN_PROJ = 3 * D_ATTN + 2 * D_SGU
DILATED_BRANCHES = ((128, 1), (512, 4), (2048, 16))
SGU_CHUNK = 128
D_FF = 4 * D_MODEL
N_REL_BUCKETS = 32
REL_MAX_DISTANCE = 1024
RMS_EPS = 1e-6
LN_EPS = 1e-5
NEG_INF = -1e30

kernel_name = "hymba_dilated_attn_gmlp_sandwich"


def rms_norm(x, g):
    xf = x.astype(jnp.float32)
    y = xf * lax.rsqrt(jnp.mean(xf * xf, axis=-1, keepdims=True) + RMS_EPS)
    return (y * g.astype(jnp.float32)).astype(x.dtype)


def layer_norm(x, g, b):
    xf = x.astype(jnp.float32)
    mu = jnp.mean(xf, axis=-1, keepdims=True)
    xc = xf - mu
    y = xc * lax.rsqrt(jnp.mean(xc * xc, axis=-1, keepdims=True) + LN_EPS)
    return (y * g.astype(jnp.float32) + b.astype(jnp.float32)).astype(x.dtype)


def t5_bucket(rel):
    half = N_REL_BUCKETS // 2
    max_exact = half // 2
    ret = jnp.where(rel > 0, half, 0)
    n = jnp.abs(rel)
    nf = jnp.maximum(n, 1).astype(jnp.float32)
    large = max_exact + (jnp.log(nf / max_exact) / math.log(REL_MAX_DISTANCE / max_exact)
                         * (half - max_exact)).astype(jnp.int32)
    large = jnp.minimum(large, half - 1)
    return ret + jnp.where(n < max_exact, n, large)


def dilated_window_branch(q, k, v, rel_bias, window, dil):
    B, S, H, Dh = q.shape
    half = window // (2 * dil)
    blk = half
    L = S // dil
    nb = -(-L // blk)
    Lp = nb * blk

    def to_blocks(t):
        t = t.reshape(B, L, dil, H, Dh).transpose(0, 2, 1, 3, 4)
        t = jnp.pad(t, ((0, 0), (0, 0), (0, Lp - L), (0, 0), (0, 0)))
        return t.reshape(B, dil, nb, blk, H, Dh)

    def neighbourhood(t):
        tp = jnp.pad(t, ((0, 0), (0, 0), (1, 1), (0, 0), (0, 0), (0, 0)))
        return jnp.concatenate([tp[:, :, :-2], tp[:, :, 1:-1], tp[:, :, 2:]], axis=3)

    qb = to_blocks(q)
    kw = neighbourhood(to_blocks(k))
    vw = neighbourhood(to_blocks(v))

    q_local = jnp.arange(blk)
    k_local = jnp.arange(3 * blk) - blk
    rel = k_local[None, :] - q_local[:, None]
    bias = rel_bias[t5_bucket(rel * dil)].astype(jnp.float32)
    k_idx = jnp.arange(nb)[:, None] * blk + k_local[None, :]
    valid = (jnp.abs(rel) <= half)[None] & ((k_idx >= 0) & (k_idx < L))[:, None, :]

    s = jnp.einsum('brnqhd,brnkhd->brnhqk', qb, kw).astype(jnp.float32)
    s = s + bias.transpose(2, 0, 1)
    s = jnp.where(valid[:, None], s, NEG_INF)
    m = jnp.max(s, axis=-1, keepdims=True)
    p = jnp.exp(s - m)
    den = jnp.sum(p, axis=-1, keepdims=True)
    o = jnp.einsum('brnhqk,brnkhd->brnqhd', (p / den).astype(v.dtype), vw)
    lse = (m + jnp.log(den))[..., 0].transpose(0, 1, 2, 4, 3)

    o = o.reshape(B, dil, Lp, H, Dh)[:, :, :L].transpose(0, 2, 1, 3, 4).reshape(B, S, H, Dh)
    lse = lse.reshape(B, dil, Lp, H)[:, :, :L].transpose(0, 2, 1, 3).reshape(B, S, H)
    return o, lse


def dilated_attention(q, k, v, rel_bias):
    outs, lses = [], []
    for window, dil in DILATED_BRANCHES:
        o, l = dilated_window_branch(q, k, v, rel_bias, window, dil)
        outs.append(o)
        lses.append(l)
    w = jax.nn.softmax(jnp.stack(lses, axis=0), axis=0)
    return jnp.einsum('ibsh,ibshd->bshd', w.astype(q.dtype), jnp.stack(outs, axis=0))


def spatial_gating(u, v, ln_g, ln_b, w_s, b_s):
    B, S, _ = u.shape
    nc = S // SGU_CHUNK
    v = layer_norm(v, ln_g, ln_b).reshape(B, nc, SGU_CHUNK, N_SGU_GROUPS, SGU_GROUP_DIM)
    mixed = jnp.einsum('gts,bnsgc->bntgc', w_s, v) + b_s.T[None, None, :, :, None]
    return u * mixed.reshape(B, S, D_SGU)


def hybrid_layer(x, g_pre_mix, w_in, sgu_ln_g, sgu_ln_b, sgu_w, sgu_b, w_out, g_post_mix,
                 g_pre_ffn, w_ff1, w_ff2, g_post_ffn, rel_bias):
    B, S, _ = x.shape
    h = rms_norm(x, g_pre_mix)
    z = h @ w_in
    q, k, v, zu, zv = jnp.split(
        z, [D_ATTN, 2 * D_ATTN, 3 * D_ATTN, 3 * D_ATTN + D_SGU], axis=-1)
    q = q.reshape(B, S, N_ATTN_HEADS, HEAD_DIM) * (HEAD_DIM ** -0.5)
    k = k.reshape(B, S, N_ATTN_HEADS, HEAD_DIM)
    v = v.reshape(B, S, N_ATTN_HEADS, HEAD_DIM)
    attn = dilated_attention(q, k, v, rel_bias).reshape(B, S, D_ATTN)
    sgu = spatial_gating(jax.nn.gelu(zu), jax.nn.gelu(zv), sgu_ln_g, sgu_ln_b, sgu_w, sgu_b)
    mix = jnp.concatenate([attn, sgu], axis=-1) @ w_out
    x = x + rms_norm(mix, g_post_mix)
    h = rms_norm(x, g_pre_ffn)
    f = jnp.square(jax.nn.relu(h @ w_ff1)) @ w_ff2
    return x + rms_norm(f, g_post_ffn)


def setup_inputs(seed: int = 0) -> dict:
    key = jax.random.key(seed)
    ks = jax.random.split(key, 16)
    f32 = jnp.float32

    def nrm(k, shape, scale):
        return jax.random.normal(k, shape, f32) * scale

    def gain(k, shape):
        return 1.0 + 0.05 * jax.random.normal(k, shape, f32)

    return {
        "x": jax.random.normal(ks[0], (BATCH, SEQ, D_MODEL), f32),
        "g_pre_mix": gain(ks[1], (DEPTH, D_MODEL)),
        "w_in": nrm(ks[2], (DEPTH, D_MODEL, D_IN_PROJ), D_MODEL ** -0.5),
        "sgu_ln_g": gain(ks[3], (DEPTH, D_SGU)),
        "sgu_ln_b": nrm(ks[4], (DEPTH, D_SGU), 0.02),
        "sgu_w": nrm(ks[5], (DEPTH, N_SGU_GROUPS, SGU_CHUNK, SGU_CHUNK), SGU_CHUNK ** -0.5),
        "sgu_b": gain(ks[6], (DEPTH, N_SGU_GROUPS, SGU_CHUNK)),
        "w_out": nrm(ks[7], (DEPTH, D_MIX, D_MODEL), D_MIX ** -0.5),
        "g_post_mix": gain(ks[8], (DEPTH, D_MODEL)),
        "g_pre_ffn": gain(ks[9], (DEPTH, D_MODEL)),
        "w_ff1": nrm(ks[10], (DEPTH, D_MODEL, D_FF), D_MODEL ** -0.5),
        "w_ff2": nrm(ks[11], (DEPTH, D_FF, D_MODEL), D_FF ** -0.5),
        "g_post_ffn": gain(ks[12], (DEPTH, D_MODEL)),
        "rel_bias": nrm(ks[13], (N_REL_BUCKETS, N_ATTN_HEADS), 0.5),
    }


def reference(x, g_pre_mix, w_in, sgu_ln_g, sgu_ln_b, sgu_w, sgu_b, w_out, g_post_mix,
              g_pre_ffn, w_ff1, w_ff2, g_post_ffn, rel_bias):
    for layer in range(DEPTH):
        x = hybrid_layer(x, g_pre_mix[layer], w_in[layer], sgu_ln_g[layer], sgu_ln_b[layer],
                         sgu_w[layer], sgu_b[layer], w_out[layer], g_post_mix[layer],
                         g_pre_ffn[layer], w_ff1[layer], w_ff2[layer], g_post_ffn[layer],
                         rel_bias)
    return x
```

```python
import math
from contextlib import ExitStack

import numpy as np
import concourse.bass as bass
import concourse.mybir as mybir
from concourse.bass_utils import run_bass_kernel_spmd

F32 = mybir.dt.float32
BF16 = mybir.dt.bfloat16
AF = mybir.ActivationFunctionType
ALU = mybir.AluOpType
KB = 1024

D = 1024
S = 4096
NOWN = 2048
NLOC = 3072
PADK = 1024
RMS_EPS = 1e-6
LN_EPS = 1e-5
MASKV = -30000.0
STRICT_SAME_ENGINE = True


class Prog:
    def __init__(self, nc, n_dma_sems=8):
        self.nc = nc
        self.ops = []
        self.last_writer = {}
        self.readers = {}
        self.n_dma_sems = n_dma_sems
        self.bar_deps = set()
        self.since_bar_dma = []
        self.last_on = {}

    def add(self, eng, emit, reads=(), writes=(), dma=False, nobar=False):
        oid = len(self.ops)
        deps = set(self.bar_deps)
        for r in reads:
            if r in self.last_writer:
                deps.add(self.last_writer[r])
        for w in writes:
            if w in self.last_writer:
                deps.add(self.last_writer[w])
            for rd in self.readers.get(w, ()):
                deps.add(rd)
        for r in reads:
            self.readers.setdefault(r, []).append(oid)
        for w in writes:
            self.last_writer[w] = oid
            self.readers[w] = []
        deps.discard(oid)
        self.ops.append(dict(id=oid, eng=eng, emit=emit, deps=deps, dma=dma,
                             reads=tuple(reads), writes=tuple(writes)))
        if dma:
            if not nobar:
                self.since_bar_dma.append(oid)
        else:
            self.last_on[eng] = oid
        return oid

    def barrier(self):
        deps = set(self.since_bar_dma)
        for e, oid in self.last_on.items():
            deps.add(oid)
        self.bar_deps = deps
        self.since_bar_dma = []

    def finalize(self, stack):
        nc = self.nc
        ops = self.ops
        for op in ops:
            keep = set()
            for d in op["deps"]:
                dop = ops[d]
                if dop["dma"] or op["dma"]:
                    keep.add(d)
                    continue
                if dop["eng"] == op["eng"]:
                    if op["eng"] == "pe":
                        continue
                    if STRICT_SAME_ENGINE or (set(dop["writes"]) & set(op["reads"])):
                        keep.add(d)
                    continue
                keep.add(d)
            op["deps"] = keep
        signaled = set()
        for op in ops:
            signaled |= op["deps"]
        self.esem = {e: stack.enter_context(nc.semaphore("s_" + e)) for e in ("pe", "act", "dve", "pool")}
        self.dsem = {}
        for q in ("sp", "act", "pool"):
            self.dsem[q] = [stack.enter_context(nc.semaphore(f"d_{q}{i}")) for i in range(self.n_dma_sems)]
        cnt = {e: 0 for e in self.esem}
        dcnt = {q: [0] * self.n_dma_sems for q in self.dsem}
        dnext = {q: 0 for q in self.dsem}
        for op in ops:
            if op["dma"]:
                q = op["eng"]
                i = dnext[q] % self.n_dma_sems
                dnext[q] += 1
                op["prev_tok"] = (self.dsem[q][i], dcnt[q][i]) if dcnt[q][i] > 0 else None
                dcnt[q][i] += 16
                op["tok"] = (self.dsem[q][i], dcnt[q][i])
                op["sig"] = True
            elif op["id"] in signaled:
                cnt[op["eng"]] += 1
                op["tok"] = (self.esem[op["eng"]], cnt[op["eng"]])
                op["sig"] = True
            else:
                op["sig"] = False

    def run_engine(self, ename, eng):
        ops = self.ops
        waited = {}
        for op in ops:
            if op["eng"] != ename:
                continue
            need = {}
            toks = [ops[d]["tok"] for d in op["deps"]]
            if op["dma"] and op["prev_tok"] is not None:
                toks.append(op["prev_tok"])
            for (s, v) in toks:
                k = id(s)
                if k not in need or need[k][1] < v:
                    need[k] = (s, v)
            for k, (s, v) in need.items():
                if waited.get(k, 0) >= v:
                    continue
                eng.wait_ge(s, v)
                waited[k] = v
            ins = op["emit"](eng)
            if op["sig"]:
                s, v = op["tok"]
                ins.then_inc(s, 16 if op["dma"] else 1)

    def emit_all(self, block, final_ops):
        P = self

        def mk(ename):
            def f(eng):
                P.run_engine(ename, eng)
                if ename == "sp":
                    for oid in final_ops:
                        s, v = P.ops[oid]["tok"]
                        eng.wait_ge(s, v)
            return f
        block.tensor(mk("pe"))
        block.scalar(mk("act"))
        block.vector(mk("dve"))
        block.gpsimd(mk("pool"))
        block.sync(mk("sp"))


class Arena:
    def __init__(self, nc, stack, nbytes):
        self.nbytes = nbytes
        self.t = stack.enter_context(nc.sbuf_tensor("arena", [128, nbytes // 2], BF16))

    def view(self, off, dtype, shape):
        n = 1
        for s_ in shape:
            n *= s_
        esz = 4 if dtype == F32 else 2
        assert off % 4 == 0 and off + n * esz <= self.nbytes, (off, n * esz, self.nbytes)
        a = self.t[:, off // 2: off // 2 + n * esz // 2]
        if dtype == F32:
            a = a.bitcast(F32)
        if len(shape) == 2:
            a = a.rearrange("p (a b) -> p a b", b=shape[1])
        elif len(shape) == 3:
            a = a.rearrange("p (a b c) -> p a b c", b=shape[1], c=shape[2])
        return a


class Lay:
    def __init__(self, arena, start):
        self.arena = arena
        self.off = start

    def get(self, dtype, shape):
        n = 1
        for s_ in shape:
            n *= s_
        esz = 4 if dtype == F32 else 2
        v = self.arena.view(self.off, dtype, shape)
        self.off += (n * esz + 3) // 4 * 4
        return v


def seg_tiles(br, seg):
    tiles = []
    if br in (0, 1):
        dil = 1 if br == 0 else 4
        rel = [(0, 128, 128), (0, 256, 0), (128, 384, 0), (256, 512, 0), (384, 512, 0)]
        sb = [(0, 128), (0, 256), (1, 0), (1, 256), (0, 0)]
        for m in range(5):
            qlo, qhi, bc0 = rel[m]
            if br == 0:
                k0 = 512 * seg - 64 + 128 * m
                kstart, qstart = k0, 512 * seg + qlo
            else:
                K0 = -64 + 128 * m
                kstart, qstart = seg + 4 * K0, seg + 4 * qlo
            tiles.append(dict(kstart=kstart, kstep=dil, qlo=qlo, n=qhi - qlo, bc0=bc0, slot=m,
                              boundary=(kstart < 0), qstart=qstart, qstep=dil, sbank=sb[m][0], soff=sb[m][1]))
    else:
        for rr in range(4):
            r = 4 * seg + rr
            for m in range(2):
                K0 = -64 + 128 * m
                tiles.append(dict(kstart=r + 16 * K0, kstep=16, qlo=rr * 128, n=128, bc0=128 if m == 0 else 0,
                                  slot=rr * 2 + m, boundary=(m == 0), qstart=r, qstep=16,
                                  sbank=rr // 2, soff=(rr % 2) * 256 + (128 if m == 0 else 0)))
    return tiles


def build_nc(debug=False):
    nc = bass.Bass("TRN2", target_bir_lowering=False)

    def din(name, shape):
        return nc.dram_tensor(name, list(shape), F32, kind="ExternalInput").ap()

    x_d = din("x", [NLOC, D])
    win_d = din("w_in", [D, 2560])
    wout_d = din("w_out", [D, D])
    w1_d = din("w_ff1", [D, 4096])
    w2_d = din("w_ff2", [4096, D])
    gbc_d = din("gbc", [4, 128, D])
    wsT_d = din("wsT", [128, 8, 128])
    sgf_d = din("sgf", [3, 128, 512])
    bias_d = din("biasT", [128, 24, 256])
    ident_d = din("ident", [128, 128])
    out_d = nc.dram_tensor("out", [NOWN, D], F32, kind="ExternalOutput").ap()
    x1s_d = nc.dram_tensor("x1s", [NOWN, D], F32, kind="Internal").ap()
    w1s_d = nc.dram_tensor("w1s", [D, 4096], BF16, kind="Internal").ap()
    w2s_d = nc.dram_tensor("w2s", [4096, D], BF16, kind="Internal").ap()
    wos_d = nc.dram_tensor("wos", [D, D], BF16, kind="Internal").ap()
    dbg = {}
    if debug:
        for nm, shp, dt_ in (("qT", [128, 4, NOWN], BF16), ("kT", [128, 4, PADK + NLOC], BF16), ("vT", [128, 4, PADK + NLOC], BF16),
                             ("catS", [128, 4, NOWN], BF16), ("catA", [128, 4, NOWN], BF16), ("x1", [128, 16, D], F32),
                             ("aT", [128, 32, 512], BF16), ("h2T", [128, 8, 512], BF16), ("w2", [128, 32, D], BF16),
                             ("w1", [128, 8, 4096], BF16), ("w1s", [D, 4096], BF16), ("w2s", [4096, D], BF16), ("wos", [D, D], BF16)):
            dbg[nm] = nc.dram_tensor("dbg_" + nm, shp, dt_, kind="ExternalOutput").ap()

    st = ExitStack()
    with st:
        ARENA = 206 * KB
        ar = Arena(nc, st, ARENA)
        ps_all = st.enter_context(nc.psum_tensor("ps", [128, 8 * 512], F32))
        PF = [ps_all[:, i * 512:(i + 1) * 512] for i in range(8)]
        PB = [PF[i].bitcast(BF16) for i in range(8)]
        P = Prog(nc)
        psk = lambda i: ("ps", i)

        L0 = Lay(ar, 0)
        ident = L0.get(BF16, [128])
        ss_all = L0.get(F32, [24])
        rstd_all = L0.get(F32, [24])
        stat = L0.get(F32, [64])
        epsr = L0.get(F32, [2])
        L0.off = 2 * KB
        catS = L0.get(BF16, [4, NOWN])
        BASE = L0.off

        P.add("pool", lambda e: e.dma_start(out=ident, in_=ident_d), writes=["ident"], dma=True)
        P.add("dve", lambda e: e.memset(ss_all, 0.0), writes=["ss_all"])
        P.add("dve", lambda e: e.memset(stat, 0.0), writes=["stat"])
        P.add("dve", lambda e: e.memset(epsr[:, 0:1], RMS_EPS), writes=["epsr"])
        P.add("dve", lambda e: e.memset(epsr[:, 1:2], LN_EPS), writes=["epsr"])

        L = Lay(ar, BASE)
        w_in = L.get(BF16, [8, 2560])
        qT = L.get(BF16, [4, NOWN])
        kT = L.get(BF16, [4, PADK + NLOC])
        vT = L.get(BF16, [4, PADK + NLOC])
        Q_END = L.off
        hT = [L.get(BF16, [8, 512]) for _ in range(2)]
        xt = [L.get(F32, [D]) for _ in range(2)]
        hb = [L.get(BF16, [D]) for _ in range(2)]
        uT = [L.get(BF16, [4, 512]) for _ in range(2)]
        GV_OFF = L.off
        gv = [L.get(F32, [512]) for _ in range(4)]
        nt = [L.get(BF16, [512]) for _ in range(4)]
        wsT = L.get(BF16, [8, 128])
        Cc = L.get(F32, [4, 128])
        lngF = L.get(F32, [4, 128])
        gA = L.get(F32, [D])
        xs = L.get(F32, [D])
        tmpS = [L.get(F32, [4, 128]) for _ in range(2)]
        lnbF, bsF = tmpS
        ones_bf = L.get(BF16, [64])
        bnst = L.get(F32, [4, 6])
        mv = L.get(F32, [4, 2])
        lnsd = L.get(F32, [4])
        lnr = L.get(F32, [4])
        assert L.off <= ARENA, L.off

        P.add("pool", lambda e: e.dma_start(out=wsT, in_=wsT_d), writes=["wsT"], dma=True)
        P.add("pool", lambda e: e.memset(ones_bf, 1.0), writes=["ones_bf"])
        winv = win_d.rearrange("(c p) n -> p c n", p=128)
        for pc in range(5):
            P.add("pool", lambda e, pc=pc: e.dma_start(out=w_in[:, :, pc * 512:(pc + 1) * 512], in_=winv[:, :, pc * 512:(pc + 1) * 512]),
                  reads=([("w_in", pc - 1)] if pc > 0 else []), writes=[("w_in", pc)], dma=True, nobar=True)
            if pc == 0:
                P.add("pool", lambda e: e.memset(kT[:, :, 0:PADK], 0.0), writes=["kpad"])
                P.add("pool", lambda e: e.memset(vT[:, :, 0:PADK], 0.0), writes=["vpad"])
        P.add("sp", lambda e: e.dma_start(out=gA, in_=gbc_d[0]), writes=["gA"], dma=True)
        for i, tdst in enumerate((lngF, lnbF, bsF)):
            P.add("sp", lambda e, i=i, tdst=tdst: e.dma_start(
                out=tdst, in_=sgf_d[i].rearrange("p (a b) -> p a b", b=128)), writes=[("sgf", i)], dma=True)

        xv = x_d.rearrange("(t p) d -> t p d", p=128)

        def stats_load(t):
            P.add("sp", lambda e: e.dma_start(out=xs, in_=xv[t]), writes=["xs"], dma=True)

        def stats_square(t):
            P.add("act", lambda e: e.activation(out=xs, in_=xs, func=AF.Square, accum_out=ss_all[:, t:t + 1]),
                  reads=["xs", "ss_all"], writes=["xs", ("ss", t)])

        def stats_finish(g):
            sl = slice(4 * g, 4 * g + 4)
            P.add("act", lambda e: e.activation(out=rstd_all[:, sl], in_=ss_all[:, sl], func=AF.Sqrt, bias=epsr[:, 0:1], scale=1.0 / D),
                  reads=[("ss", t) for t in range(4 * g, 4 * g + 4)] + ["epsr"], writes=[("sd", g)])
            P.add("dve", lambda e: e.reciprocal(out=rstd_all[:, sl], in_=rstd_all[:, sl]), reads=[("sd", g)], writes=[("rstd", g)])

        def stats_group(g):
            for t in range(4 * g, 4 * g + 4):
                stats_load(t)
                stats_square(t)
            stats_finish(g)

        gvbig = ar.view(GV_OFF, F32, [D])
        pre0 = [(xt[0], ("xt", 0)), (xt[1], ("xt", 1)), (xs, "xs"), (gvbig, "gvbig")]
        for t in range(4):
            q_ = "sp" if t % 2 == 0 else "act"
            P.add(q_, lambda e, t=t: e.dma_start(out=pre0[t][0], in_=xv[t]), writes=[pre0[t][1]], dma=True)
        for t in range(4):
            P.add("act", lambda e, t=t: e.activation(out=hb[t % 2], in_=pre0[t][0], func=AF.Square, accum_out=ss_all[:, t:t + 1]),
                  reads=[pre0[t][1], "ss_all"], writes=[("hb", t % 2), ("ss", t)])
        stats_finish(0)

        for a in range(4):
            for e2 in range(2):
                g_ = 2 * a + e2
                P.add("pe", lambda e, a=a, e2=e2, g_=g_: e.matmul(
                    PF[5][e2 * 64:(e2 + 1) * 64, a * 128:(a + 1) * 128], lhsT=ones_bf, rhs=wsT[:, g_, :],
                    start=True, stop=True), reads=["ones_bf", "wsT"], writes=[psk(5)])
        pf5v = PF[5].rearrange("p (a b) -> p a b", b=128)
        P.add("dve", lambda e: e.tensor_tensor(out=Cc, in0=pf5v, in1=lnbF, op=ALU.mult),
              reads=[psk(5), ("sgf", 1)], writes=["Cc0"])
        P.add("dve", lambda e: e.tensor_tensor(out=Cc, in0=Cc, in1=bsF, op=ALU.add),
              reads=["Cc0", ("sgf", 2)], writes=["Cc"])

        casts = [lambda: P.add("pool", lambda e: e.dma_start(out=wos_d, in_=wout_d), writes=["wos"], dma=True, nobar=True)]
        for pc in range(8):
            casts.append(lambda pc=pc: P.add("pool", lambda e: e.dma_start(
                out=w2s_d[pc * 512:(pc + 1) * 512, :], in_=w2_d[pc * 512:(pc + 1) * 512, :]), writes=[("w2s", pc)], dma=True, nobar=True))
        cnt2 = dict(pb=0)

        def prep_tile(t, pre=None):
            if pre is None:
                P.add("sp", lambda e: e.dma_start(out=xt[t % 2], in_=xv[t]), writes=[("xt", t % 2)], dma=True)
                src, skey = xt[t % 2], ("xt", t % 2)
            else:
                src, skey = pre
            P.add("dve", lambda e: e.scalar_tensor_tensor(
                out=hb[t % 2], in0=src, scalar=rstd_all[:, t:t + 1], in1=gA, op0=ALU.mult, op1=ALU.mult),
                reads=[skey, ("rstd", t // 4), "gA"], writes=[("hb", t % 2)])

        def transpose_tile(t):
            g, j = t // 4, t % 4
            bank = 6 + (t % 2)
            for kc in range(8):
                P.add("pe", lambda e, kc=kc: e.transpose(
                    out=PB[bank][:, kc * 128:(kc + 1) * 128], in_=hb[t % 2][:, kc * 128:(kc + 1) * 128],
                    identity=ident), reads=[("hb", t % 2), "ident"], writes=[psk(bank)])
            src = PB[bank].rearrange("p (a b) -> p a b", b=128)
            dst = hT[g % 2][:, :, j * 128:(j + 1) * 128]
            if t % 2 == 0:
                P.add("act", lambda e: e.activation(out=dst, in_=src, func=AF.Copy), reads=[psk(bank)], writes=[("hT", g % 2, j)])
            else:
                P.add("dve", lambda e: e.tensor_copy(out=dst, in_=src), reads=[psk(bank)], writes=[("hT", g % 2, j)])

        def proj_chunk(g, kind, c):
            hTg = hT[g % 2]
            hreads = [("hT", g % 2, j) for j in range(4)]
            oc = {"q": 0, "k": 4, "v": 8, "u": 12}[kind] + c
            bank = cnt2["pb"] % 3
            cnt2["pb"] += 1
            for kc in range(8):
                P.add("pe", lambda e, kc=kc: e.matmul(
                    PF[bank], lhsT=w_in[:, kc, oc * 128:(oc + 1) * 128], rhs=hTg[:, kc, :],
                    start=(kc == 0), stop=(kc == 7)), reads=hreads + [("w_in", oc // 4)], writes=[psk(bank)])
            if kind == "q":
                dst = qT[:, c, g * 512:(g + 1) * 512]
                P.add("act", lambda e: e.activation(out=dst, in_=PF[bank], func=AF.Copy, scale=0.125),
                      reads=[psk(bank)], writes=[("qT", c, g)])
            elif kind == "k":
                dst = kT[:, c, PADK + g * 512:PADK + (g + 1) * 512]
                P.add("dve", lambda e: e.tensor_copy(out=dst, in_=PF[bank]), reads=[psk(bank)], writes=[("kT", c, g)])
            elif kind == "v":
                dst = vT[:, c, PADK + g * 512:PADK + (g + 1) * 512]
                vw = [("vT", c, g)] + (["vpad"] if g == 0 else [])
                if c % 2 == 0:
                    P.add("act", lambda e: e.activation(out=dst, in_=PF[bank], func=AF.Copy), reads=[psk(bank)], writes=vw)
                else:
                    P.add("dve", lambda e: e.tensor_copy(out=dst, in_=PF[bank]), reads=[psk(bank)], writes=vw)
            else:
                dst = uT[g % 2][:, c, :]
                P.add("act", lambda e: e.activation(out=dst, in_=PF[bank], func=AF.Gelu_apprx_tanh),
                      reads=[psk(bank)], writes=[("uT", g % 2, c)])

        def zv_tile(g, j):
            hTg = hT[g % 2]
            bank = 3 + (j % 2)
            for kc in range(8):
                P.add("pe", lambda e, kc=kc: e.matmul(
                    PF[bank], lhsT=hTg[:, kc, j * 128:(j + 1) * 128], rhs=w_in[:, kc, 2048:2560],
                    start=(kc == 0), stop=(kc == 7)), reads=[("hT", g % 2, j), ("w_in", 4)], writes=[psk(bank)])
            P.add("act", lambda e: e.activation(out=gv[j], in_=PF[bank], func=AF.Gelu_apprx_tanh),
                  reads=[psk(bank)], writes=[("gv", j)] + (["gvbig"] if j < 2 else []))
            P.add("dve", lambda e: e.bn_stats(out=bnst[:, j, :], in_=gv[j]), reads=[("gv", j)], writes=[("bnst", j)])
            P.add("dve", lambda e: e.bn_aggr(out=mv[:, j, :], in_=bnst[:, j, :]), reads=[("bnst", j)], writes=[("mv", j)])

        def ln_group(g):
            P.add("act", lambda e: e.activation(out=lnsd, in_=mv[:, :, 1], func=AF.Sqrt, bias=epsr[:, 1:2], scale=1.0),
                  reads=[("mv", j) for j in range(4)] + ["epsr"], writes=["lnsd"])
            P.add("dve", lambda e: e.reciprocal(out=lnr, in_=lnsd), reads=["lnsd"], writes=["lnr"])
            for j in range(4):
                P.add("dve", lambda e, j=j: e.tensor_scalar(out=nt[j], in0=gv[j], scalar1=mv[:, j, 0:1], scalar2=lnr[:, j:j + 1],
                                                            op0=ALU.subtract, op1=ALU.mult),
                      reads=[("gv", j), ("mv", j), "lnr"], writes=[("nt", j)])

        def sgu_tile(g, j):
            t = 4 * g + j
            for a in range(4):
                for e2 in range(2):
                    g_ = 2 * a + e2
                    P.add("pe", lambda e, a=a, e2=e2, g_=g_: e.matmul(
                        PF[5][e2 * 64:(e2 + 1) * 64, a * 128:(a + 1) * 128], lhsT=nt[j][:, g_ * 64:(g_ + 1) * 64],
                        rhs=wsT[:, g_, :], start=True, stop=True), reads=[("nt", j), "wsT"], writes=[psk(5)])
            tm = tmpS[j % 2]
            P.add("dve", lambda e: e.tensor_tensor(out=tm, in0=pf5v, in1=lngF, op=ALU.mult),
                  reads=[psk(5), ("sgf", 0)], writes=[("sgf", 1 + j % 2)])
            P.add("pool", lambda e: e.tensor_tensor(out=tm, in0=tm, in1=Cc, op=ALU.add),
                  reads=[("sgf", 1 + j % 2), "Cc"], writes=[("sgf", 1 + j % 2)])
            dst = catS[:, :, t * 128:(t + 1) * 128]
            usrc = uT[g % 2][:, :, j * 128:(j + 1) * 128]
            P.add("pool", lambda e: e.tensor_tensor(out=dst, in0=tm, in1=usrc, op=ALU.mult),
                  reads=[("sgf", 1 + j % 2)] + [("uT", g % 2, c) for c in range(4)], writes=[("catS", t)])

        for j in range(4):
            prep_tile(j, pre=pre0[j])
            transpose_tile(j)
        stats_group(1)
        for g in range(6):
            plan = []
            if g < 4:
                plan += [("q", c) for c in range(4)]
            plan += [("k", c) for c in range(4)] + [("v", c) for c in range(4)]
            if g < 4:
                plan += [("u", c) for c in range(4)]
            per = len(plan) // 4
            if g + 1 < 6:
                prep_tile(4 * (g + 1))
            for ci, (kind, c) in enumerate(plan):
                proj_chunk(g, kind, c)
                if g + 2 < 6:
                    k_, ph = divmod(ci, per)
                    if per >= 4:
                        if ph == 1:
                            stats_load(4 * (g + 2) + k_)
                        elif ph == 3:
                            stats_square(4 * (g + 2) + k_)
                    else:
                        if ph == 0:
                            stats_load(4 * (g + 2) + k_)
                        else:
                            stats_square(4 * (g + 2) + k_)
                    if ci == len(plan) - 1:
                        stats_finish(g + 2)
                if (ci + 1) % per == 0 and g + 1 < 6:
                    jn = (ci + 1) // per - 1
                    transpose_tile(4 * (g + 1) + jn)
                    if jn + 1 < 4:
                        prep_tile(4 * (g + 1) + jn + 1)
                    if 1 <= g <= 4:
                        sgu_tile(g - 1, jn)
            if g < 4:
                for j in range(4):
                    zv_tile(g, j)
                ln_group(g)
                for _ in range(4):
                    if casts:
                        casts.pop(0)()
            else:
                while casts:
                    casts.pop(0)()

        P.barrier()
        dbg_ops = []
        if debug:
            for nm, buf in (("qT", qT), ("kT", kT), ("vT", vT), ("catS", catS)):
                dbg_ops.append(P.add("sp", lambda e, nm=nm, buf=buf: e.dma_start(out=dbg[nm], in_=buf), dma=True))
        L = Lay(ar, BASE)
        catA = L.get(BF16, [4, NOWN])
        biasT = L.get(BF16, [24, 256])
        Vaug = [L.get(BF16, [8, 2, 128]) for _ in range(2)]
        assert L.off <= BASE + 40 * KB
        L = Lay(ar, Q_END)
        qmA = L.get(BF16, [4, NOWN])
        PTs = [L.get(BF16, [1024]) for _ in range(3)]
        rden = L.get(F32, [NOWN])
        ACC0_OFF = L.off
        accs = [L.get(F32, [2, NOWN]) for _ in range(2)]
        ACC1_OFF = ACC0_OFF + 16 * KB
        w_out = ar.view(ACC0_OFF, BF16, [8, D])
        assert L.off <= ARENA, L.off
        qm = [qmA, qT]
        for hb_ in range(0, 24, 6):
            P.add("pool", lambda e, hb_=hb_: e.dma_start(out=biasT[:, hb_:hb_ + 6, :], in_=bias_d[:, hb_:hb_ + 6, :]),
                  writes=[("biasT", hb_)], dma=True)
        def q_mask(c):
            P.add("pool", lambda e: e.memset(qmA[64:128, c, :], 0.0), writes=[("qm", 0, c)])
            if c % 2 == 0:
                P.add("act", lambda e: e.activation(out=qmA[0:64, c, :], in_=qT[0:64, c, :], func=AF.Copy),
                      reads=[("qm", 1, c)], writes=[("qm", 0, c)])
            else:
                P.add("dve", lambda e: e.tensor_copy(out=qmA[0:64, c, :], in_=qT[0:64, c, :]),
                      reads=[("qm", 1, c)], writes=[("qm", 0, c)])
            P.add("pool", lambda e: e.memset(qT[0:64, c, :], 0.0), writes=[("qm", 1, c)])

        q_mask(0)

        bias_reads = [("biasT", i) for i in range(0, 24, 6)]
        def bias_exp(hb_):
            P.add("act", lambda e: e.activation(out=biasT[:, hb_:hb_ + 6, :], in_=biasT[:, hb_:hb_ + 6, :], func=AF.Exp),
                  reads=[("biasT", hb_)], writes=[("biasT", hb_)])

        bias_exp(0)
        steps = [(c, br, seg, e2) for c in range(4) for br in range(3) for seg in range(4) for e2 in range(2)]
        NS = len(steps)
        pending_norm = []
        vones_state = [None, None]

        def seg_ctx(i):
            c, br, seg, e2 = steps[i]
            it_ = i // 2
            return c, br, seg, e2, it_, seg_tiles(br, seg), Vaug[it_ % 2], 6 + (it_ % 2), it_ % 2

        def emit_qk(i):
            c, br, seg, e2, it_, tiles, Va, vbank, vb = seg_ctx(i)
            ntile = len(tiles)
            if e2 == 0:
                for tl in tiles:
                    cs = PADK + tl["kstart"]
                    src = vT[:, c, cs: cs + 127 * tl["kstep"] + 1: tl["kstep"]]
                    P.add("pe", lambda e, src=src, sl=tl["slot"], vbank=vbank: e.transpose(
                        out=PB[vbank][:, sl * 128:(sl + 1) * 128], in_=src, identity=ident),
                        reads=["vT_all", "ident"], writes=[psk(vbank)])
                srcv = PB[vbank][:, 0:ntile * 128].rearrange("p (s h d) -> p s h d", h=2, d=64)
                dstv = Va[:, 0:ntile, :, 0:64]
                P.add("act", lambda e, srcv=srcv, dstv=dstv: e.activation(out=dstv, in_=srcv, func=AF.Copy),
                      reads=[psk(vbank)], writes=[("Vaug", vb)])
                want = frozenset(tl["slot"] for tl in tiles if tl["boundary"])
                have = vones_state[vb]
                if have is None:
                    P.add("pool", lambda e, Va=Va: e.memset(Va[:, :, :, 64:128], 1.0), writes=[("Vones", vb)])
                    have = frozenset()
                for sl in sorted(have - want):
                    P.add("pool", lambda e, Va=Va, sl=sl: e.memset(Va[0:64, sl, :, 64:128], 1.0), writes=[("Vones", vb)])
                for sl in sorted(want - have):
                    P.add("pool", lambda e, Va=Va, sl=sl: e.memset(Va[0:64, sl, :, 64:128], 0.0), writes=[("Vones", vb)])
                vones_state[vb] = want
            h = 2 * c + e2
            sbanks = [2 * (i % 2), 2 * (i % 2) + 1]
            pt = PTs[i % 3]
            ptk = ("PT", i % 3)
            hbi = h * 3 + br
            for tl in tiles:
                sb = sbanks[tl["sbank"]]
                n = tl["n"]
                so = tl["soff"]
                ks = PADK + tl["kstart"]
                kap = kT[:, c, ks: ks + 127 * tl["kstep"] + 1: tl["kstep"]]
                qap = qm[e2][:, c, tl["qstart"]: tl["qstart"] + (n - 1) * tl["qstep"] + 1: tl["qstep"]]
                P.add("pe", lambda e, sb=sb, so=so, n=n, kap=kap, qap=qap: e.matmul(
                    PF[sb][:, so:so + n], lhsT=kap, rhs=qap, start=True, stop=True, skip_group_check=True),
                    reads=["kT_all", ("qm", e2, c)], writes=[psk(sb)])
            for bi in range(2):
                P.add("act", lambda e, bi=bi, sbanks=sbanks, pt=pt: e.activation(
                    out=pt[:, bi * 512:(bi + 1) * 512], in_=PF[sbanks[bi]], func=AF.Exp),
                    reads=[psk(sbanks[bi])], writes=[(ptk, bi)])
            ebv = biasT[:, hbi:hbi + 1, :].broadcast_to([128, 2, 256])
            for bi, eng_ in ((0, "pool"), (1, "dve")):
                ptv = pt[:, bi * 512:(bi + 1) * 512].rearrange("p (a b) -> p a b", b=256)
                P.add(eng_, lambda e, ptv=ptv, ebv=ebv: e.tensor_tensor(out=ptv, in0=ptv, in1=ebv, op=ALU.mult),
                      reads=[(ptk, bi), ("biasT", hbi // 6 * 6)], writes=[(ptk, bi)])

        def emit_pv(i):
            c, br, seg, e2, it_, tiles, Va, vbank, vb = seg_ctx(i)
            ntile = len(tiles)
            acc = accs[c % 2]
            akey = ("acc", c % 2, e2)
            obank = 4 + (i % 2)
            pt = PTs[i % 3]
            ptk = ("PT", i % 3)
            for ti, tl in enumerate(tiles):
                n = tl["n"]
                po = tl["sbank"] * 512 + tl["soff"]
                P.add("pe", lambda e, ti=ti, tl=tl, n=n, po=po: e.matmul(
                    PF[obank][:, tl["qlo"]:tl["qlo"] + n], lhsT=Va[:, tl["slot"], e2, :], rhs=pt[:, po:po + n],
                    start=(ti == 0), stop=(ti == ntile - 1), skip_group_check=True),
                    reads=[("Vaug", vb), ("Vones", vb), (ptk, tl["sbank"])], writes=[psk(obank)])
            if br == 0:
                dst = acc[:, e2, seg * 512:(seg + 1) * 512]
                P.add("dve", lambda e: e.tensor_copy(out=dst, in_=PF[obank]), reads=[psk(obank)], writes=[akey])
            elif br == 1:
                dst = acc[:, e2, seg:NOWN:4]
                P.add("dve", lambda e: e.tensor_tensor(out=dst, in0=PF[obank], in1=dst, op=ALU.add),
                      reads=[psk(obank), akey], writes=[akey])
            else:
                dst = acc[:, e2, :].rearrange("p (i r) -> p r i", r=16)[:, 4 * seg:4 * seg + 4, :]
                srco = PF[obank].rearrange("p (r i) -> p r i", i=128)
                P.add("dve", lambda e: e.tensor_tensor(out=dst, in0=srco, in1=dst, op=ALU.add),
                      reads=[psk(obank), akey], writes=[akey])
            if (br, seg, e2) == (2, 3, 1):
                for ee in range(2):
                    for blk in range(4):
                        pending_norm.append((c, ee, blk))
            elif i % 2 == 1 and pending_norm:
                emit_norm(*pending_norm.pop(0))

        def emit_norm(c, ee, blk):
            acc = accs[c % 2]
            akey = ("acc", c % 2, ee)
            cols = slice(blk * 512, (blk + 1) * 512)
            P.add("act", lambda e: e.activation(out=rden[0:64, cols], in_=acc[64:128, ee, cols], func=AF.Ln),
                  reads=[akey], writes=[("rden", blk)])
            P.add("act", lambda e: e.activation(out=rden[0:64, cols], in_=rden[0:64, cols], func=AF.Exp, scale=-1.0),
                  reads=[("rden", blk)], writes=[("rden", blk)])
            dst = catA[ee * 64:(ee + 1) * 64, c, cols]
            P.add("pool", lambda e: e.tensor_tensor(out=dst, in0=acc[0:64, ee, cols], in1=rden[0:64, cols], op=ALU.mult),
                  reads=[akey, ("rden", blk)], writes=[("catA", c, ee, blk)])

        woutv = wos_d.rearrange("(c p) n -> p c n", p=128)
        for i in range(NS + 2):
            if i in (4, 8, 12):
                bias_exp(6 * (i // 4))
                q_mask(i // 4)
            if i < NS:
                emit_qk(i)
            if i >= 2:
                emit_pv(i - 2)
            if i == NS - 6:
                for c2 in range(0, 8, 2):
                    P.add("sp", lambda e, c2=c2: e.dma_start(out=w_out[:, c2:c2 + 2, :], in_=woutv[:, c2:c2 + 2, :]),
                          reads=[("acc", 0, 0), ("acc", 0, 1), "wos"], writes=[("acc", 0, 0), ("acc", 0, 1), ("w_out", c2)], dma=True)
        while pending_norm:
            emit_norm(*pending_norm.pop(0))

        P.barrier()
        if debug:
            dbg_ops.append(P.add("sp", lambda e: e.dma_start(out=dbg["catA"], in_=catA), dma=True))
        L = Lay(ar, BASE + 16 * KB)
        w1 = L.get(BF16, [8, 4096])
        w2 = L.get(BF16, [32, D])
        W2_END = L.off
        gB = L.get(F32, [D])
        sq4 = L.get(BF16, [D])
        assert L.off <= ACC0_OFF, (L.off, ACC0_OFF)
        L = Lay(ar, BASE + 16 * KB + 64 * KB)
        xt4 = [L.get(F32, [D]) for _ in range(2)]
        tmp4 = [L.get(F32, [D]) for _ in range(2)]
        L = Lay(ar, ACC1_OFF)
        h2T = [L.get(BF16, [8, 512])] * 2
        gC = L.get(F32, [D])
        hb5 = [L.get(BF16, [D])] * 2
        P5A_OFF = L.off
        assert L.off <= ARENA, L.off
        hb4 = [ar.view(BASE + 16 * KB + 64 * KB + 16 * KB + i * 2 * KB, BF16, [D]) for i in range(4)]
        pend_tr = []

        def p4_transposes(t):
            for kc in range(8):
                P.add("pe", lambda e, kc=kc: e.transpose(
                    out=PB[7][:, kc * 128:(kc + 1) * 128], in_=hb4[t][:, kc * 128:(kc + 1) * 128], identity=ident),
                    reads=[("hb4", t), "ident"], writes=[psk(7)])
            srcT = PB[7].rearrange("p (a b) -> p a b", b=128)
            dstT = h2T[0][:, :, t * 128:(t + 1) * 128]
            P.add("act", lambda e: e.activation(out=dstT, in_=srcT, func=AF.Copy), reads=[psk(7)], writes=[("h2T", 0, t)])

        P.add("sp", lambda e: e.dma_start(out=gB, in_=gbc_d[1]), writes=["gB"], dma=True)
        P.add("sp", lambda e: e.dma_start(out=gC, in_=gbc_d[2]), writes=["gC"], dma=True)
        w1f = w1_d.rearrange("(c p) n -> p c n", p=128)
        w2v = w2s_d.rearrange("(f p) n -> p f n", p=128)
        wst = [ar.view(BASE + 16 * KB + 64 * KB + 24 * KB + i * 16 * KB, F32, [8, 512]) for i in range(2)]
        wloads = []
        for pc in range(8):
            wloads.append(lambda pc=pc: P.add("sp", lambda e: e.dma_start(
                out=wst[pc % 2], in_=w1f[:, :, pc * 512:(pc + 1) * 512]), writes=[("wst", pc % 2)], dma=True))

        def w1_cast(pc):
            P.add("dve", lambda e: e.tensor_copy(out=w1[:, :, pc * 512:(pc + 1) * 512], in_=wst[pc % 2]),
                  reads=[("wst", pc % 2)], writes=[("w1", pc)])
        for f in range(0, 32, 4):
            wloads.append(lambda f=f: P.add("sp", lambda e: e.dma_start(out=w2[:, f:f + 4, :], in_=w2v[:, f:f + 4, :]),
                                            reads=[("w2s", f // 4)], writes=[("w2", f)], dma=True, nobar=True))
        x1v = x1s_d.rearrange("(t p) d -> t p d", p=128)
        def load_x4(t):
            P.add("sp", lambda e: e.dma_start(out=xt4[t % 2], in_=xv[t]), writes=[("xt4", t % 2)], dma=True)

        load_x4(0)
        for t in range(16):
            if t + 1 < 16:
                load_x4(t + 1)
            if t % 2 == 0:
                if t >= 2:
                    w1_cast(t // 2 - 1)
                wloads[t // 2]()
            for hh in range(2):
                bank = 2 * (t % 2) + hh
                for kc in range(8):
                    src = (catA[:, kc, t * 128:(t + 1) * 128] if kc < 4 else catS[:, kc - 4, t * 128:(t + 1) * 128])
                    P.add("pe", lambda e, src=src, kc=kc, hh=hh, bank=bank: e.matmul(
                        PF[bank], lhsT=src, rhs=w_out[:, kc, hh * 512:(hh + 1) * 512], start=(kc == 0), stop=(kc == 7)),
                        reads=["catA_all", "catS_all", ("w_out", kc // 2 * 2)], writes=[psk(bank)])
            if pend_tr and pend_tr[0] <= t - 2:
                p4_transposes(pend_tr.pop(0))
            b0 = 2 * (t % 2)
            ps2 = ps_all[:, b0 * 512:(b0 + 2) * 512]
            P.add("act", lambda e, t=t, ps2=ps2: e.activation(out=sq4, in_=ps2, func=AF.Square, accum_out=stat[:, 32 + t:33 + t]),
                  reads=[psk(b0), psk(b0 + 1), "stat"], writes=["sq4", ("sst", t)])
            P.add("act", lambda e, t=t: e.activation(out=stat[:, 48 + t:49 + t], in_=stat[:, 32 + t:33 + t], func=AF.Sqrt,
                                                     bias=epsr[:, 0:1], scale=1.0 / D),
                  reads=[("sst", t), "epsr"], writes=[("ssd", t)])
            P.add("dve", lambda e, t=t: e.reciprocal(out=stat[:, 48 + t:49 + t], in_=stat[:, 48 + t:49 + t]),
                  reads=[("ssd", t)], writes=[("rs4", t)])
            P.add("dve", lambda e, t=t, ps2=ps2: e.scalar_tensor_tensor(
                out=tmp4[t % 2], in0=ps2, scalar=stat[:, 48 + t:49 + t], in1=gB, op0=ALU.mult, op1=ALU.mult),
                reads=[psk(b0), psk(b0 + 1), ("rs4", t), "gB"], writes=[("tmp4", t % 2)])
            P.add("pool", lambda e, t=t: e.tensor_tensor(out=tmp4[t % 2], in0=tmp4[t % 2], in1=xt4[t % 2], op=ALU.add),
                  reads=[("tmp4", t % 2), ("xt4", t % 2)], writes=[("tmp4", t % 2)])
            P.add("sp", lambda e, t=t: e.dma_start(out=x1v[t], in_=tmp4[t % 2]), reads=[("tmp4", t % 2)],
                  writes=[("x1s", t)], dma=True)
            if t < 4:
                P.add("act", lambda e, t=t: e.activation(out=hb4[t], in_=tmp4[t % 2], func=AF.Square, accum_out=stat[:, t:t + 1]),
                      reads=[("tmp4", t % 2), "stat"], writes=[("hb4", t), ("s5", t)])
                P.add("act", lambda e, t=t: e.activation(out=stat[:, 8 + t:9 + t], in_=stat[:, t:t + 1], func=AF.Sqrt,
                                                         bias=epsr[:, 0:1], scale=1.0 / D),
                      reads=[("s5", t), "epsr"], writes=[("sd5", t)])
                P.add("dve", lambda e, t=t: e.reciprocal(out=stat[:, 8 + t:9 + t], in_=stat[:, 8 + t:9 + t]),
                      reads=[("sd5", t)], writes=[("r5", t)])
                P.add("dve", lambda e, t=t: e.scalar_tensor_tensor(
                    out=hb4[t], in0=tmp4[t % 2], scalar=stat[:, 8 + t:9 + t], in1=gC, op0=ALU.mult, op1=ALU.mult),
                    reads=[("tmp4", t % 2), ("r5", t), "gC"], writes=[("hb4", t)])
                pend_tr.append(t)

        w1_cast(7)
        P.barrier()
        if debug:
            x1dv = dbg["x1"]
            for t in range(16):
                dbg_ops.append(P.add("sp", lambda e, t=t: e.dma_start(out=x1dv[:, t, :], in_=x1v[t]), dma=True))
        L = Lay(ar, 2 * KB)
        aT = L.get(BF16, [32, 512])
        assert L.off <= BASE + 16 * KB, L.off
        L = Lay(ar, W2_END)
        gD = L.get(F32, [D])
        xa = [L.get(F32, [D]) for _ in range(2)]
        xb = [L.get(F32, [D]) for _ in range(2)]
        st5 = L.get(F32, [160])
        assert L.off <= ACC1_OFF, (L.off, ACC1_OFF)
        L = Lay(ar, P5A_OFF)
        RR_OFF = L.off
        rr_ = [L.get(F32, [512]) for _ in range(2)]
        ost = [L.get(F32, [D])] * 2
        sq5 = ar.view(RR_OFF, BF16, [D])
        assert L.off <= ARENA, L.off
        P.add("sp", lambda e: e.dma_start(out=gD, in_=gbc_d[3]), writes=["gD"], dma=True)
        P.add("dve", lambda e: e.memset(st5, 0.0), writes=["st5"])
        outv = out_d.rearrange("(t p) d -> t p d", p=128)
        final_ops = []
        cnt5 = dict(fb=0, rq=0)

        def prenorm_chain(t):
            P.add("act", lambda e: e.activation(out=hb5[0], in_=xa[t % 2], func=AF.Square, accum_out=st5[:, t:t + 1]),
                  reads=[("xa", t % 2), "st5"], writes=[("hb5", 0), ("s5", t)])
            P.add("act", lambda e: e.activation(out=st5[:, 16 + t:17 + t], in_=st5[:, t:t + 1], func=AF.Sqrt,
                                                bias=epsr[:, 0:1], scale=1.0 / D),
                  reads=[("s5", t), "epsr"], writes=[("sd5", t)])
            P.add("dve", lambda e: e.reciprocal(out=st5[:, 16 + t:17 + t], in_=st5[:, 16 + t:17 + t]),
                  reads=[("sd5", t)], writes=[("r5", t)])
            P.add("dve", lambda e: e.scalar_tensor_tensor(
                out=hb5[0], in0=xa[t % 2], scalar=st5[:, 16 + t:17 + t], in1=gC, op0=ALU.mult, op1=ALU.mult),
                reads=[("xa", t % 2), ("r5", t), "gC"], writes=[("hb5", 0)])

        def prenorm_transpose(t):
            j = t % 4
            for kc in range(8):
                P.add("pe", lambda e, kc=kc: e.transpose(
                    out=PB[7][:, kc * 128:(kc + 1) * 128], in_=hb5[0][:, kc * 128:(kc + 1) * 128], identity=ident),
                    reads=[("hb5", 0), "ident"], writes=[psk(7)])
            src = PB[7].rearrange("p (a b) -> p a b", b=128)
            dst = h2T[0][:, :, j * 128:(j + 1) * 128]
            P.add("act", lambda e: e.activation(out=dst, in_=src, func=AF.Copy), reads=[psk(7)], writes=[("h2T", 0, j)])

        def load_xa(t):
            P.add("sp", lambda e: e.dma_start(out=xa[t % 2], in_=x1v[t]), reads=[("x1s", t)], writes=[("xa", t % 2)], dma=True)

        def ff1_group(G):
            h2 = h2T[0]
            h2reads = [("h2T", 0, j) for j in range(4)]
            for F_ in range(32):
                if G == 0 and F_ % 4 == 0:
                    wloads[8 + F_ // 4]()
                bank = cnt5["fb"] % 3
                cnt5["fb"] += 1
                for kc in range(8):
                    P.add("pe", lambda e, F_=F_, kc=kc, bank=bank: e.matmul(
                        PF[bank], lhsT=w1[:, kc, F_ * 128:(F_ + 1) * 128], rhs=h2[:, kc, :], start=(kc == 0), stop=(kc == 7)),
                        reads=[("w1", F_ // 4)] + h2reads, writes=[psk(bank)])
                r_ = rr_[cnt5["rq"] % 2]
                rk = ("rr", cnt5["rq"] % 2)
                cnt5["rq"] += 1
                P.add("act", lambda e, r_=r_, bank=bank: e.activation(out=r_, in_=PF[bank], func=AF.Relu),
                      reads=[psk(bank)], writes=[rk])
                P.add("dve", lambda e, r_=r_, F_=F_: e.tensor_tensor(out=aT[:, F_, :], in0=r_, in1=r_, op=ALU.mult),
                      reads=[rk], writes=[("aT", F_)])

        def ff2_mm(t):
            j = t % 4
            for hh in range(2):
                bank = 3 + 2 * (t % 2) + hh
                for F_ in range(32):
                    P.add("pe", lambda e, F_=F_, hh=hh, bank=bank: e.matmul(
                        PF[bank], lhsT=aT[:, F_, j * 128:(j + 1) * 128], rhs=w2[:, F_, hh * 512:(hh + 1) * 512],
                        start=(F_ == 0), stop=(F_ == 31)), reads=[("aT", F_), ("w2", F_ // 4 * 4)], writes=[psk(bank)])

        def ff2_evac(t):
            b0 = 3 + 2 * (t % 2)
            ps2 = ps_all[:, b0 * 512:(b0 + 2) * 512]
            P.add("act", lambda e: e.activation(out=sq5, in_=ps2, func=AF.Square, accum_out=st5[:, 96 + t:97 + t]),
                  reads=[psk(b0), psk(b0 + 1), "st5"], writes=[("rr", 0), ("sft", t)])
            P.add("act", lambda e: e.activation(out=st5[:, 112 + t:113 + t], in_=st5[:, 96 + t:97 + t], func=AF.Sqrt,
                                                bias=epsr[:, 0:1], scale=1.0 / D),
                  reads=[("sft", t), "epsr"], writes=[("sfd", t)])
            P.add("dve", lambda e: e.reciprocal(out=st5[:, 112 + t:113 + t], in_=st5[:, 112 + t:113 + t]),
                  reads=[("sfd", t)], writes=[("rf", t)])
            P.add("dve", lambda e: e.scalar_tensor_tensor(
                out=ost[0], in0=ps2, scalar=st5[:, 112 + t:113 + t], in1=gD, op0=ALU.mult, op1=ALU.mult),
                reads=[psk(b0), psk(b0 + 1), ("rf", t), "gD"], writes=[("ost", 0)])
            P.add("pool", lambda e: e.tensor_tensor(out=ost[0], in0=ost[0], in1=xb[t % 2], op=ALU.add),
                  reads=[("ost", 0), ("xb", t % 2)], writes=[("ost", 0)])
            o = P.add("sp", lambda e: e.dma_start(out=outv[t], in_=ost[0]), reads=[("ost", 0)], writes=[("ost", 0)], dma=True)
            final_ops.append(o)

        def load_xb(t):
            P.add("sp", lambda e: e.dma_start(out=xb[t % 2], in_=x1v[t]), reads=[("x1s", t)], writes=[("xb", t % 2)], dma=True)

        for G in range(4):
            ff1_group(G)
            for j in range(4):
                t = 4 * G + j
                load_xb(t)
                if G < 3:
                    load_xa(t + 4)
                    prenorm_chain(t + 4)
                ff2_mm(t)
                if G < 3:
                    prenorm_transpose(t + 4)
                ff2_evac(t)

        if debug:
            P.barrier()
            for nm, buf in (("aT", aT), ("h2T", h2T[0]), ("w2", w2), ("w1", w1), ("w1s", w1s_d), ("w2s", w2s_d), ("wos", wos_d)):
                dbg_ops.append(P.add("sp", lambda e, nm=nm, buf=buf: e.dma_start(out=dbg[nm], in_=buf), dma=True))
        P.finalize(st)
        with nc.Block() as block:
            P.emit_all(block, final_ops + dbg_ops)
    return nc


def _t5_bucket(rel):
    half, max_exact = 16, 8
    ret = np.where(rel > 0, half, 0)
    n = np.abs(rel)
    nf = np.maximum(n, 1).astype(np.float32)
    large = max_exact + (np.log(nf / np.float32(max_exact)) / np.float32(math.log(1024 / max_exact))
                         * np.float32(half - max_exact)).astype(np.int32)
    large = np.minimum(large, half - 1)
    return ret + np.where(n < max_exact, n, large)


def _bias_tiles(rel_bias, sign):
    kl = np.arange(128)[:, None]
    ql = np.arange(256)[None, :]
    j = kl - ql + 64
    valid = np.abs(j) <= 64
    out = np.empty((128, 24, 256), np.float32)
    for br, dil in enumerate((1, 4, 16)):
        bidx = _t5_bucket((sign * dil * j).astype(np.int32))
        for h in range(8):
            tile = rel_bias[bidx, h]
            out[:, h * 3 + br, :] = np.where(valid, tile, np.float32(MASKV))
    return out


_NC_CACHE = {}


def kernel(x, g_pre_mix, w_in, sgu_ln_g, sgu_ln_b, sgu_w, sgu_b, w_out, g_post_mix,
           g_pre_ffn, w_ff1, w_ff2, g_post_ffn, rel_bias, _debug=None):
    f32 = np.float32
    x = np.asarray(x, f32)
    B = x.shape[0]
    key = "dbg" if _debug is not None else "nc"
    if key not in _NC_CACHE:
        _NC_CACHE[key] = build_nc(debug=_debug is not None)
    nc = _NC_CACHE[key]
    gbc = np.stack([np.broadcast_to(np.asarray(v, f32)[0][None, :], (128, D))
                    for v in (g_pre_mix, g_post_mix, g_pre_ffn, g_post_ffn)]).astype(f32)
    lng = np.asarray(sgu_ln_g, f32)[0]
    lnb = np.asarray(sgu_ln_b, f32)[0]
    Ws = np.asarray(sgu_w, f32)[0]
    bs = np.asarray(sgu_b, f32)[0]
    rb = np.asarray(rel_bias, f32)
    ident = np.eye(128, dtype=f32)
    pidx_g = (np.arange(4)[None, :] * 2 + (np.arange(128)[:, None] // 64))
    pidx_c = np.arange(128)[:, None] % 64
    feat = pidx_g * 64 + pidx_c
    lngF = np.broadcast_to(lng[feat][:, :, None], (128, 4, 128))
    lnbF = np.broadcast_to(lnb[feat][:, :, None], (128, 4, 128))
    in_maps = []
    for core in range(8):
        b, half = core // 2, core % 2
        if half == 0:
            idx = np.arange(NLOC)
            Wl, bl, sign = Ws, bs, 1
        else:
            idx = S - 1 - np.arange(NLOC)
            Wl, bl, sign = Ws[:, ::-1, ::-1], bs[:, ::-1], -1
        xl = np.ascontiguousarray(x[b][idx])
        wsT = np.ascontiguousarray(np.transpose(Wl, (2, 0, 1)))
        bsF = bl[pidx_g]
        sgf = np.stack([lngF.reshape(128, 512), lnbF.reshape(128, 512), bsF.reshape(128, 512)]).astype(f32)
        in_maps.append({
            "x": xl, "w_in": np.asarray(w_in, f32)[0], "w_out": np.asarray(w_out, f32)[0],
            "w_ff1": np.asarray(w_ff1, f32)[0], "w_ff2": np.asarray(w_ff2, f32)[0],
            "gbc": gbc, "wsT": wsT, "sgf": np.ascontiguousarray(sgf),
            "biasT": _bias_tiles(rb, sign), "ident": ident,
        })
    res = run_bass_kernel_spmd(nc, in_maps, core_ids=list(range(8)))
    if _debug is not None:
        _debug.extend(res.results)
    out = np.empty((B, S, D), f32)
    for core in range(8):
        b, half = core // 2, core % 2
        o = res.results[core]["out"]
        if half == 0:
            out[b, 0:NOWN] = o
        else:
            out[b, S - 1 - np.arange(NOWN)] = o
    return out
```

```python
import math
from contextlib import ExitStack

import numpy as np
import concourse.bass as bass
import concourse.mybir as mybir
from concourse.bass_utils import run_bass_kernel_spmd

F32 = mybir.dt.float32
BF16 = mybir.dt.bfloat16
AF = mybir.ActivationFunctionType
ALU = mybir.AluOpType
KB = 1024

D = 1024
S = 4096
NOWN = 2048
NLOC = 3072
PADK = 1024
RMS_EPS = 1e-6
LN_EPS = 1e-5
MASKV = -30000.0
STRICT_SAME_ENGINE = True


class Prog:
    def __init__(self, nc, n_dma_sems=8):
        self.nc = nc
        self.ops = []
        self.last_writer = {}
        self.readers = {}
        self.n_dma_sems = n_dma_sems
        self.bar_deps = set()
        self.since_bar_dma = []
        self.last_on = {}

    def add(self, eng, emit, reads=(), writes=(), dma=False, nobar=False):
        oid = len(self.ops)
        deps = set(self.bar_deps)
        for r in reads:
            if r in self.last_writer:
                deps.add(self.last_writer[r])
        for w in writes:
            if w in self.last_writer:
                deps.add(self.last_writer[w])
            for rd in self.readers.get(w, ()):
                deps.add(rd)
        for r in reads:
            self.readers.setdefault(r, []).append(oid)
        for w in writes:
            self.last_writer[w] = oid
            self.readers[w] = []
        deps.discard(oid)
        self.ops.append(dict(id=oid, eng=eng, emit=emit, deps=deps, dma=dma,
                             reads=tuple(reads), writes=tuple(writes)))
        if dma:
            if not nobar:
                self.since_bar_dma.append(oid)
        else:
            self.last_on[eng] = oid
        return oid

    def barrier(self):
        deps = set(self.since_bar_dma)
        for e, oid in self.last_on.items():
            deps.add(oid)
        self.bar_deps = deps
        self.since_bar_dma = []

    def finalize(self, stack):
        nc = self.nc
        ops = self.ops
        for op in ops:
            keep = set()
            for d in op["deps"]:
                dop = ops[d]
                if dop["dma"] or op["dma"]:
                    keep.add(d)
                    continue
                if dop["eng"] == op["eng"]:
                    if op["eng"] == "pe":
                        continue
                    if STRICT_SAME_ENGINE or (set(dop["writes"]) & set(op["reads"])):
                        keep.add(d)
                    continue
                keep.add(d)
            op["deps"] = keep
        signaled = set()
        for op in ops:
            signaled |= op["deps"]
        self.esem = {e: stack.enter_context(nc.semaphore("s_" + e)) for e in ("pe", "act", "dve", "pool")}
        self.dsem = {}
        for q in ("sp", "act", "pool"):
            self.dsem[q] = [stack.enter_context(nc.semaphore(f"d_{q}{i}")) for i in range(self.n_dma_sems)]
        cnt = {e: 0 for e in self.esem}
        dcnt = {q: [0] * self.n_dma_sems for q in self.dsem}
        dnext = {q: 0 for q in self.dsem}
        for op in ops:
            if op["dma"]:
                q = op["eng"]
                i = dnext[q] % self.n_dma_sems
                dnext[q] += 1
                op["prev_tok"] = (self.dsem[q][i], dcnt[q][i]) if dcnt[q][i] > 0 else None
                dcnt[q][i] += 16
                op["tok"] = (self.dsem[q][i], dcnt[q][i])
                op["sig"] = True
            elif op["id"] in signaled:
                cnt[op["eng"]] += 1
                op["tok"] = (self.esem[op["eng"]], cnt[op["eng"]])
                op["sig"] = True
            else:
                op["sig"] = False

    def run_engine(self, ename, eng):
        ops = self.ops
        waited = {}
        for op in ops:
            if op["eng"] != ename:
                continue
            need = {}
            toks = [ops[d]["tok"] for d in op["deps"]]
            if op["dma"] and op["prev_tok"] is not None:
                toks.append(op["prev_tok"])
            for (s, v) in toks:
                k = id(s)
                if k not in need or need[k][1] < v:
                    need[k] = (s, v)
            for k, (s, v) in need.items():
                if waited.get(k, 0) >= v:
                    continue
                eng.wait_ge(s, v)
                waited[k] = v
            ins = op["emit"](eng)
            if op["sig"]:
                s, v = op["tok"]
                ins.then_inc(s, 16 if op["dma"] else 1)

    def emit_all(self, block, final_ops):
        P = self

        def mk(ename):
            def f(eng):
                P.run_engine(ename, eng)
                if ename == "sp":
                    for oid in final_ops:
                        s, v = P.ops[oid]["tok"]
                        eng.wait_ge(s, v)
            return f
        block.tensor(mk("pe"))
        block.scalar(mk("act"))
        block.vector(mk("dve"))
        block.gpsimd(mk("pool"))
        block.sync(mk("sp"))


class Arena:
    def __init__(self, nc, stack, nbytes):
        self.nbytes = nbytes
        self.t = stack.enter_context(nc.sbuf_tensor("arena", [128, nbytes // 2], BF16))

    def view(self, off, dtype, shape):
        n = 1
        for s_ in shape:
            n *= s_
        esz = 4 if dtype == F32 else 2
        assert off % 4 == 0 and off + n * esz <= self.nbytes, (off, n * esz, self.nbytes)
        a = self.t[:, off // 2: off // 2 + n * esz // 2]
        if dtype == F32:
            a = a.bitcast(F32)
        if len(shape) == 2:
            a = a.rearrange("p (a b) -> p a b", b=shape[1])
        elif len(shape) == 3:
            a = a.rearrange("p (a b c) -> p a b c", b=shape[1], c=shape[2])
        return a


class Lay:
    def __init__(self, arena, start):
        self.arena = arena
        self.off = start

    def get(self, dtype, shape):
        n = 1
        for s_ in shape:
            n *= s_
        esz = 4 if dtype == F32 else 2
        v = self.arena.view(self.off, dtype, shape)
        self.off += (n * esz + 3) // 4 * 4
        return v


def seg_tiles(br, seg):
    tiles = []
    if br in (0, 1):
        dil = 1 if br == 0 else 4
        rel = [(0, 128, 128), (0, 256, 0), (128, 384, 0), (256, 512, 0), (384, 512, 0)]
        sb = [(0, 128), (0, 256), (1, 0), (1, 256), (0, 0)]
        for m in range(5):
            qlo, qhi, bc0 = rel[m]
            if br == 0:
                k0 = 512 * seg - 64 + 128 * m
                kstart, qstart = k0, 512 * seg + qlo
            else:
                K0 = -64 + 128 * m
                kstart, qstart = seg + 4 * K0, seg + 4 * qlo
            tiles.append(dict(kstart=kstart, kstep=dil, qlo=qlo, n=qhi - qlo, bc0=bc0, slot=m,
                              boundary=(kstart < 0), qstart=qstart, qstep=dil, sbank=sb[m][0], soff=sb[m][1]))
    else:
        for rr in range(4):
            r = 4 * seg + rr
            for m in range(2):
                K0 = -64 + 128 * m
                tiles.append(dict(kstart=r + 16 * K0, kstep=16, qlo=rr * 128, n=128, bc0=128 if m == 0 else 0,
                                  slot=rr * 2 + m, boundary=(m == 0), qstart=r, qstep=16,
                                  sbank=rr // 2, soff=(rr % 2) * 256 + (128 if m == 0 else 0)))
    return tiles


def build_nc(debug=False):
    nc = bass.Bass("TRN2", target_bir_lowering=False)

    def din(name, shape):
        return nc.dram_tensor(name, list(shape), F32, kind="ExternalInput").ap()

    x_d = din("x", [NLOC, D])
    win_d = din("w_in", [D, 2560])
    wout_d = din("w_out", [D, D])
    w1_d = din("w_ff1", [D, 4096])
    w2_d = din("w_ff2", [4096, D])
    gbc_d = din("gbc", [4, 128, D])
    wsT_d = din("wsT", [128, 8, 128])
    sgf_d = din("sgf", [3, 128, 512])
    bias_d = din("biasT", [128, 24, 256])
    ident_d = din("ident", [128, 128])
    out_d = nc.dram_tensor("out", [NOWN, D], F32, kind="ExternalOutput").ap()
    x1s_d = nc.dram_tensor("x1s", [NOWN, D], F32, kind="Internal").ap()
    w1s_d = nc.dram_tensor("w1s", [D, 4096], BF16, kind="Internal").ap()
    w2s_d = nc.dram_tensor("w2s", [4096, D], BF16, kind="Internal").ap()
    wos_d = nc.dram_tensor("wos", [D, D], BF16, kind="Internal").ap()
    dbg = {}
    if debug:
        for nm, shp, dt_ in (("qT", [128, 4, NOWN], BF16), ("kT", [128, 4, PADK + NLOC], BF16), ("vT", [128, 4, PADK + NLOC], BF16),
                             ("catS", [128, 4, NOWN], BF16), ("catA", [128, 4, NOWN], BF16), ("x1", [128, 16, D], F32),
                             ("aT", [128, 32, 512], BF16), ("h2T", [128, 8, 512], BF16), ("w2", [128, 32, D], BF16),
                             ("w1", [128, 8, 4096], BF16), ("w1s", [D, 4096], BF16), ("w2s", [4096, D], BF16), ("wos", [D, D], BF16)):
            dbg[nm] = nc.dram_tensor("dbg_" + nm, shp, dt_, kind="ExternalOutput").ap()

    st = ExitStack()
    with st:
        ARENA = 206 * KB
        ar = Arena(nc, st, ARENA)
        ps_all = st.enter_context(nc.psum_tensor("ps", [128, 8 * 512], F32))
        PF = [ps_all[:, i * 512:(i + 1) * 512] for i in range(8)]
        PB = [PF[i].bitcast(BF16) for i in range(8)]
        P = Prog(nc)
        psk = lambda i: ("ps", i)

        L0 = Lay(ar, 0)
        ident = L0.get(BF16, [128])
        ss_all = L0.get(F32, [24])
        rstd_all = L0.get(F32, [24])
        stat = L0.get(F32, [64])
        epsr = L0.get(F32, [2])
        L0.off = 2 * KB
        catS = L0.get(BF16, [4, NOWN])
        BASE = L0.off

        P.add("pool", lambda e: e.dma_start(out=ident, in_=ident_d), writes=["ident"], dma=True)
        P.add("dve", lambda e: e.memset(ss_all, 0.0), writes=["ss_all"])
        P.add("dve", lambda e: e.memset(stat, 0.0), writes=["stat"])
        P.add("dve", lambda e: e.memset(epsr[:, 0:1], RMS_EPS), writes=["epsr"])
        P.add("dve", lambda e: e.memset(epsr[:, 1:2], LN_EPS), writes=["epsr"])

        L = Lay(ar, BASE)
        w_in = L.get(BF16, [8, 2560])
        qT = L.get(BF16, [4, NOWN])
        kT = L.get(BF16, [4, PADK + NLOC])
        vT = L.get(BF16, [4, PADK + NLOC])
        Q_END = L.off
        hT = [L.get(BF16, [8, 512]) for _ in range(2)]
        xt = [L.get(F32, [D]) for _ in range(2)]
        hb = [L.get(BF16, [D]) for _ in range(2)]
        uT = [L.get(BF16, [4, 512]) for _ in range(2)]
        GV_OFF = L.off
        gv = [L.get(F32, [512]) for _ in range(4)]
        nt = [L.get(BF16, [512]) for _ in range(4)]
        wsT = L.get(BF16, [8, 128])
        Cc = L.get(F32, [4, 128])
        lngF = L.get(F32, [4, 128])
        gA = L.get(F32, [D])
        xs = L.get(F32, [D])
        tmpS = [L.get(F32, [4, 128]) for _ in range(2)]
        lnbF, bsF = tmpS
        ones_bf = L.get(BF16, [64])
        bnst = L.get(F32, [4, 6])
        mv = L.get(F32, [4, 2])
        lnsd = L.get(F32, [4])
        lnr = L.get(F32, [4])
        assert L.off <= ARENA, L.off

        winv = win_d.rearrange("(c p) n -> p c n", p=128)
        for pc in range(5):
            P.add("pool", lambda e, pc=pc: e.dma_start(out=w_in[:, :, pc * 512:(pc + 1) * 512], in_=winv[:, :, pc * 512:(pc + 1) * 512]),
                  reads=([("w_in", pc - 1)] if pc > 0 else []), writes=[("w_in", pc)], dma=True, nobar=True)
            if pc == 0:
                P.add("pool", lambda e: e.memset(kT[:, :, 0:PADK], 0.0), writes=["kpad"])
                P.add("pool", lambda e: e.memset(vT[:, :, 0:PADK], 0.0), writes=["vpad"])
                P.add("pool", lambda e: e.dma_start(out=wsT, in_=wsT_d), writes=["wsT"], dma=True)
                P.add("pool", lambda e: e.memset(ones_bf, 1.0), writes=["ones_bf"])
        P.add("sp", lambda e: e.dma_start(out=gA, in_=gbc_d[0]), writes=["gA"], dma=True)
        for i, tdst in enumerate((lngF, lnbF, bsF)):
            P.add("sp", lambda e, i=i, tdst=tdst: e.dma_start(
                out=tdst, in_=sgf_d[i].rearrange("p (a b) -> p a b", b=128)), writes=[("sgf", i)], dma=True)

        xv = x_d.rearrange("(t p) d -> t p d", p=128)

        def stats_load(t):
            P.add("sp", lambda e: e.dma_start(out=xs, in_=xv[t]), writes=["xs"], dma=True)

        def stats_square(t):
            P.add("act", lambda e: e.activation(out=xs, in_=xs, func=AF.Square, accum_out=ss_all[:, t:t + 1]),
                  reads=["xs", "ss_all"], writes=["xs", ("ss", t)])

        def stats_finish(g):
            sl = slice(4 * g, 4 * g + 4)
            P.add("act", lambda e: e.activation(out=rstd_all[:, sl], in_=ss_all[:, sl], func=AF.Sqrt, bias=epsr[:, 0:1], scale=1.0 / D),
                  reads=[("ss", t) for t in range(4 * g, 4 * g + 4)] + ["epsr"], writes=[("sd", g)])
            P.add("dve", lambda e: e.reciprocal(out=rstd_all[:, sl], in_=rstd_all[:, sl]), reads=[("sd", g)], writes=[("rstd", g)])

        def stats_group(g):
            for t in range(4 * g, 4 * g + 4):
                stats_load(t)
                stats_square(t)
            stats_finish(g)

        gvbig = ar.view(GV_OFF, F32, [D])
        pre0 = [(xt[0], ("xt", 0)), (xt[1], ("xt", 1)), (xs, "xs"), (gvbig, "gvbig")]
        for t in range(4):
            q_ = "sp" if t % 2 == 0 else "act"
            P.add(q_, lambda e, t=t: e.dma_start(out=pre0[t][0], in_=xv[t]), writes=[pre0[t][1]], dma=True)
        for t in range(4):
            P.add("act", lambda e, t=t: e.activation(out=hb[t % 2], in_=pre0[t][0], func=AF.Square, accum_out=ss_all[:, t:t + 1]),
                  reads=[pre0[t][1], "ss_all"], writes=[("hb", t % 2), ("ss", t)])
        stats_finish(0)

        for a in range(4):
            for e2 in range(2):
                g_ = 2 * a + e2
                P.add("pe", lambda e, a=a, e2=e2, g_=g_: e.matmul(
                    PF[5][e2 * 64:(e2 + 1) * 64, a * 128:(a + 1) * 128], lhsT=ones_bf, rhs=wsT[:, g_, :],
                    start=True, stop=True), reads=["ones_bf", "wsT"], writes=[psk(5)])
        pf5v = PF[5].rearrange("p (a b) -> p a b", b=128)
        P.add("dve", lambda e: e.tensor_tensor(out=Cc, in0=pf5v, in1=lnbF, op=ALU.mult),
              reads=[psk(5), ("sgf", 1)], writes=["Cc0"])
        P.add("dve", lambda e: e.tensor_tensor(out=Cc, in0=Cc, in1=bsF, op=ALU.add),
              reads=["Cc0", ("sgf", 2)], writes=["Cc"])

        casts = [lambda: P.add("pool", lambda e: e.dma_start(out=wos_d, in_=wout_d), writes=["wos"], dma=True, nobar=True)]
        for pc in range(8):
            casts.append(lambda pc=pc: P.add("pool", lambda e: e.dma_start(
                out=w2s_d[pc * 512:(pc + 1) * 512, :], in_=w2_d[pc * 512:(pc + 1) * 512, :]), writes=[("w2s", pc)], dma=True, nobar=True))
        cnt2 = dict(pb=0)

        def prep_tile(t, pre=None):
            if pre is None:
                P.add("sp", lambda e: e.dma_start(out=xt[t % 2], in_=xv[t]), writes=[("xt", t % 2)], dma=True)
                src, skey = xt[t % 2], ("xt", t % 2)
            else:
                src, skey = pre
            P.add("dve", lambda e: e.scalar_tensor_tensor(
                out=hb[t % 2], in0=src, scalar=rstd_all[:, t:t + 1], in1=gA, op0=ALU.mult, op1=ALU.mult),
                reads=[skey, ("rstd", t // 4), "gA"], writes=[("hb", t % 2)])

        def transpose_tile(t):
            g, j = t // 4, t % 4
            bank = 6 + (t % 2)
            for kc in range(8):
                P.add("pe", lambda e, kc=kc: e.transpose(
                    out=PB[bank][:, kc * 128:(kc + 1) * 128], in_=hb[t % 2][:, kc * 128:(kc + 1) * 128],
                    identity=ident), reads=[("hb", t % 2), "ident"], writes=[psk(bank)])
            src = PB[bank].rearrange("p (a b) -> p a b", b=128)
            dst = hT[g % 2][:, :, j * 128:(j + 1) * 128]
            if t % 2 == 0:
                P.add("act", lambda e: e.activation(out=dst, in_=src, func=AF.Copy), reads=[psk(bank)], writes=[("hT", g % 2, j)])
            else:
                P.add("dve", lambda e: e.tensor_copy(out=dst, in_=src), reads=[psk(bank)], writes=[("hT", g % 2, j)])

        def proj_chunk(g, kind, c):
            hTg = hT[g % 2]
            hreads = [("hT", g % 2, j) for j in range(4)]
            oc = {"q": 0, "k": 4, "v": 8, "u": 12}[kind] + c
            bank = cnt2["pb"] % 3
            cnt2["pb"] += 1
            for kc in range(8):
                P.add("pe", lambda e, kc=kc: e.matmul(
                    PF[bank], lhsT=w_in[:, kc, oc * 128:(oc + 1) * 128], rhs=hTg[:, kc, :],
                    start=(kc == 0), stop=(kc == 7)), reads=hreads + [("w_in", oc // 4)], writes=[psk(bank)])
            if kind == "q":
                dst = qT[:, c, g * 512:(g + 1) * 512]
                P.add("act", lambda e: e.activation(out=dst, in_=PF[bank], func=AF.Copy, scale=0.125),
                      reads=[psk(bank)], writes=[("qT", c, g)])
            elif kind == "k":
                dst = kT[:, c, PADK + g * 512:PADK + (g + 1) * 512]
                P.add("dve", lambda e: e.tensor_copy(out=dst, in_=PF[bank]), reads=[psk(bank)], writes=[("kT", c, g)])
            elif kind == "v":
                dst = vT[:, c, PADK + g * 512:PADK + (g + 1) * 512]
                vw = [("vT", c, g)] + (["vpad"] if g == 0 else [])
                if c % 2 == 0:
                    P.add("act", lambda e: e.activation(out=dst, in_=PF[bank], func=AF.Copy), reads=[psk(bank)], writes=vw)
                else:
                    P.add("dve", lambda e: e.tensor_copy(out=dst, in_=PF[bank]), reads=[psk(bank)], writes=vw)
            else:
                dst = uT[g % 2][:, c, :]
                P.add("act", lambda e: e.activation(out=dst, in_=PF[bank], func=AF.Gelu_apprx_tanh),
                      reads=[psk(bank)], writes=[("uT", g % 2, c)])

        def zv_tile(g, j):
            hTg = hT[g % 2]
            bank = 3 + (j % 2)
            for kc in range(8):
                P.add("pe", lambda e, kc=kc: e.matmul(
                    PF[bank], lhsT=hTg[:, kc, j * 128:(j + 1) * 128], rhs=w_in[:, kc, 2048:2560],
                    start=(kc == 0), stop=(kc == 7)), reads=[("hT", g % 2, j), ("w_in", 4)], writes=[psk(bank)])
            P.add("act", lambda e: e.activation(out=gv[j], in_=PF[bank], func=AF.Gelu_apprx_tanh),
                  reads=[psk(bank)], writes=[("gv", j)] + (["gvbig"] if j < 2 else []))
            P.add("dve", lambda e: e.bn_stats(out=bnst[:, j, :], in_=gv[j]), reads=[("gv", j)], writes=[("bnst", j)])
            P.add("dve", lambda e: e.bn_aggr(out=mv[:, j, :], in_=bnst[:, j, :]), reads=[("bnst", j)], writes=[("mv", j)])

        def ln_group(g):
            P.add("act", lambda e: e.activation(out=lnsd, in_=mv[:, :, 1], func=AF.Sqrt, bias=epsr[:, 1:2], scale=1.0),
                  reads=[("mv", j) for j in range(4)] + ["epsr"], writes=["lnsd"])
            P.add("dve", lambda e: e.reciprocal(out=lnr, in_=lnsd), reads=["lnsd"], writes=["lnr"])
            for j in range(4):
                P.add("dve", lambda e, j=j: e.tensor_scalar(out=nt[j], in0=gv[j], scalar1=mv[:, j, 0:1], scalar2=lnr[:, j:j + 1],
                                                            op0=ALU.subtract, op1=ALU.mult),
                      reads=[("gv", j), ("mv", j), "lnr"], writes=[("nt", j)])

        def sgu_tile(g, j):
            t = 4 * g + j
            for a in range(4):
                for e2 in range(2):
                    g_ = 2 * a + e2
                    P.add("pe", lambda e, a=a, e2=e2, g_=g_: e.matmul(
                        PF[5][e2 * 64:(e2 + 1) * 64, a * 128:(a + 1) * 128], lhsT=nt[j][:, g_ * 64:(g_ + 1) * 64],
                        rhs=wsT[:, g_, :], start=True, stop=True), reads=[("nt", j), "wsT"], writes=[psk(5)])
            tm = tmpS[j % 2]
            P.add("dve", lambda e: e.tensor_tensor(out=tm, in0=pf5v, in1=lngF, op=ALU.mult),
                  reads=[psk(5), ("sgf", 0)], writes=[("sgf", 1 + j % 2)])
            P.add("pool", lambda e: e.tensor_tensor(out=tm, in0=tm, in1=Cc, op=ALU.add),
                  reads=[("sgf", 1 + j % 2), "Cc"], writes=[("sgf", 1 + j % 2)])
            dst = catS[:, :, t * 128:(t + 1) * 128]
            usrc = uT[g % 2][:, :, j * 128:(j + 1) * 128]
            P.add("pool", lambda e: e.tensor_tensor(out=dst, in0=tm, in1=usrc, op=ALU.mult),
                  reads=[("sgf", 1 + j % 2)] + [("uT", g % 2, c) for c in range(4)], writes=[("catS", t)])

        for j in range(4):
            prep_tile(j, pre=pre0[j])
            transpose_tile(j)
        stats_group(1)
        for g in range(6):
            plan = []
            if g < 4:
                plan += [("q", c) for c in range(4)]
            plan += [("k", c) for c in range(4)] + [("v", c) for c in range(4)]
            if g < 4:
                plan += [("u", c) for c in range(4)]
            per = len(plan) // 4
            if g + 1 < 6:
                prep_tile(4 * (g + 1))
            for ci, (kind, c) in enumerate(plan):
                proj_chunk(g, kind, c)
                if g + 2 < 6:
                    k_, ph = divmod(ci, per)
                    if per >= 4:
                        if ph == 1:
                            stats_load(4 * (g + 2) + k_)
                        elif ph == 3:
                            stats_square(4 * (g + 2) + k_)
                    else:
                        if ph == 0:
                            stats_load(4 * (g + 2) + k_)
                        else:
                            stats_square(4 * (g + 2) + k_)
                    if ci == len(plan) - 1:
                        stats_finish(g + 2)
                if (ci + 1) % per == 0 and g + 1 < 6:
                    jn = (ci + 1) // per - 1
                    transpose_tile(4 * (g + 1) + jn)
                    if jn + 1 < 4:
                        prep_tile(4 * (g + 1) + jn + 1)
                    if 1 <= g <= 4:
                        sgu_tile(g - 1, jn)
            if g < 4:
                for j in range(4):
                    zv_tile(g, j)
                ln_group(g)
                for _ in range(4):
                    if casts:
                        casts.pop(0)()
            else:
                while casts:
                    casts.pop(0)()

        P.barrier()
        dbg_ops = []
        if debug:
            for nm, buf in (("qT", qT), ("kT", kT), ("vT", vT), ("catS", catS)):
                dbg_ops.append(P.add("sp", lambda e, nm=nm, buf=buf: e.dma_start(out=dbg[nm], in_=buf), dma=True))
        L = Lay(ar, BASE)
        catA = L.get(BF16, [4, NOWN])
        biasT = L.get(BF16, [24, 256])
        Vaug = [L.get(BF16, [8, 2, 128]) for _ in range(2)]
        assert L.off <= BASE + 40 * KB
        L = Lay(ar, Q_END)
        qmA = L.get(BF16, [4, NOWN])
        PTs = [L.get(BF16, [1024]) for _ in range(3)]
        rden = L.get(F32, [NOWN])
        ACC0_OFF = L.off
        accs = [L.get(F32, [2, NOWN]) for _ in range(2)]
        ACC1_OFF = ACC0_OFF + 16 * KB
        w_out = ar.view(ACC0_OFF, BF16, [8, D])
        assert L.off <= ARENA, L.off
        qm = [qmA, qT]
        for hb_ in range(0, 24, 6):
            P.add("pool", lambda e, hb_=hb_: e.dma_start(out=biasT[:, hb_:hb_ + 6, :], in_=bias_d[:, hb_:hb_ + 6, :]),
                  writes=[("biasT", hb_)], dma=True)
        def q_mask(c):
            P.add("pool", lambda e: e.memset(qmA[64:128, c, :], 0.0), writes=[("qm", 0, c)])
            if c % 2 == 0:
                P.add("act", lambda e: e.activation(out=qmA[0:64, c, :], in_=qT[0:64, c, :], func=AF.Copy),
                      reads=[("qm", 1, c)], writes=[("qm", 0, c)])
            else:
                P.add("dve", lambda e: e.tensor_copy(out=qmA[0:64, c, :], in_=qT[0:64, c, :]),
                      reads=[("qm", 1, c)], writes=[("qm", 0, c)])
            P.add("pool", lambda e: e.memset(qT[0:64, c, :], 0.0), writes=[("qm", 1, c)])

        q_mask(0)

        bias_reads = [("biasT", i) for i in range(0, 24, 6)]
        def bias_exp(hb_):
            P.add("act", lambda e: e.activation(out=biasT[:, hb_:hb_ + 6, :], in_=biasT[:, hb_:hb_ + 6, :], func=AF.Exp),
                  reads=[("biasT", hb_)], writes=[("biasT", hb_)])

        bias_exp(0)
        steps = [(c, br, seg, e2) for c in range(4) for br in range(3) for seg in range(4) for e2 in range(2)]
        NS = len(steps)
        pending_norm = []
        vones_state = [None, None]

        def seg_ctx(i):
            c, br, seg, e2 = steps[i]
            it_ = i // 2
            return c, br, seg, e2, it_, seg_tiles(br, seg), Vaug[it_ % 2], 6 + (it_ % 2), it_ % 2

        def emit_qk(i):
            c, br, seg, e2, it_, tiles, Va, vbank, vb = seg_ctx(i)
            ntile = len(tiles)
            if e2 == 0:
                for tl in tiles:
                    cs = PADK + tl["kstart"]
                    src = vT[:, c, cs: cs + 127 * tl["kstep"] + 1: tl["kstep"]]
                    P.add("pe", lambda e, src=src, sl=tl["slot"], vbank=vbank: e.transpose(
                        out=PB[vbank][:, sl * 128:(sl + 1) * 128], in_=src, identity=ident),
                        reads=["vT_all", "ident"], writes=[psk(vbank)])
                srcv = PB[vbank][:, 0:ntile * 128].rearrange("p (s h d) -> p s h d", h=2, d=64)
                dstv = Va[:, 0:ntile, :, 0:64]
                P.add("act", lambda e, srcv=srcv, dstv=dstv: e.activation(out=dstv, in_=srcv, func=AF.Copy),
                      reads=[psk(vbank)], writes=[("Vaug", vb)])
                want = frozenset(tl["slot"] for tl in tiles if tl["boundary"])
                have = vones_state[vb]
                if have is None:
                    P.add("pool", lambda e, Va=Va: e.memset(Va[:, :, :, 64:128], 1.0), writes=[("Vones", vb)])
                    have = frozenset()
                for sl in sorted(have - want):
                    P.add("pool", lambda e, Va=Va, sl=sl: e.memset(Va[0:64, sl, :, 64:128], 1.0), writes=[("Vones", vb)])
                for sl in sorted(want - have):
                    P.add("pool", lambda e, Va=Va, sl=sl: e.memset(Va[0:64, sl, :, 64:128], 0.0), writes=[("Vones", vb)])
                vones_state[vb] = want
            h = 2 * c + e2
            sbanks = [2 * (i % 2), 2 * (i % 2) + 1]
            pt = PTs[i % 3]
            ptk = ("PT", i % 3)
            hbi = h * 3 + br
            for tl in tiles:
                sb = sbanks[tl["sbank"]]
                n = tl["n"]
                so = tl["soff"]
                ks = PADK + tl["kstart"]
                kap = kT[:, c, ks: ks + 127 * tl["kstep"] + 1: tl["kstep"]]
                qap = qm[e2][:, c, tl["qstart"]: tl["qstart"] + (n - 1) * tl["qstep"] + 1: tl["qstep"]]
                P.add("pe", lambda e, sb=sb, so=so, n=n, kap=kap, qap=qap: e.matmul(
                    PF[sb][:, so:so + n], lhsT=kap, rhs=qap, start=True, stop=True, skip_group_check=True),
                    reads=["kT_all", ("qm", e2, c)], writes=[psk(sb)])
            for bi in range(2):
                P.add("act", lambda e, bi=bi, sbanks=sbanks, pt=pt: e.activation(
                    out=pt[:, bi * 512:(bi + 1) * 512], in_=PF[sbanks[bi]], func=AF.Exp),
                    reads=[psk(sbanks[bi])], writes=[(ptk, bi)])
            ebv = biasT[:, hbi:hbi + 1, :].broadcast_to([128, 2, 256])
            for bi, eng_ in ((0, "pool"), (1, "dve")):
                ptv = pt[:, bi * 512:(bi + 1) * 512].rearrange("p (a b) -> p a b", b=256)
                P.add(eng_, lambda e, ptv=ptv, ebv=ebv: e.tensor_tensor(out=ptv, in0=ptv, in1=ebv, op=ALU.mult),
                      reads=[(ptk, bi), ("biasT", hbi // 6 * 6)], writes=[(ptk, bi)])

        def emit_pv(i):
            c, br, seg, e2, it_, tiles, Va, vbank, vb = seg_ctx(i)
            ntile = len(tiles)
            acc = accs[c % 2]
            akey = ("acc", c % 2, e2)
            obank = 4 + (i % 2)
            pt = PTs[i % 3]
            ptk = ("PT", i % 3)
            for ti, tl in enumerate(tiles):
                n = tl["n"]
                po = tl["sbank"] * 512 + tl["soff"]
                P.add("pe", lambda e, ti=ti, tl=tl, n=n, po=po: e.matmul(
                    PF[obank][:, tl["qlo"]:tl["qlo"] + n], lhsT=Va[:, tl["slot"], e2, :], rhs=pt[:, po:po + n],
                    start=(ti == 0), stop=(ti == ntile - 1), skip_group_check=True),
                    reads=[("Vaug", vb), ("Vones", vb), (ptk, tl["sbank"])], writes=[psk(obank)])
            if br == 0:
                dst = acc[:, e2, seg * 512:(seg + 1) * 512]
                P.add("dve", lambda e: e.tensor_copy(out=dst, in_=PF[obank]), reads=[psk(obank)], writes=[akey])
            elif br == 1:
                dst = acc[:, e2, seg:NOWN:4]
                P.add("dve", lambda e: e.tensor_tensor(out=dst, in0=PF[obank], in1=dst, op=ALU.add),
                      reads=[psk(obank), akey], writes=[akey])
            else:
                dst = acc[:, e2, :].rearrange("p (i r) -> p r i", r=16)[:, 4 * seg:4 * seg + 4, :]
                srco = PF[obank].rearrange("p (r i) -> p r i", i=128)
                P.add("dve", lambda e: e.tensor_tensor(out=dst, in0=srco, in1=dst, op=ALU.add),
                      reads=[psk(obank), akey], writes=[akey])
            if (br, seg, e2) == (2, 3, 1):
                for ee in range(2):
                    for blk in range(4):
                        pending_norm.append((c, ee, blk))
            elif i % 2 == 1 and pending_norm:
                emit_norm(*pending_norm.pop(0))

        def emit_norm(c, ee, blk):
            acc = accs[c % 2]
            akey = ("acc", c % 2, ee)
            cols = slice(blk * 512, (blk + 1) * 512)
            P.add("act", lambda e: e.activation(out=rden[0:64, cols], in_=acc[64:128, ee, cols], func=AF.Ln),
                  reads=[akey], writes=[("rden", blk)])
            P.add("act", lambda e: e.activation(out=rden[0:64, cols], in_=rden[0:64, cols], func=AF.Exp, scale=-1.0),
                  reads=[("rden", blk)], writes=[("rden", blk)])
            dst = catA[ee * 64:(ee + 1) * 64, c, cols]
            P.add("pool", lambda e: e.tensor_tensor(out=dst, in0=acc[0:64, ee, cols], in1=rden[0:64, cols], op=ALU.mult),
                  reads=[akey, ("rden", blk)], writes=[("catA", c, ee, blk)])

        woutv = wos_d.rearrange("(c p) n -> p c n", p=128)
        for i in range(NS + 2):
            if i in (4, 8, 12):
                bias_exp(6 * (i // 4))
                q_mask(i // 4)
            if i < NS:
                emit_qk(i)
            if i >= 2:
                emit_pv(i - 2)
            if i == NS - 6:
                for c2 in range(0, 8, 2):
                    P.add("sp", lambda e, c2=c2: e.dma_start(out=w_out[:, c2:c2 + 2, :], in_=woutv[:, c2:c2 + 2, :]),
                          reads=[("acc", 0, 0), ("acc", 0, 1), "wos"], writes=[("acc", 0, 0), ("acc", 0, 1), ("w_out", c2)], dma=True)
        while pending_norm:
            emit_norm(*pending_norm.pop(0))

        P.barrier()
        if debug:
            dbg_ops.append(P.add("sp", lambda e: e.dma_start(out=dbg["catA"], in_=catA), dma=True))
        L = Lay(ar, BASE + 16 * KB)
        w1 = L.get(BF16, [8, 4096])
        w2 = L.get(BF16, [32, D])
        W2_END = L.off
        gB = L.get(F32, [D])
        sq4 = L.get(BF16, [D])
        assert L.off <= ACC0_OFF, (L.off, ACC0_OFF)
        L = Lay(ar, BASE + 16 * KB + 64 * KB)
        xt4 = [L.get(F32, [D]) for _ in range(2)]
        tmp4 = [L.get(F32, [D]) for _ in range(2)]
        L = Lay(ar, ACC1_OFF)
        h2T = [L.get(BF16, [8, 512])] * 2
        gC = L.get(F32, [D])
        hb5 = [L.get(BF16, [D])] * 2
        P5A_OFF = L.off
        assert L.off <= ARENA, L.off
        hb4 = [ar.view(BASE + 16 * KB + 64 * KB + 16 * KB + i * 2 * KB, BF16, [D]) for i in range(4)]
        pend_tr = []

        def p4_transposes(t):
            for kc in range(8):
                P.add("pe", lambda e, kc=kc: e.transpose(
                    out=PB[7][:, kc * 128:(kc + 1) * 128], in_=hb4[t][:, kc * 128:(kc + 1) * 128], identity=ident),
                    reads=[("hb4", t), "ident"], writes=[psk(7)])
            srcT = PB[7].rearrange("p (a b) -> p a b", b=128)
            dstT = h2T[0][:, :, t * 128:(t + 1) * 128]
            P.add("act", lambda e: e.activation(out=dstT, in_=srcT, func=AF.Copy), reads=[psk(7)], writes=[("h2T", 0, t)])

        P.add("sp", lambda e: e.dma_start(out=gB, in_=gbc_d[1]), writes=["gB"], dma=True)
        P.add("sp", lambda e: e.dma_start(out=gC, in_=gbc_d[2]), writes=["gC"], dma=True)
        w1f = w1_d.rearrange("(c p) n -> p c n", p=128)
        w2v = w2s_d.rearrange("(f p) n -> p f n", p=128)
        wst = [ar.view(BASE + 16 * KB + 64 * KB + 24 * KB + i * 16 * KB, F32, [8, 512]) for i in range(2)]
        wloads = []
        for pc in range(8):
            wloads.append(lambda pc=pc: P.add("sp", lambda e: e.dma_start(
                out=wst[pc % 2], in_=w1f[:, :, pc * 512:(pc + 1) * 512]), writes=[("wst", pc % 2)], dma=True))

        def w1_cast(pc):
            P.add("dve", lambda e: e.tensor_copy(out=w1[:, :, pc * 512:(pc + 1) * 512], in_=wst[pc % 2]),
                  reads=[("wst", pc % 2)], writes=[("w1", pc)])
        for f in range(0, 32, 4):
            wloads.append(lambda f=f: P.add("sp", lambda e: e.dma_start(out=w2[:, f:f + 4, :], in_=w2v[:, f:f + 4, :]),
                                            reads=[("w2s", f // 4)], writes=[("w2", f)], dma=True, nobar=True))
        x1v = x1s_d.rearrange("(t p) d -> t p d", p=128)
        def load_x4(t):
            P.add("sp", lambda e: e.dma_start(out=xt4[t % 2], in_=xv[t]), writes=[("xt4", t % 2)], dma=True)

        load_x4(0)
        for t in range(16):
            if t + 1 < 16:
                load_x4(t + 1)
            if t % 2 == 0:
                if t >= 2:
                    w1_cast(t // 2 - 1)
                wloads[t // 2]()
            for hh in range(2):
                bank = 2 * (t % 2) + hh
                for kc in range(8):
                    src = (catA[:, kc, t * 128:(t + 1) * 128] if kc < 4 else catS[:, kc - 4, t * 128:(t + 1) * 128])
                    P.add("pe", lambda e, src=src, kc=kc, hh=hh, bank=bank: e.matmul(
                        PF[bank], lhsT=src, rhs=w_out[:, kc, hh * 512:(hh + 1) * 512], start=(kc == 0), stop=(kc == 7)),
                        reads=["catA_all", "catS_all", ("w_out", kc // 2 * 2)], writes=[psk(bank)])
            if pend_tr and pend_tr[0] <= t - 2:
                p4_transposes(pend_tr.pop(0))
            b0 = 2 * (t % 2)
            ps2 = ps_all[:, b0 * 512:(b0 + 2) * 512]
            P.add("act", lambda e, t=t, ps2=ps2: e.activation(out=sq4, in_=ps2, func=AF.Square, accum_out=stat[:, 32 + t:33 + t]),
                  reads=[psk(b0), psk(b0 + 1), "stat"], writes=["sq4", ("sst", t)])
            P.add("act", lambda e, t=t: e.activation(out=stat[:, 48 + t:49 + t], in_=stat[:, 32 + t:33 + t], func=AF.Sqrt,
                                                     bias=epsr[:, 0:1], scale=1.0 / D),
                  reads=[("sst", t), "epsr"], writes=[("ssd", t)])
            P.add("dve", lambda e, t=t: e.reciprocal(out=stat[:, 48 + t:49 + t], in_=stat[:, 48 + t:49 + t]),
                  reads=[("ssd", t)], writes=[("rs4", t)])
            P.add("dve", lambda e, t=t, ps2=ps2: e.scalar_tensor_tensor(
                out=tmp4[t % 2], in0=ps2, scalar=stat[:, 48 + t:49 + t], in1=gB, op0=ALU.mult, op1=ALU.mult),
                reads=[psk(b0), psk(b0 + 1), ("rs4", t), "gB"], writes=[("tmp4", t % 2)])
            P.add("pool", lambda e, t=t: e.tensor_tensor(out=tmp4[t % 2], in0=tmp4[t % 2], in1=xt4[t % 2], op=ALU.add),
                  reads=[("tmp4", t % 2), ("xt4", t % 2)], writes=[("tmp4", t % 2)])
            P.add("sp", lambda e, t=t: e.dma_start(out=x1v[t], in_=tmp4[t % 2]), reads=[("tmp4", t % 2)],
                  writes=[("x1s", t)], dma=True)
            if t < 4:
                P.add("act", lambda e, t=t: e.activation(out=hb4[t], in_=tmp4[t % 2], func=AF.Square, accum_out=stat[:, t:t + 1]),
                      reads=[("tmp4", t % 2), "stat"], writes=[("hb4", t), ("s5", t)])
                P.add("act", lambda e, t=t: e.activation(out=stat[:, 8 + t:9 + t], in_=stat[:, t:t + 1], func=AF.Sqrt,
                                                         bias=epsr[:, 0:1], scale=1.0 / D),
                      reads=[("s5", t), "epsr"], writes=[("sd5", t)])
                P.add("dve", lambda e, t=t: e.reciprocal(out=stat[:, 8 + t:9 + t], in_=stat[:, 8 + t:9 + t]),
                      reads=[("sd5", t)], writes=[("r5", t)])
                P.add("dve", lambda e, t=t: e.scalar_tensor_tensor(
                    out=hb4[t], in0=tmp4[t % 2], scalar=stat[:, 8 + t:9 + t], in1=gC, op0=ALU.mult, op1=ALU.mult),
                    reads=[("tmp4", t % 2), ("r5", t), "gC"], writes=[("hb4", t)])
                pend_tr.append(t)

        w1_cast(7)
        P.barrier()
        if debug:
            x1dv = dbg["x1"]
            for t in range(16):
                dbg_ops.append(P.add("sp", lambda e, t=t: e.dma_start(out=x1dv[:, t, :], in_=x1v[t]), dma=True))
        L = Lay(ar, 2 * KB)
        aT = L.get(BF16, [32, 512])
        assert L.off <= BASE + 16 * KB, L.off
        L = Lay(ar, W2_END)
        gD = L.get(F32, [D])
        xa = [L.get(F32, [D]) for _ in range(2)]
        xb = [L.get(F32, [D]) for _ in range(2)]
        st5 = L.get(F32, [160])
        assert L.off <= ACC1_OFF, (L.off, ACC1_OFF)
        L = Lay(ar, P5A_OFF)
        RR_OFF = L.off
        rr_ = [L.get(F32, [512]) for _ in range(2)]
        ost = [L.get(F32, [D])] * 2
        sq5 = ar.view(RR_OFF, BF16, [D])
        assert L.off <= ARENA, L.off
        P.add("sp", lambda e: e.dma_start(out=gD, in_=gbc_d[3]), writes=["gD"], dma=True)
        P.add("dve", lambda e: e.memset(st5, 0.0), writes=["st5"])
        outv = out_d.rearrange("(t p) d -> t p d", p=128)
        final_ops = []
        cnt5 = dict(fb=0, rq=0)

        def prenorm_chain(t):
            P.add("act", lambda e: e.activation(out=hb5[0], in_=xa[t % 2], func=AF.Square, accum_out=st5[:, t:t + 1]),
                  reads=[("xa", t % 2), "st5"], writes=[("hb5", 0), ("s5", t)])
            P.add("act", lambda e: e.activation(out=st5[:, 16 + t:17 + t], in_=st5[:, t:t + 1], func=AF.Sqrt,
                                                bias=epsr[:, 0:1], scale=1.0 / D),
                  reads=[("s5", t), "epsr"], writes=[("sd5", t)])
            P.add("dve", lambda e: e.reciprocal(out=st5[:, 16 + t:17 + t], in_=st5[:, 16 + t:17 + t]),
                  reads=[("sd5", t)], writes=[("r5", t)])
            P.add("dve", lambda e: e.scalar_tensor_tensor(
                out=hb5[0], in0=xa[t % 2], scalar=st5[:, 16 + t:17 + t], in1=gC, op0=ALU.mult, op1=ALU.mult),
                reads=[("xa", t % 2), ("r5", t), "gC"], writes=[("hb5", 0)])

        def prenorm_transpose(t):
            j = t % 4
            for kc in range(8):
                P.add("pe", lambda e, kc=kc: e.transpose(
                    out=PB[7][:, kc * 128:(kc + 1) * 128], in_=hb5[0][:, kc * 128:(kc + 1) * 128], identity=ident),
                    reads=[("hb5", 0), "ident"], writes=[psk(7)])
            src = PB[7].rearrange("p (a b) -> p a b", b=128)
            dst = h2T[0][:, :, j * 128:(j + 1) * 128]
            P.add("act", lambda e: e.activation(out=dst, in_=src, func=AF.Copy), reads=[psk(7)], writes=[("h2T", 0, j)])

        def load_xa(t):
            P.add("sp", lambda e: e.dma_start(out=xa[t % 2], in_=x1v[t]), reads=[("x1s", t)], writes=[("xa", t % 2)], dma=True)

        def ff1_group(G):
            h2 = h2T[0]
            h2reads = [("h2T", 0, j) for j in range(4)]
            for F_ in range(32):
                if G == 0 and F_ % 4 == 0:
                    wloads[8 + F_ // 4]()
                bank = cnt5["fb"] % 3
                cnt5["fb"] += 1
                for kc in range(8):
                    P.add("pe", lambda e, F_=F_, kc=kc, bank=bank: e.matmul(
                        PF[bank], lhsT=w1[:, kc, F_ * 128:(F_ + 1) * 128], rhs=h2[:, kc, :], start=(kc == 0), stop=(kc == 7)),
                        reads=[("w1", F_ // 4)] + h2reads, writes=[psk(bank)])
                r_ = rr_[cnt5["rq"] % 2]
                rk = ("rr", cnt5["rq"] % 2)
                cnt5["rq"] += 1
                P.add("act", lambda e, r_=r_, bank=bank: e.activation(out=r_, in_=PF[bank], func=AF.Relu),
                      reads=[psk(bank)], writes=[rk])
                P.add("dve", lambda e, r_=r_, F_=F_: e.tensor_tensor(out=aT[:, F_, :], in0=r_, in1=r_, op=ALU.mult),
                      reads=[rk], writes=[("aT", F_)])

        def ff2_mm(t):
            j = t % 4
            for hh in range(2):
                bank = 3 + 2 * (t % 2) + hh
                for F_ in range(32):
                    P.add("pe", lambda e, F_=F_, hh=hh, bank=bank: e.matmul(
                        PF[bank], lhsT=aT[:, F_, j * 128:(j + 1) * 128], rhs=w2[:, F_, hh * 512:(hh + 1) * 512],
                        start=(F_ == 0), stop=(F_ == 31)), reads=[("aT", F_), ("w2", F_ // 4 * 4)], writes=[psk(bank)])

        def ff2_evac(t):
            b0 = 3 + 2 * (t % 2)
            ps2 = ps_all[:, b0 * 512:(b0 + 2) * 512]
            P.add("act", lambda e: e.activation(out=sq5, in_=ps2, func=AF.Square, accum_out=st5[:, 96 + t:97 + t]),
                  reads=[psk(b0), psk(b0 + 1), "st5"], writes=[("rr", 0), ("sft", t)])
            P.add("act", lambda e: e.activation(out=st5[:, 112 + t:113 + t], in_=st5[:, 96 + t:97 + t], func=AF.Sqrt,
                                                bias=epsr[:, 0:1], scale=1.0 / D),
                  reads=[("sft", t), "epsr"], writes=[("sfd", t)])
            P.add("dve", lambda e: e.reciprocal(out=st5[:, 112 + t:113 + t], in_=st5[:, 112 + t:113 + t]),
                  reads=[("sfd", t)], writes=[("rf", t)])
            P.add("dve", lambda e: e.scalar_tensor_tensor(
                out=ost[0], in0=ps2, scalar=st5[:, 112 + t:113 + t], in1=gD, op0=ALU.mult, op1=ALU.mult),
                reads=[psk(b0), psk(b0 + 1), ("rf", t), "gD"], writes=[("ost", 0)])
            P.add("pool", lambda e: e.tensor_tensor(out=ost[0], in0=ost[0], in1=xb[t % 2], op=ALU.add),
                  reads=[("ost", 0), ("xb", t % 2)], writes=[("ost", 0)])
            o = P.add("sp", lambda e: e.dma_start(out=outv[t], in_=ost[0]), reads=[("ost", 0)], writes=[("ost", 0)], dma=True)
            final_ops.append(o)

        def load_xb(t):
            P.add("sp", lambda e: e.dma_start(out=xb[t % 2], in_=x1v[t]), reads=[("x1s", t)], writes=[("xb", t % 2)], dma=True)

        for G in range(4):
            ff1_group(G)
            for j in range(4):
                t = 4 * G + j
                load_xb(t)
                if G < 3:
                    load_xa(t + 4)
                    prenorm_chain(t + 4)
                ff2_mm(t)
                if G < 3:
                    prenorm_transpose(t + 4)
                ff2_evac(t)

        if debug:
            P.barrier()
            for nm, buf in (("aT", aT), ("h2T", h2T[0]), ("w2", w2), ("w1", w1), ("w1s", w1s_d), ("w2s", w2s_d), ("wos", wos_d)):
                dbg_ops.append(P.add("sp", lambda e, nm=nm, buf=buf: e.dma_start(out=dbg[nm], in_=buf), dma=True))
        P.finalize(st)
        with nc.Block() as block:
            P.emit_all(block, final_ops + dbg_ops)
    return nc


def _t5_bucket(rel):
    half, max_exact = 16, 8
    ret = np.where(rel > 0, half, 0)
    n = np.abs(rel)
    nf = np.maximum(n, 1).astype(np.float32)
    large = max_exact + (np.log(nf / np.float32(max_exact)) / np.float32(math.log(1024 / max_exact))
                         * np.float32(half - max_exact)).astype(np.int32)
    large = np.minimum(large, half - 1)
    return ret + np.where(n < max_exact, n, large)


def _bias_tiles(rel_bias, sign):
    kl = np.arange(128)[:, None]
    ql = np.arange(256)[None, :]
    j = kl - ql + 64
    valid = np.abs(j) <= 64
    out = np.empty((128, 24, 256), np.float32)
    for br, dil in enumerate((1, 4, 16)):
        bidx = _t5_bucket((sign * dil * j).astype(np.int32))
        for h in range(8):
            tile = rel_bias[bidx, h]
            out[:, h * 3 + br, :] = np.where(valid, tile, np.float32(MASKV))
    return out


_NC_CACHE = {}


def kernel(x, g_pre_mix, w_in, sgu_ln_g, sgu_ln_b, sgu_w, sgu_b, w_out, g_post_mix,
           g_pre_ffn, w_ff1, w_ff2, g_post_ffn, rel_bias, _debug=None):
    f32 = np.float32
    x = np.asarray(x, f32)
    B = x.shape[0]
    key = "dbg" if _debug is not None else "nc"
    if key not in _NC_CACHE:
        _NC_CACHE[key] = build_nc(debug=_debug is not None)
    nc = _NC_CACHE[key]
    gbc = np.stack([np.broadcast_to(np.asarray(v, f32)[0][None, :], (128, D))
                    for v in (g_pre_mix, g_post_mix, g_pre_ffn, g_post_ffn)]).astype(f32)
    lng = np.asarray(sgu_ln_g, f32)[0]
    lnb = np.asarray(sgu_ln_b, f32)[0]
    Ws = np.asarray(sgu_w, f32)[0]
    bs = np.asarray(sgu_b, f32)[0]
    rb = np.asarray(rel_bias, f32)
    ident = np.eye(128, dtype=f32)
    pidx_g = (np.arange(4)[None, :] * 2 + (np.arange(128)[:, None] // 64))
    pidx_c = np.arange(128)[:, None] % 64
    feat = pidx_g * 64 + pidx_c
    lngF = np.broadcast_to(lng[feat][:, :, None], (128, 4, 128))
    lnbF = np.broadcast_to(lnb[feat][:, :, None], (128, 4, 128))
    in_maps = []
    for core in range(8):
        b, half = core // 2, core % 2
        if half == 0:
            idx = np.arange(NLOC)
            Wl, bl, sign = Ws, bs, 1
        else:
            idx = S - 1 - np.arange(NLOC)
            Wl, bl, sign = Ws[:, ::-1, ::-1], bs[:, ::-1], -1
        xl = np.ascontiguousarray(x[b][idx])
        wsT = np.ascontiguousarray(np.transpose(Wl, (2, 0, 1)))
        bsF = bl[pidx_g]
        sgf = np.stack([lngF.reshape(128, 512), lnbF.reshape(128, 512), bsF.reshape(128, 512)]).astype(f32)
        in_maps.append({
            "x": xl, "w_in": np.asarray(w_in, f32)[0], "w_out": np.asarray(w_out, f32)[0],
            "w_ff1": np.asarray(w_ff1, f32)[0], "w_ff2": np.asarray(w_ff2, f32)[0],
            "gbc": gbc, "wsT": wsT, "sgf": np.ascontiguousarray(sgf),
            "biasT": _bias_tiles(rb, sign), "ident": ident,
        })
    res = run_bass_kernel_spmd(nc, in_maps, core_ids=list(range(8)))
    if _debug is not None:
        _debug.extend(res.results)
    out = np.empty((B, S, D), f32)
    for core in range(8):
        b, half = core // 2, core % 2
        o = res.results[core]["out"]
        if half == 0:
            out[b, 0:NOWN] = o
        else:
            out[b, S - 1 - np.arange(NOWN)] = o
    return out
```

```python
import math
from contextlib import ExitStack

import numpy as np
import concourse.bass as bass
import concourse.mybir as mybir
from concourse.bass_utils import run_bass_kernel_spmd

F32 = mybir.dt.float32
BF16 = mybir.dt.bfloat16
AF = mybir.ActivationFunctionType
ALU = mybir.AluOpType
KB = 1024

D = 1024
S = 4096
NOWN = 2048
NLOC = 3072
PADK = 1024
RMS_EPS = 1e-6
LN_EPS = 1e-5
MASKV = -30000.0
STRICT_SAME_ENGINE = True


class Prog:
    def __init__(self, nc, n_dma_sems=8):
        self.nc = nc
        self.ops = []
        self.last_writer = {}
        self.readers = {}
        self.n_dma_sems = n_dma_sems
        self.bar_deps = set()
        self.since_bar_dma = []
        self.last_on = {}

    def add(self, eng, emit, reads=(), writes=(), dma=False, nobar=False):
        oid = len(self.ops)
        deps = set(self.bar_deps)
        for r in reads:
            if r in self.last_writer:
                deps.add(self.last_writer[r])
        for w in writes:
            if w in self.last_writer:
                deps.add(self.last_writer[w])
            for rd in self.readers.get(w, ()):
                deps.add(rd)
        for r in reads:
            self.readers.setdefault(r, []).append(oid)
        for w in writes:
            self.last_writer[w] = oid
            self.readers[w] = []
        deps.discard(oid)
        self.ops.append(dict(id=oid, eng=eng, emit=emit, deps=deps, dma=dma,
                             reads=tuple(reads), writes=tuple(writes)))
        if dma:
            if not nobar:
                self.since_bar_dma.append(oid)
        else:
            self.last_on[eng] = oid
        return oid

    def barrier(self):
        deps = set(self.since_bar_dma)
        for e, oid in self.last_on.items():
            deps.add(oid)
        self.bar_deps = deps
        self.since_bar_dma = []

    def finalize(self, stack):
        nc = self.nc
        ops = self.ops
        for op in ops:
            keep = set()
            for d in op["deps"]:
                dop = ops[d]
                if dop["dma"] or op["dma"]:
                    keep.add(d)
                    continue
                if dop["eng"] == op["eng"]:
                    if op["eng"] == "pe":
                        continue
                    if STRICT_SAME_ENGINE or (set(dop["writes"]) & set(op["reads"])):
                        keep.add(d)
                    continue
                keep.add(d)
            op["deps"] = keep
        signaled = set()
        for op in ops:
            signaled |= op["deps"]
        self.esem = {e: stack.enter_context(nc.semaphore("s_" + e)) for e in ("pe", "act", "dve", "pool")}
        self.dsem = {}
        for q in ("sp", "act", "pool"):
            self.dsem[q] = [stack.enter_context(nc.semaphore(f"d_{q}{i}")) for i in range(self.n_dma_sems)]
        cnt = {e: 0 for e in self.esem}
        dcnt = {q: [0] * self.n_dma_sems for q in self.dsem}
        dnext = {q: 0 for q in self.dsem}
        for op in ops:
            if op["dma"]:
                q = op["eng"]
                i = dnext[q] % self.n_dma_sems
                dnext[q] += 1
                op["prev_tok"] = (self.dsem[q][i], dcnt[q][i]) if dcnt[q][i] > 0 else None
                dcnt[q][i] += 16
                op["tok"] = (self.dsem[q][i], dcnt[q][i])
                op["sig"] = True
            elif op["id"] in signaled:
                cnt[op["eng"]] += 1
                op["tok"] = (self.esem[op["eng"]], cnt[op["eng"]])
                op["sig"] = True
            else:
                op["sig"] = False

    def run_engine(self, ename, eng):
        ops = self.ops
        waited = {}
        for op in ops:
            if op["eng"] != ename:
                continue
            need = {}
            toks = [ops[d]["tok"] for d in op["deps"]]
            if op["dma"] and op["prev_tok"] is not None:
                toks.append(op["prev_tok"])
            for (s, v) in toks:
                k = id(s)
                if k not in need or need[k][1] < v:
                    need[k] = (s, v)
            for k, (s, v) in need.items():
                if waited.get(k, 0) >= v:
                    continue
                eng.wait_ge(s, v)
                waited[k] = v
            ins = op["emit"](eng)
            if op["sig"]:
                s, v = op["tok"]
                ins.then_inc(s, 16 if op["dma"] else 1)

    def emit_all(self, block, final_ops):
        P = self

        def mk(ename):
            def f(eng):
                P.run_engine(ename, eng)
                if ename == "sp":
                    for oid in final_ops:
                        s, v = P.ops[oid]["tok"]
                        eng.wait_ge(s, v)
            return f
        block.tensor(mk("pe"))
        block.scalar(mk("act"))
        block.vector(mk("dve"))
        block.gpsimd(mk("pool"))
        block.sync(mk("sp"))


class Arena:
    def __init__(self, nc, stack, nbytes):
        self.nbytes = nbytes
        self.t = stack.enter_context(nc.sbuf_tensor("arena", [128, nbytes // 2], BF16))

    def view(self, off, dtype, shape):
        n = 1
        for s_ in shape:
            n *= s_
        esz = 4 if dtype == F32 else 2
        assert off % 4 == 0 and off + n * esz <= self.nbytes, (off, n * esz, self.nbytes)
        a = self.t[:, off // 2: off // 2 + n * esz // 2]
        if dtype == F32:
            a = a.bitcast(F32)
        if len(shape) == 2:
            a = a.rearrange("p (a b) -> p a b", b=shape[1])
        elif len(shape) == 3:
            a = a.rearrange("p (a b c) -> p a b c", b=shape[1], c=shape[2])
        return a


class Lay:
    def __init__(self, arena, start):
        self.arena = arena
        self.off = start

    def get(self, dtype, shape):
        n = 1
        for s_ in shape:
            n *= s_
        esz = 4 if dtype == F32 else 2
        v = self.arena.view(self.off, dtype, shape)
        self.off += (n * esz + 3) // 4 * 4
        return v


def seg_tiles(br, seg):
    tiles = []
    if br in (0, 1):
        dil = 1 if br == 0 else 4
        rel = [(0, 128, 128), (0, 256, 0), (128, 384, 0), (256, 512, 0), (384, 512, 0)]
        sb = [(0, 128), (0, 256), (1, 0), (1, 256), (0, 0)]
        for m in range(5):
            qlo, qhi, bc0 = rel[m]
            if br == 0:
                k0 = 512 * seg - 64 + 128 * m
                kstart, qstart = k0, 512 * seg + qlo
            else:
                K0 = -64 + 128 * m
                kstart, qstart = seg + 4 * K0, seg + 4 * qlo
            tiles.append(dict(kstart=kstart, kstep=dil, qlo=qlo, n=qhi - qlo, bc0=bc0, slot=m,
                              boundary=(kstart < 0), qstart=qstart, qstep=dil, sbank=sb[m][0], soff=sb[m][1]))
    else:
        for rr in range(4):
            r = 4 * seg + rr
            for m in range(2):
                K0 = -64 + 128 * m
                tiles.append(dict(kstart=r + 16 * K0, kstep=16, qlo=rr * 128, n=128, bc0=128 if m == 0 else 0,
                                  slot=rr * 2 + m, boundary=(m == 0), qstart=r, qstep=16,
                                  sbank=rr // 2, soff=(rr % 2) * 256 + (128 if m == 0 else 0)))
    return tiles


def build_nc(debug=False):
    nc = bass.Bass("TRN2", target_bir_lowering=False)

    def din(name, shape):
        return nc.dram_tensor(name, list(shape), F32, kind="ExternalInput").ap()

    x_d = din("x", [NLOC, D])
    win_d = din("w_in", [D, 2560])
    wout_d = din("w_out", [D, D])
    w1_d = din("w_ff1", [D, 4096])
    w2_d = din("w_ff2", [4096, D])
    gbc_d = din("gbc", [4, 128, D])
    wsT_d = din("wsT", [128, 8, 128])
    sgf_d = din("sgf", [3, 128, 512])
    bias_d = din("biasT", [128, 24, 256])
    ident_d = din("ident", [128, 128])
    out_d = nc.dram_tensor("out", [NOWN, D], F32, kind="ExternalOutput").ap()
    x1s_d = nc.dram_tensor("x1s", [NOWN, D], F32, kind="Internal").ap()
    w1s_d = nc.dram_tensor("w1s", [D, 4096], BF16, kind="Internal").ap()
    w2s_d = nc.dram_tensor("w2s", [4096, D], BF16, kind="Internal").ap()
    wos_d = nc.dram_tensor("wos", [D, D], BF16, kind="Internal").ap()
    dbg = {}
    if debug:
        for nm, shp, dt_ in (("qT", [128, 4, NOWN], BF16), ("kT", [128, 4, PADK + NLOC], BF16), ("vT", [128, 4, PADK + NLOC], BF16),
                             ("catS", [128, 4, NOWN], BF16), ("catA", [128, 4, NOWN], BF16), ("x1", [128, 16, D], F32),
                             ("aT", [128, 32, 512], BF16), ("h2T", [128, 8, 512], BF16), ("w2", [128, 32, D], BF16),
                             ("w1", [128, 8, 4096], BF16), ("w1s", [D, 4096], BF16), ("w2s", [4096, D], BF16), ("wos", [D, D], BF16)):
            dbg[nm] = nc.dram_tensor("dbg_" + nm, shp, dt_, kind="ExternalOutput").ap()

    st = ExitStack()
    with st:
        ARENA = 206 * KB
        ar = Arena(nc, st, ARENA)
        ps_all = st.enter_context(nc.psum_tensor("ps", [128, 8 * 512], F32))
        PF = [ps_all[:, i * 512:(i + 1) * 512] for i in range(8)]
        PB = [PF[i].bitcast(BF16) for i in range(8)]
        P = Prog(nc)
        psk = lambda i: ("ps", i)

        L0 = Lay(ar, 0)
        ident = L0.get(BF16, [128])
        ss_all = L0.get(F32, [24])
        rstd_all = L0.get(F32, [24])
        stat = L0.get(F32, [64])
        epsr = L0.get(F32, [2])
        L0.off = 2 * KB
        catS = L0.get(BF16, [4, NOWN])
        BASE = L0.off

        P.add("pool", lambda e: e.dma_start(out=ident, in_=ident_d), writes=["ident"], dma=True)
        P.add("dve", lambda e: e.memset(ss_all, 0.0), writes=["ss_all"])
        P.add("dve", lambda e: e.memset(stat, 0.0), writes=["stat"])
        P.add("dve", lambda e: e.memset(epsr[:, 0:1], RMS_EPS), writes=["epsr"])
        P.add("dve", lambda e: e.memset(epsr[:, 1:2], LN_EPS), writes=["epsr"])

        L = Lay(ar, BASE)
        w_in = L.get(BF16, [8, 2560])
        qT = L.get(BF16, [4, NOWN])
        kT = L.get(BF16, [4, PADK + NLOC])
        vT = L.get(BF16, [4, PADK + NLOC])
        Q_END = L.off
        hT = [L.get(BF16, [8, 512]) for _ in range(2)]
        xt = [L.get(F32, [D]) for _ in range(2)]
        hb = [L.get(BF16, [D]) for _ in range(2)]
        uT = [L.get(BF16, [4, 512]) for _ in range(2)]
        GV_OFF = L.off
        gv = [L.get(F32, [512]) for _ in range(4)]
        nt = [L.get(BF16, [512]) for _ in range(4)]
        wsT = L.get(BF16, [8, 128])
        Cc = L.get(F32, [4, 128])
        lngF = L.get(F32, [4, 128])
        gA = L.get(F32, [D])
        xs = L.get(F32, [D])
        tmpS = [L.get(F32, [4, 128]) for _ in range(2)]
        lnbF, bsF = tmpS
        ones_bf = L.get(BF16, [64])
        bnst = L.get(F32, [4, 6])
        mv = L.get(F32, [4, 2])
        lnsd = L.get(F32, [4])
        lnr = L.get(F32, [4])
        assert L.off <= ARENA, L.off

        winv = win_d.rearrange("(c p) n -> p c n", p=128)
        for pc in range(5):
            P.add("pool", lambda e, pc=pc: e.dma_start(out=w_in[:, :, pc * 512:(pc + 1) * 512], in_=winv[:, :, pc * 512:(pc + 1) * 512]),
                  reads=([("w_in", pc - 1)] if pc > 0 else []), writes=[("w_in", pc)], dma=True, nobar=True)
            if pc == 0:
                P.add("pool", lambda e: e.memset(kT[:, :, 0:PADK], 0.0), writes=["kpad"])
                P.add("pool", lambda e: e.memset(vT[:, :, 0:PADK], 0.0), writes=["vpad"])
                P.add("pool", lambda e: e.dma_start(out=wsT, in_=wsT_d), writes=["wsT"], dma=True)
                P.add("pool", lambda e: e.memset(ones_bf, 1.0), writes=["ones_bf"])
        P.add("sp", lambda e: e.dma_start(out=gA, in_=gbc_d[0]), writes=["gA"], dma=True)
        for i, tdst in enumerate((lngF, lnbF, bsF)):
            P.add("sp", lambda e, i=i, tdst=tdst: e.dma_start(
                out=tdst, in_=sgf_d[i].rearrange("p (a b) -> p a b", b=128)), writes=[("sgf", i)], dma=True)

        xv = x_d.rearrange("(t p) d -> t p d", p=128)

        def stats_load(t):
            P.add("sp", lambda e: e.dma_start(out=xs, in_=xv[t]), writes=["xs"], dma=True)

        def stats_square(t):
            P.add("act", lambda e: e.activation(out=xs, in_=xs, func=AF.Square, accum_out=ss_all[:, t:t + 1]),
                  reads=["xs", "ss_all"], writes=["xs", ("ss", t)])

        def stats_finish(g):
            sl = slice(4 * g, 4 * g + 4)
            P.add("act", lambda e: e.activation(out=rstd_all[:, sl], in_=ss_all[:, sl], func=AF.Sqrt, bias=epsr[:, 0:1], scale=1.0 / D),
                  reads=[("ss", t) for t in range(4 * g, 4 * g + 4)] + ["epsr"], writes=[("sd", g)])
            P.add("dve", lambda e: e.reciprocal(out=rstd_all[:, sl], in_=rstd_all[:, sl]), reads=[("sd", g)], writes=[("rstd", g)])

        def stats_group(g):
            for t in range(4 * g, 4 * g + 4):
                stats_load(t)
                stats_square(t)
            stats_finish(g)

        gvbig = ar.view(GV_OFF, F32, [D])
        pre0 = [(xt[0], ("xt", 0)), (xt[1], ("xt", 1)), (xs, "xs"), (gvbig, "gvbig")]
        for t in range(4):
            q_ = "sp" if t % 2 == 0 else "act"
            P.add(q_, lambda e, t=t: e.dma_start(out=pre0[t][0], in_=xv[t]), writes=[pre0[t][1]], dma=True)
        for t in range(4):
            P.add("act", lambda e, t=t: e.activation(out=hb[t % 2], in_=pre0[t][0], func=AF.Square, accum_out=ss_all[:, t:t + 1]),
                  reads=[pre0[t][1], "ss_all"], writes=[("hb", t % 2), ("ss", t)])
        stats_finish(0)

        pf5v = PF[5].rearrange("p (a b) -> p a b", b=128)
        def sgu_const():
            for a in range(4):
                for e2 in range(2):
                    g_ = 2 * a + e2
                    P.add("pe", lambda e, a=a, e2=e2, g_=g_: e.matmul(
                        PF[5][e2 * 64:(e2 + 1) * 64, a * 128:(a + 1) * 128], lhsT=ones_bf, rhs=wsT[:, g_, :],
                        start=True, stop=True), reads=["ones_bf", "wsT"], writes=[psk(5)])
            P.add("dve", lambda e: e.tensor_tensor(out=Cc, in0=pf5v, in1=lnbF, op=ALU.mult),
                  reads=[psk(5), ("sgf", 1)], writes=["Cc0"])
            P.add("dve", lambda e: e.tensor_tensor(out=Cc, in0=Cc, in1=bsF, op=ALU.add),
                  reads=["Cc0", ("sgf", 2)], writes=["Cc"])

        casts = [lambda: P.add("pool", lambda e: e.dma_start(out=wos_d, in_=wout_d), writes=["wos"], dma=True, nobar=True)]
        for pc in range(8):
            casts.append(lambda pc=pc: P.add("pool", lambda e: e.dma_start(
                out=w2s_d[pc * 512:(pc + 1) * 512, :], in_=w2_d[pc * 512:(pc + 1) * 512, :]), writes=[("w2s", pc)], dma=True, nobar=True))
        cnt2 = dict(pb=0)

        def prep_tile(t, pre=None):
            if pre is None:
                P.add("sp", lambda e: e.dma_start(out=xt[t % 2], in_=xv[t]), writes=[("xt", t % 2)], dma=True)
                src, skey = xt[t % 2], ("xt", t % 2)
            else:
                src, skey = pre
            P.add("dve", lambda e: e.scalar_tensor_tensor(
                out=hb[t % 2], in0=src, scalar=rstd_all[:, t:t + 1], in1=gA, op0=ALU.mult, op1=ALU.mult),
                reads=[skey, ("rstd", t // 4), "gA"], writes=[("hb", t % 2)])

        def transpose_tile(t):
            g, j = t // 4, t % 4
            bank = 6 + (t % 2)
            for kc in range(8):
                P.add("pe", lambda e, kc=kc: e.transpose(
                    out=PB[bank][:, kc * 128:(kc + 1) * 128], in_=hb[t % 2][:, kc * 128:(kc + 1) * 128],
                    identity=ident), reads=[("hb", t % 2), "ident"], writes=[psk(bank)])
            src = PB[bank].rearrange("p (a b) -> p a b", b=128)
            dst = hT[g % 2][:, :, j * 128:(j + 1) * 128]
            if t % 2 == 0:
                P.add("act", lambda e: e.activation(out=dst, in_=src, func=AF.Copy), reads=[psk(bank)], writes=[("hT", g % 2, j)])
            else:
                P.add("dve", lambda e: e.tensor_copy(out=dst, in_=src), reads=[psk(bank)], writes=[("hT", g % 2, j)])

        def proj_chunk(g, kind, c):
            hTg = hT[g % 2]
            hreads = [("hT", g % 2, j) for j in range(4)]
            oc = {"q": 0, "k": 4, "v": 8, "u": 12}[kind] + c
            bank = cnt2["pb"] % 3
            cnt2["pb"] += 1
            for kc in range(8):
                P.add("pe", lambda e, kc=kc: e.matmul(
                    PF[bank], lhsT=w_in[:, kc, oc * 128:(oc + 1) * 128], rhs=hTg[:, kc, :],
                    start=(kc == 0), stop=(kc == 7)), reads=hreads + [("w_in", oc // 4)], writes=[psk(bank)])
            if kind == "q":
                dst = qT[:, c, g * 512:(g + 1) * 512]
                P.add("act", lambda e: e.activation(out=dst, in_=PF[bank], func=AF.Copy, scale=0.125),
                      reads=[psk(bank)], writes=[("qT", c, g)])
            elif kind == "k":
                dst = kT[:, c, PADK + g * 512:PADK + (g + 1) * 512]
                P.add("dve", lambda e: e.tensor_copy(out=dst, in_=PF[bank]), reads=[psk(bank)], writes=[("kT", c, g)])
            elif kind == "v":
                dst = vT[:, c, PADK + g * 512:PADK + (g + 1) * 512]
                vw = [("vT", c, g)] + (["vpad"] if g == 0 else [])
                if c % 2 == 0:
                    P.add("act", lambda e: e.activation(out=dst, in_=PF[bank], func=AF.Copy), reads=[psk(bank)], writes=vw)
                else:
                    P.add("dve", lambda e: e.tensor_copy(out=dst, in_=PF[bank]), reads=[psk(bank)], writes=vw)
            else:
                dst = uT[g % 2][:, c, :]
                P.add("act", lambda e: e.activation(out=dst, in_=PF[bank], func=AF.Gelu_apprx_tanh),
                      reads=[psk(bank)], writes=[("uT", g % 2, c)])

        def zv_tile(g, j):
            hTg = hT[g % 2]
            bank = 3 + (j % 2)
            for kc in range(8):
                P.add("pe", lambda e, kc=kc: e.matmul(
                    PF[bank], lhsT=hTg[:, kc, j * 128:(j + 1) * 128], rhs=w_in[:, kc, 2048:2560],
                    start=(kc == 0), stop=(kc == 7)), reads=[("hT", g % 2, j), ("w_in", 4)], writes=[psk(bank)])
            P.add("act", lambda e: e.activation(out=gv[j], in_=PF[bank], func=AF.Gelu_apprx_tanh),
                  reads=[psk(bank)], writes=[("gv", j)] + (["gvbig"] if j < 2 else []))
            P.add("dve", lambda e: e.bn_stats(out=bnst[:, j, :], in_=gv[j]), reads=[("gv", j)], writes=[("bnst", j)])
            P.add("dve", lambda e: e.bn_aggr(out=mv[:, j, :], in_=bnst[:, j, :]), reads=[("bnst", j)], writes=[("mv", j)])

        def ln_group(g):
            P.add("act", lambda e: e.activation(out=lnsd, in_=mv[:, :, 1], func=AF.Sqrt, bias=epsr[:, 1:2], scale=1.0),
                  reads=[("mv", j) for j in range(4)] + ["epsr"], writes=["lnsd"])
            P.add("dve", lambda e: e.reciprocal(out=lnr, in_=lnsd), reads=["lnsd"], writes=["lnr"])
            for j in range(4):
                P.add("dve", lambda e, j=j: e.tensor_scalar(out=nt[j], in0=gv[j], scalar1=mv[:, j, 0:1], scalar2=lnr[:, j:j + 1],
                                                            op0=ALU.subtract, op1=ALU.mult),
                      reads=[("gv", j), ("mv", j), "lnr"], writes=[("nt", j)])

        def sgu_tile(g, j):
            t = 4 * g + j
            for a in range(4):
                for e2 in range(2):
                    g_ = 2 * a + e2
                    P.add("pe", lambda e, a=a, e2=e2, g_=g_: e.matmul(
                        PF[5][e2 * 64:(e2 + 1) * 64, a * 128:(a + 1) * 128], lhsT=nt[j][:, g_ * 64:(g_ + 1) * 64],
                        rhs=wsT[:, g_, :], start=True, stop=True), reads=[("nt", j), "wsT"], writes=[psk(5)])
            tm = tmpS[j % 2]
            P.add("dve", lambda e: e.tensor_tensor(out=tm, in0=pf5v, in1=lngF, op=ALU.mult),
                  reads=[psk(5), ("sgf", 0)], writes=[("sgf", 1 + j % 2)])
            P.add("pool", lambda e: e.tensor_tensor(out=tm, in0=tm, in1=Cc, op=ALU.add),
                  reads=[("sgf", 1 + j % 2), "Cc"], writes=[("sgf", 1 + j % 2)])
            dst = catS[:, :, t * 128:(t + 1) * 128]
            usrc = uT[g % 2][:, :, j * 128:(j + 1) * 128]
            P.add("pool", lambda e: e.tensor_tensor(out=dst, in0=tm, in1=usrc, op=ALU.mult),
                  reads=[("sgf", 1 + j % 2)] + [("uT", g % 2, c) for c in range(4)], writes=[("catS", t)])

        for j in range(4):
            prep_tile(j, pre=pre0[j])
            transpose_tile(j)
        sgu_const()
        stats_group(1)
        for g in range(6):
            plan = []
            if g < 4:
                plan += [("q", c) for c in range(4)]
            plan += [("k", c) for c in range(4)] + [("v", c) for c in range(4)]
            if g < 4:
                plan += [("u", c) for c in range(4)]
            per = len(plan) // 4
            if g + 1 < 6:
                prep_tile(4 * (g + 1))
            for ci, (kind, c) in enumerate(plan):
                proj_chunk(g, kind, c)
                if g + 2 < 6:
                    k_, ph = divmod(ci, per)
                    if per >= 4:
                        if ph == 1:
                            stats_load(4 * (g + 2) + k_)
                        elif ph == 3:
                            stats_square(4 * (g + 2) + k_)
                    else:
                        if ph == 0:
                            stats_load(4 * (g + 2) + k_)
                        else:
                            stats_square(4 * (g + 2) + k_)
                    if ci == len(plan) - 1:
                        stats_finish(g + 2)
                if (ci + 1) % per == 0 and g + 1 < 6:
                    jn = (ci + 1) // per - 1
                    transpose_tile(4 * (g + 1) + jn)
                    if jn + 1 < 4:
                        prep_tile(4 * (g + 1) + jn + 1)
                    if 1 <= g <= 4:
                        sgu_tile(g - 1, jn)
            if g < 4:
                for j in range(4):
                    zv_tile(g, j)
                ln_group(g)
                for _ in range(4):
                    if casts:
                        casts.pop(0)()
            else:
                while casts:
                    casts.pop(0)()

        P.barrier()
        dbg_ops = []
        if debug:
            for nm, buf in (("qT", qT), ("kT", kT), ("vT", vT), ("catS", catS)):
                dbg_ops.append(P.add("sp", lambda e, nm=nm, buf=buf: e.dma_start(out=dbg[nm], in_=buf), dma=True))
        L = Lay(ar, BASE)
        catA = L.get(BF16, [4, NOWN])
        biasT = L.get(BF16, [24, 256])
        Vaug = [L.get(BF16, [8, 2, 128]) for _ in range(2)]
        assert L.off <= BASE + 40 * KB
        L = Lay(ar, Q_END)
        qmA = L.get(BF16, [4, NOWN])
        PTs = [L.get(BF16, [1024]) for _ in range(3)]
        rden = L.get(F32, [NOWN])
        ACC0_OFF = L.off
        accs = [L.get(F32, [2, NOWN]) for _ in range(2)]
        ACC1_OFF = ACC0_OFF + 16 * KB
        w_out = ar.view(ACC0_OFF, BF16, [8, D])
        assert L.off <= ARENA, L.off
        qm = [qmA, qT]
        for hb_ in range(0, 24, 6):
            P.add("pool", lambda e, hb_=hb_: e.dma_start(out=biasT[:, hb_:hb_ + 6, :], in_=bias_d[:, hb_:hb_ + 6, :]),
                  writes=[("biasT", hb_)], dma=True)
        def q_mask(c):
            P.add("pool", lambda e: e.memset(qmA[64:128, c, :], 0.0), writes=[("qm", 0, c)])
            if c % 2 == 0:
                P.add("act", lambda e: e.activation(out=qmA[0:64, c, :], in_=qT[0:64, c, :], func=AF.Copy),
                      reads=[("qm", 1, c)], writes=[("qm", 0, c)])
            else:
                P.add("dve", lambda e: e.tensor_copy(out=qmA[0:64, c, :], in_=qT[0:64, c, :]),
                      reads=[("qm", 1, c)], writes=[("qm", 0, c)])
            P.add("pool", lambda e: e.memset(qT[0:64, c, :], 0.0), writes=[("qm", 1, c)])

        q_mask(0)

        bias_reads = [("biasT", i) for i in range(0, 24, 6)]
        def bias_exp(hb_):
            P.add("act", lambda e: e.activation(out=biasT[:, hb_:hb_ + 6, :], in_=biasT[:, hb_:hb_ + 6, :], func=AF.Exp),
                  reads=[("biasT", hb_)], writes=[("biasT", hb_)])

        bias_exp(0)
        steps = [(c, br, seg, e2) for c in range(4) for br in range(3) for seg in range(4) for e2 in range(2)]
        NS = len(steps)
        pending_norm = []
        vones_state = [None, None]

        def seg_ctx(i):
            c, br, seg, e2 = steps[i]
            it_ = i // 2
            return c, br, seg, e2, it_, seg_tiles(br, seg), Vaug[it_ % 2], 6 + (it_ % 2), it_ % 2

        def emit_qk(i):
            c, br, seg, e2, it_, tiles, Va, vbank, vb = seg_ctx(i)
            ntile = len(tiles)
            if e2 == 0:
                for tl in tiles:
                    cs = PADK + tl["kstart"]
                    src = vT[:, c, cs: cs + 127 * tl["kstep"] + 1: tl["kstep"]]
                    P.add("pe", lambda e, src=src, sl=tl["slot"], vbank=vbank: e.transpose(
                        out=PB[vbank][:, sl * 128:(sl + 1) * 128], in_=src, identity=ident),
                        reads=["vT_all", "ident"], writes=[psk(vbank)])
                srcv = PB[vbank][:, 0:ntile * 128].rearrange("p (s h d) -> p s h d", h=2, d=64)
                dstv = Va[:, 0:ntile, :, 0:64]
                P.add("act", lambda e, srcv=srcv, dstv=dstv: e.activation(out=dstv, in_=srcv, func=AF.Copy),
                      reads=[psk(vbank)], writes=[("Vaug", vb)])
                want = frozenset(tl["slot"] for tl in tiles if tl["boundary"])
                have = vones_state[vb]
                if have is None:
                    P.add("pool", lambda e, Va=Va: e.memset(Va[:, :, :, 64:128], 1.0), writes=[("Vones", vb)])
                    have = frozenset()
                for sl in sorted(have - want):
                    P.add("pool", lambda e, Va=Va, sl=sl: e.memset(Va[0:64, sl, :, 64:128], 1.0), writes=[("Vones", vb)])
                for sl in sorted(want - have):
                    P.add("pool", lambda e, Va=Va, sl=sl: e.memset(Va[0:64, sl, :, 64:128], 0.0), writes=[("Vones", vb)])
                vones_state[vb] = want
            h = 2 * c + e2
            sbanks = [2 * (i % 2), 2 * (i % 2) + 1]
            pt = PTs[i % 3]
            ptk = ("PT", i % 3)
            hbi = h * 3 + br
            for tl in tiles:
                sb = sbanks[tl["sbank"]]
                n = tl["n"]
                so = tl["soff"]
                ks = PADK + tl["kstart"]
                kap = kT[:, c, ks: ks + 127 * tl["kstep"] + 1: tl["kstep"]]
                qap = qm[e2][:, c, tl["qstart"]: tl["qstart"] + (n - 1) * tl["qstep"] + 1: tl["qstep"]]
                P.add("pe", lambda e, sb=sb, so=so, n=n, kap=kap, qap=qap: e.matmul(
                    PF[sb][:, so:so + n], lhsT=kap, rhs=qap, start=True, stop=True, skip_group_check=True),
                    reads=["kT_all", ("qm", e2, c)], writes=[psk(sb)])
            for bi in range(2):
                P.add("act", lambda e, bi=bi, sbanks=sbanks, pt=pt: e.activation(
                    out=pt[:, bi * 512:(bi + 1) * 512], in_=PF[sbanks[bi]], func=AF.Exp),
                    reads=[psk(sbanks[bi])], writes=[(ptk, bi)])
            ebv = biasT[:, hbi:hbi + 1, :].broadcast_to([128, 2, 256])
            for bi, eng_ in ((0, "pool"), (1, "dve")):
                ptv = pt[:, bi * 512:(bi + 1) * 512].rearrange("p (a b) -> p a b", b=256)
                P.add(eng_, lambda e, ptv=ptv, ebv=ebv: e.tensor_tensor(out=ptv, in0=ptv, in1=ebv, op=ALU.mult),
                      reads=[(ptk, bi), ("biasT", hbi // 6 * 6)], writes=[(ptk, bi)])

        def emit_pv(i):
            c, br, seg, e2, it_, tiles, Va, vbank, vb = seg_ctx(i)
            ntile = len(tiles)
            acc = accs[c % 2]
            akey = ("acc", c % 2, e2)
            obank = 4 + (i % 2)
            pt = PTs[i % 3]
            ptk = ("PT", i % 3)
            for ti, tl in enumerate(tiles):
                n = tl["n"]
                po = tl["sbank"] * 512 + tl["soff"]
                P.add("pe", lambda e, ti=ti, tl=tl, n=n, po=po: e.matmul(
                    PF[obank][:, tl["qlo"]:tl["qlo"] + n], lhsT=Va[:, tl["slot"], e2, :], rhs=pt[:, po:po + n],
                    start=(ti == 0), stop=(ti == ntile - 1), skip_group_check=True),
                    reads=[("Vaug", vb), ("Vones", vb), (ptk, tl["sbank"])], writes=[psk(obank)])
            if br == 0:
                dst = acc[:, e2, seg * 512:(seg + 1) * 512]
                P.add("dve", lambda e: e.tensor_copy(out=dst, in_=PF[obank]), reads=[psk(obank)], writes=[akey])
            elif br == 1:
                dst = acc[:, e2, seg:NOWN:4]
                P.add("dve", lambda e: e.tensor_tensor(out=dst, in0=PF[obank], in1=dst, op=ALU.add),
                      reads=[psk(obank), akey], writes=[akey])
            else:
                dst = acc[:, e2, :].rearrange("p (i r) -> p r i", r=16)[:, 4 * seg:4 * seg + 4, :]
                srco = PF[obank].rearrange("p (r i) -> p r i", i=128)
                P.add("dve", lambda e: e.tensor_tensor(out=dst, in0=srco, in1=dst, op=ALU.add),
                      reads=[psk(obank), akey], writes=[akey])
            if (br, seg, e2) == (2, 3, 1):
                for ee in range(2):
                    for blk in range(4):
                        pending_norm.append((c, ee, blk))
            elif i % 2 == 1 and pending_norm:
                emit_norm(*pending_norm.pop(0))

        def emit_norm(c, ee, blk):
            acc = accs[c % 2]
            akey = ("acc", c % 2, ee)
            cols = slice(blk * 512, (blk + 1) * 512)
            P.add("act", lambda e: e.activation(out=rden[0:64, cols], in_=acc[64:128, ee, cols], func=AF.Ln),
                  reads=[akey], writes=[("rden", blk)])
            P.add("act", lambda e: e.activation(out=rden[0:64, cols], in_=rden[0:64, cols], func=AF.Exp, scale=-1.0),
                  reads=[("rden", blk)], writes=[("rden", blk)])
            dst = catA[ee * 64:(ee + 1) * 64, c, cols]
            P.add("pool", lambda e: e.tensor_tensor(out=dst, in0=acc[0:64, ee, cols], in1=rden[0:64, cols], op=ALU.mult),
                  reads=[akey, ("rden", blk)], writes=[("catA", c, ee, blk)])

        woutv = wos_d.rearrange("(c p) n -> p c n", p=128)
        for i in range(NS + 2):
            if i in (4, 8, 12):
                bias_exp(6 * (i // 4))
                q_mask(i // 4)
            if i < NS:
                emit_qk(i)
            if i >= 2:
                emit_pv(i - 2)
            if i == NS - 6:
                for c2 in range(0, 8, 2):
                    P.add("sp", lambda e, c2=c2: e.dma_start(out=w_out[:, c2:c2 + 2, :], in_=woutv[:, c2:c2 + 2, :]),
                          reads=[("acc", 0, 0), ("acc", 0, 1), "wos"], writes=[("acc", 0, 0), ("acc", 0, 1), ("w_out", c2)], dma=True)
        while pending_norm:
            emit_norm(*pending_norm.pop(0))

        P.barrier()
        if debug:
            dbg_ops.append(P.add("sp", lambda e: e.dma_start(out=dbg["catA"], in_=catA), dma=True))
        L = Lay(ar, BASE + 16 * KB)
        w1 = L.get(BF16, [8, 4096])
        w2 = L.get(BF16, [32, D])
        W2_END = L.off
        gB = L.get(F32, [D])
        sq4 = L.get(BF16, [D])
        assert L.off <= ACC0_OFF, (L.off, ACC0_OFF)
        L = Lay(ar, BASE + 16 * KB + 64 * KB)
        xt4 = [L.get(F32, [D]) for _ in range(2)]
        tmp4 = [L.get(F32, [D]) for _ in range(2)]
        L = Lay(ar, ACC1_OFF)
        h2T = [L.get(BF16, [8, 512])] * 2
        gC = L.get(F32, [D])
        hb5 = [L.get(BF16, [D])] * 2
        P5A_OFF = L.off
        assert L.off <= ARENA, L.off
        hb4 = [ar.view(BASE + 16 * KB + 64 * KB + 16 * KB + i * 2 * KB, BF16, [D]) for i in range(4)]
        pend_tr = []

        def p4_transposes(t):
            for kc in range(8):
                P.add("pe", lambda e, kc=kc: e.transpose(
                    out=PB[7][:, kc * 128:(kc + 1) * 128], in_=hb4[t][:, kc * 128:(kc + 1) * 128], identity=ident),
                    reads=[("hb4", t), "ident"], writes=[psk(7)])
            srcT = PB[7].rearrange("p (a b) -> p a b", b=128)
            dstT = h2T[0][:, :, t * 128:(t + 1) * 128]
            P.add("act", lambda e: e.activation(out=dstT, in_=srcT, func=AF.Copy), reads=[psk(7)], writes=[("h2T", 0, t)])

        P.add("sp", lambda e: e.dma_start(out=gB, in_=gbc_d[1]), writes=["gB"], dma=True)
        P.add("sp", lambda e: e.dma_start(out=gC, in_=gbc_d[2]), writes=["gC"], dma=True)
        w1f = w1_d.rearrange("(c p) n -> p c n", p=128)
        w2v = w2s_d.rearrange("(f p) n -> p f n", p=128)
        wst = [ar.view(BASE + 16 * KB + 64 * KB + 24 * KB + i * 16 * KB, F32, [8, 512]) for i in range(2)]
        wloads = []
        for pc in range(8):
            wloads.append(lambda pc=pc: P.add("sp", lambda e: e.dma_start(
                out=wst[pc % 2], in_=w1f[:, :, pc * 512:(pc + 1) * 512]), writes=[("wst", pc % 2)], dma=True))

        def w1_cast(pc):
            P.add("dve", lambda e: e.tensor_copy(out=w1[:, :, pc * 512:(pc + 1) * 512], in_=wst[pc % 2]),
                  reads=[("wst", pc % 2)], writes=[("w1", pc)])
        for f in range(0, 32, 4):
            wloads.append(lambda f=f: P.add("sp", lambda e: e.dma_start(out=w2[:, f:f + 4, :], in_=w2v[:, f:f + 4, :]),
                                            reads=[("w2s", f // 4)], writes=[("w2", f)], dma=True, nobar=True))
        x1v = x1s_d.rearrange("(t p) d -> t p d", p=128)
        def load_x4(t):
            P.add("sp", lambda e: e.dma_start(out=xt4[t % 2], in_=xv[t]), writes=[("xt4", t % 2)], dma=True)

        load_x4(0)
        for t in range(16):
            if t + 1 < 16:
                load_x4(t + 1)
            if t % 2 == 0:
                if t >= 2:
                    w1_cast(t // 2 - 1)
                wloads[t // 2]()
            for hh in range(2):
                bank = 2 * (t % 2) + hh
                for kc in range(8):
                    src = (catA[:, kc, t * 128:(t + 1) * 128] if kc < 4 else catS[:, kc - 4, t * 128:(t + 1) * 128])
                    P.add("pe", lambda e, src=src, kc=kc, hh=hh, bank=bank: e.matmul(
                        PF[bank], lhsT=src, rhs=w_out[:, kc, hh * 512:(hh + 1) * 512], start=(kc == 0), stop=(kc == 7)),
                        reads=["catA_all", "catS_all", ("w_out", kc // 2 * 2)], writes=[psk(bank)])
            if pend_tr and pend_tr[0] <= t - 2:
                p4_transposes(pend_tr.pop(0))
            b0 = 2 * (t % 2)
            ps2 = ps_all[:, b0 * 512:(b0 + 2) * 512]
            P.add("act", lambda e, t=t, ps2=ps2: e.activation(out=sq4, in_=ps2, func=AF.Square, accum_out=stat[:, 32 + t:33 + t]),
                  reads=[psk(b0), psk(b0 + 1), "stat"], writes=["sq4", ("sst", t)])
            P.add("act", lambda e, t=t: e.activation(out=stat[:, 48 + t:49 + t], in_=stat[:, 32 + t:33 + t], func=AF.Sqrt,
                                                     bias=epsr[:, 0:1], scale=1.0 / D),
                  reads=[("sst", t), "epsr"], writes=[("ssd", t)])
            P.add("dve", lambda e, t=t: e.reciprocal(out=stat[:, 48 + t:49 + t], in_=stat[:, 48 + t:49 + t]),
                  reads=[("ssd", t)], writes=[("rs4", t)])
            P.add("dve", lambda e, t=t, ps2=ps2: e.scalar_tensor_tensor(
                out=tmp4[t % 2], in0=ps2, scalar=stat[:, 48 + t:49 + t], in1=gB, op0=ALU.mult, op1=ALU.mult),
                reads=[psk(b0), psk(b0 + 1), ("rs4", t), "gB"], writes=[("tmp4", t % 2)])
            P.add("pool", lambda e, t=t: e.tensor_tensor(out=tmp4[t % 2], in0=tmp4[t % 2], in1=xt4[t % 2], op=ALU.add),
                  reads=[("tmp4", t % 2), ("xt4", t % 2)], writes=[("tmp4", t % 2)])
            P.add("sp", lambda e, t=t: e.dma_start(out=x1v[t], in_=tmp4[t % 2]), reads=[("tmp4", t % 2)],
                  writes=[("x1s", t)], dma=True)
            if t < 4:
                P.add("act", lambda e, t=t: e.activation(out=hb4[t], in_=tmp4[t % 2], func=AF.Square, accum_out=stat[:, t:t + 1]),
                      reads=[("tmp4", t % 2), "stat"], writes=[("hb4", t), ("s5", t)])
                P.add("act", lambda e, t=t: e.activation(out=stat[:, 8 + t:9 + t], in_=stat[:, t:t + 1], func=AF.Sqrt,
                                                         bias=epsr[:, 0:1], scale=1.0 / D),
                      reads=[("s5", t), "epsr"], writes=[("sd5", t)])
                P.add("dve", lambda e, t=t: e.reciprocal(out=stat[:, 8 + t:9 + t], in_=stat[:, 8 + t:9 + t]),
                      reads=[("sd5", t)], writes=[("r5", t)])
                P.add("dve", lambda e, t=t: e.scalar_tensor_tensor(
                    out=hb4[t], in0=tmp4[t % 2], scalar=stat[:, 8 + t:9 + t], in1=gC, op0=ALU.mult, op1=ALU.mult),
                    reads=[("tmp4", t % 2), ("r5", t), "gC"], writes=[("hb4", t)])
                pend_tr.append(t)

        w1_cast(7)
        P.barrier()
        if debug:
            x1dv = dbg["x1"]
            for t in range(16):
                dbg_ops.append(P.add("sp", lambda e, t=t: e.dma_start(out=x1dv[:, t, :], in_=x1v[t]), dma=True))
        L = Lay(ar, 2 * KB)
        aT = L.get(BF16, [32, 512])
        assert L.off <= BASE + 16 * KB, L.off
        L = Lay(ar, W2_END)
        gD = L.get(F32, [D])
        xa = [L.get(F32, [D]) for _ in range(2)]
        xb = [L.get(F32, [D]) for _ in range(2)]
        st5 = L.get(F32, [160])
        assert L.off <= ACC1_OFF, (L.off, ACC1_OFF)
        L = Lay(ar, P5A_OFF)
        RR_OFF = L.off
        rr_ = [L.get(F32, [512]) for _ in range(2)]
        ost = [L.get(F32, [D])] * 2
        sq5 = ar.view(RR_OFF, BF16, [D])
        assert L.off <= ARENA, L.off
        P.add("sp", lambda e: e.dma_start(out=gD, in_=gbc_d[3]), writes=["gD"], dma=True)
        P.add("dve", lambda e: e.memset(st5, 0.0), writes=["st5"])
        outv = out_d.rearrange("(t p) d -> t p d", p=128)
        final_ops = []
        cnt5 = dict(fb=0, rq=0)

        def prenorm_chain(t):
            P.add("act", lambda e: e.activation(out=hb5[0], in_=xa[t % 2], func=AF.Square, accum_out=st5[:, t:t + 1]),
                  reads=[("xa", t % 2), "st5"], writes=[("hb5", 0), ("s5", t)])
            P.add("act", lambda e: e.activation(out=st5[:, 16 + t:17 + t], in_=st5[:, t:t + 1], func=AF.Sqrt,
                                                bias=epsr[:, 0:1], scale=1.0 / D),
                  reads=[("s5", t), "epsr"], writes=[("sd5", t)])
            P.add("dve", lambda e: e.reciprocal(out=st5[:, 16 + t:17 + t], in_=st5[:, 16 + t:17 + t]),
                  reads=[("sd5", t)], writes=[("r5", t)])
            P.add("dve", lambda e: e.scalar_tensor_tensor(
                out=hb5[0], in0=xa[t % 2], scalar=st5[:, 16 + t:17 + t], in1=gC, op0=ALU.mult, op1=ALU.mult),
                reads=[("xa", t % 2), ("r5", t), "gC"], writes=[("hb5", 0)])

        def prenorm_transpose(t):
            j = t % 4
            for kc in range(8):
                P.add("pe", lambda e, kc=kc: e.transpose(
                    out=PB[7][:, kc * 128:(kc + 1) * 128], in_=hb5[0][:, kc * 128:(kc + 1) * 128], identity=ident),
                    reads=[("hb5", 0), "ident"], writes=[psk(7)])
            src = PB[7].rearrange("p (a b) -> p a b", b=128)
            dst = h2T[0][:, :, j * 128:(j + 1) * 128]
            P.add("act", lambda e: e.activation(out=dst, in_=src, func=AF.Copy), reads=[psk(7)], writes=[("h2T", 0, j)])

        def load_xa(t):
            P.add("sp", lambda e: e.dma_start(out=xa[t % 2], in_=x1v[t]), reads=[("x1s", t)], writes=[("xa", t % 2)], dma=True)

        def ff1_group(G):
            h2 = h2T[0]
            h2reads = [("h2T", 0, j) for j in range(4)]
            for F_ in range(32):
                if G == 0 and F_ % 4 == 0:
                    wloads[8 + F_ // 4]()
                bank = cnt5["fb"] % 3
                cnt5["fb"] += 1
                for kc in range(8):
                    P.add("pe", lambda e, F_=F_, kc=kc, bank=bank: e.matmul(
                        PF[bank], lhsT=w1[:, kc, F_ * 128:(F_ + 1) * 128], rhs=h2[:, kc, :], start=(kc == 0), stop=(kc == 7)),
                        reads=[("w1", F_ // 4)] + h2reads, writes=[psk(bank)])
                r_ = rr_[cnt5["rq"] % 2]
                rk = ("rr", cnt5["rq"] % 2)
                cnt5["rq"] += 1
                P.add("act", lambda e, r_=r_, bank=bank: e.activation(out=r_, in_=PF[bank], func=AF.Relu),
                      reads=[psk(bank)], writes=[rk])
                P.add("dve", lambda e, r_=r_, F_=F_: e.tensor_tensor(out=aT[:, F_, :], in0=r_, in1=r_, op=ALU.mult),
                      reads=[rk], writes=[("aT", F_)])

        def ff2_mm(t):
            j = t % 4
            for hh in range(2):
                bank = 3 + 2 * (t % 2) + hh
                for F_ in range(32):
                    P.add("pe", lambda e, F_=F_, hh=hh, bank=bank: e.matmul(
                        PF[bank], lhsT=aT[:, F_, j * 128:(j + 1) * 128], rhs=w2[:, F_, hh * 512:(hh + 1) * 512],
                        start=(F_ == 0), stop=(F_ == 31)), reads=[("aT", F_), ("w2", F_ // 4 * 4)], writes=[psk(bank)])

        def ff2_evac(t):
            b0 = 3 + 2 * (t % 2)
            ps2 = ps_all[:, b0 * 512:(b0 + 2) * 512]
            P.add("act", lambda e: e.activation(out=sq5, in_=ps2, func=AF.Square, accum_out=st5[:, 96 + t:97 + t]),
                  reads=[psk(b0), psk(b0 + 1), "st5"], writes=[("rr", 0), ("sft", t)])
            P.add("act", lambda e: e.activation(out=st5[:, 112 + t:113 + t], in_=st5[:, 96 + t:97 + t], func=AF.Sqrt,
                                                bias=epsr[:, 0:1], scale=1.0 / D),
                  reads=[("sft", t), "epsr"], writes=[("sfd", t)])
            P.add("dve", lambda e: e.reciprocal(out=st5[:, 112 + t:113 + t], in_=st5[:, 112 + t:113 + t]),
                  reads=[("sfd", t)], writes=[("rf", t)])
            P.add("dve", lambda e: e.scalar_tensor_tensor(
                out=ost[0], in0=ps2, scalar=st5[:, 112 + t:113 + t], in1=gD, op0=ALU.mult, op1=ALU.mult),
                reads=[psk(b0), psk(b0 + 1), ("rf", t), "gD"], writes=[("ost", 0)])
            P.add("pool", lambda e: e.tensor_tensor(out=ost[0], in0=ost[0], in1=xb[t % 2], op=ALU.add),
                  reads=[("ost", 0), ("xb", t % 2)], writes=[("ost", 0)])
            o = P.add("sp", lambda e: e.dma_start(out=outv[t], in_=ost[0]), reads=[("ost", 0)], writes=[("ost", 0)], dma=True)
            final_ops.append(o)

        def load_xb(t):
            P.add("sp", lambda e: e.dma_start(out=xb[t % 2], in_=x1v[t]), reads=[("x1s", t)], writes=[("xb", t % 2)], dma=True)

        for G in range(4):
            ff1_group(G)
            for j in range(4):
                t = 4 * G + j
                load_xb(t)
                if G < 3:
                    load_xa(t + 4)
                    prenorm_chain(t + 4)
                ff2_mm(t)
                if G < 3:
                    prenorm_transpose(t + 4)
                ff2_evac(t)

        if debug:
            P.barrier()
            for nm, buf in (("aT", aT), ("h2T", h2T[0]), ("w2", w2), ("w1", w1), ("w1s", w1s_d), ("w2s", w2s_d), ("wos", wos_d)):
                dbg_ops.append(P.add("sp", lambda e, nm=nm, buf=buf: e.dma_start(out=dbg[nm], in_=buf), dma=True))
        P.finalize(st)
        with nc.Block() as block:
            P.emit_all(block, final_ops + dbg_ops)
    return nc


def _t5_bucket(rel):
    half, max_exact = 16, 8
    ret = np.where(rel > 0, half, 0)
    n = np.abs(rel)
    nf = np.maximum(n, 1).astype(np.float32)
    large = max_exact + (np.log(nf / np.float32(max_exact)) / np.float32(math.log(1024 / max_exact))
                         * np.float32(half - max_exact)).astype(np.int32)
    large = np.minimum(large, half - 1)
    return ret + np.where(n < max_exact, n, large)


def _bias_tiles(rel_bias, sign):
    kl = np.arange(128)[:, None]
    ql = np.arange(256)[None, :]
    j = kl - ql + 64
    valid = np.abs(j) <= 64
    out = np.empty((128, 24, 256), np.float32)
    for br, dil in enumerate((1, 4, 16)):
        bidx = _t5_bucket((sign * dil * j).astype(np.int32))
        for h in range(8):
            tile = rel_bias[bidx, h]
            out[:, h * 3 + br, :] = np.where(valid, tile, np.float32(MASKV))
    return out


_NC_CACHE = {}


def kernel(x, g_pre_mix, w_in, sgu_ln_g, sgu_ln_b, sgu_w, sgu_b, w_out, g_post_mix,
           g_pre_ffn, w_ff1, w_ff2, g_post_ffn, rel_bias, _debug=None):
    f32 = np.float32
    x = np.asarray(x, f32)
    B = x.shape[0]
    key = "dbg" if _debug is not None else "nc"
    if key not in _NC_CACHE:
        _NC_CACHE[key] = build_nc(debug=_debug is not None)
    nc = _NC_CACHE[key]
    gbc = np.stack([np.broadcast_to(np.asarray(v, f32)[0][None, :], (128, D))
                    for v in (g_pre_mix, g_post_mix, g_pre_ffn, g_post_ffn)]).astype(f32)
    lng = np.asarray(sgu_ln_g, f32)[0]
    lnb = np.asarray(sgu_ln_b, f32)[0]
    Ws = np.asarray(sgu_w, f32)[0]
    bs = np.asarray(sgu_b, f32)[0]
    rb = np.asarray(rel_bias, f32)
    ident = np.eye(128, dtype=f32)
    pidx_g = (np.arange(4)[None, :] * 2 + (np.arange(128)[:, None] // 64))
    pidx_c = np.arange(128)[:, None] % 64
    feat = pidx_g * 64 + pidx_c
    lngF = np.broadcast_to(lng[feat][:, :, None], (128, 4, 128))
    lnbF = np.broadcast_to(lnb[feat][:, :, None], (128, 4, 128))
    in_maps = []
    for core in range(8):
        b, half = core // 2, core % 2
        if half == 0:
            idx = np.arange(NLOC)
            Wl, bl, sign = Ws, bs, 1
        else:
            idx = S - 1 - np.arange(NLOC)
            Wl, bl, sign = Ws[:, ::-1, ::-1], bs[:, ::-1], -1
        xl = np.ascontiguousarray(x[b][idx])
        wsT = np.ascontiguousarray(np.transpose(Wl, (2, 0, 1)))
        bsF = bl[pidx_g]
        sgf = np.stack([lngF.reshape(128, 512), lnbF.reshape(128, 512), bsF.reshape(128, 512)]).astype(f32)
        in_maps.append({
            "x": xl, "w_in": np.asarray(w_in, f32)[0], "w_out": np.asarray(w_out, f32)[0],
            "w_ff1": np.asarray(w_ff1, f32)[0], "w_ff2": np.asarray(w_ff2, f32)[0],
            "gbc": gbc, "wsT": wsT, "sgf": np.ascontiguousarray(sgf),
            "biasT": _bias_tiles(rb, sign), "ident": ident,
        })
    res = run_bass_kernel_spmd(nc, in_maps, core_ids=list(range(8)))
    if _debug is not None:
        _debug.extend(res.results)
    out = np.empty((B, S, D), f32)
    for core in range(8):
        b, half = core // 2, core % 2
        o = res.results[core]["out"]
        if half == 0:
            out[b, 0:NOWN] = o
        else:
            out[b, S - 1 - np.arange(NOWN)] = o
    return out
```

```python
import math
from contextlib import ExitStack

import numpy as np
import concourse.bass as bass
import concourse.mybir as mybir
from concourse.bass_utils import run_bass_kernel_spmd

F32 = mybir.dt.float32
BF16 = mybir.dt.bfloat16
AF = mybir.ActivationFunctionType
ALU = mybir.AluOpType
KB = 1024

D = 1024
S = 4096
NOWN = 2048
NLOC = 3072
PADK = 1024
RMS_EPS = 1e-6
LN_EPS = 1e-5
MASKV = -30000.0
STRICT_SAME_ENGINE = True


class Prog:
    def __init__(self, nc, n_dma_sems=8):
        self.nc = nc
        self.ops = []
        self.last_writer = {}
        self.readers = {}
        self.n_dma_sems = n_dma_sems
        self.bar_deps = set()
        self.since_bar_dma = []
        self.last_on = {}

    def add(self, eng, emit, reads=(), writes=(), dma=False, nobar=False):
        oid = len(self.ops)
        deps = set(self.bar_deps)
        for r in reads:
            if r in self.last_writer:
                deps.add(self.last_writer[r])
        for w in writes:
            if w in self.last_writer:
                deps.add(self.last_writer[w])
            for rd in self.readers.get(w, ()):
                deps.add(rd)
        for r in reads:
            self.readers.setdefault(r, []).append(oid)
        for w in writes:
            self.last_writer[w] = oid
            self.readers[w] = []
        deps.discard(oid)
        self.ops.append(dict(id=oid, eng=eng, emit=emit, deps=deps, dma=dma,
                             reads=tuple(reads), writes=tuple(writes)))
        if dma:
            if not nobar:
                self.since_bar_dma.append(oid)
        else:
            self.last_on[eng] = oid
        return oid

    def barrier(self):
        deps = set(self.since_bar_dma)
        for e, oid in self.last_on.items():
            deps.add(oid)
        self.bar_deps = deps
        self.since_bar_dma = []

    def finalize(self, stack):
        nc = self.nc
        ops = self.ops
        for op in ops:
            keep = set()
            for d in op["deps"]:
                dop = ops[d]
                if dop["dma"] or op["dma"]:
                    keep.add(d)
                    continue
                if dop["eng"] == op["eng"]:
                    if op["eng"] == "pe":
                        continue
                    if STRICT_SAME_ENGINE or (set(dop["writes"]) & set(op["reads"])):
                        keep.add(d)
                    continue
                keep.add(d)
            op["deps"] = keep
        signaled = set()
        for op in ops:
            signaled |= op["deps"]
        self.esem = {e: stack.enter_context(nc.semaphore("s_" + e)) for e in ("pe", "act", "dve", "pool")}
        self.dsem = {}
        for q in ("sp", "act", "pool"):
            self.dsem[q] = [stack.enter_context(nc.semaphore(f"d_{q}{i}")) for i in range(self.n_dma_sems)]
        cnt = {e: 0 for e in self.esem}
        dcnt = {q: [0] * self.n_dma_sems for q in self.dsem}
        dnext = {q: 0 for q in self.dsem}
        for op in ops:
            if op["dma"]:
                q = op["eng"]
                i = dnext[q] % self.n_dma_sems
                dnext[q] += 1
                op["prev_tok"] = (self.dsem[q][i], dcnt[q][i]) if dcnt[q][i] > 0 else None
                dcnt[q][i] += 16
                op["tok"] = (self.dsem[q][i], dcnt[q][i])
                op["sig"] = True
            elif op["id"] in signaled:
                cnt[op["eng"]] += 1
                op["tok"] = (self.esem[op["eng"]], cnt[op["eng"]])
                op["sig"] = True
            else:
                op["sig"] = False

    def run_engine(self, ename, eng):
        ops = self.ops
        waited = {}
        for op in ops:
            if op["eng"] != ename:
                continue
            need = {}
            toks = [ops[d]["tok"] for d in op["deps"]]
            if op["dma"] and op["prev_tok"] is not None:
                toks.append(op["prev_tok"])
            for (s, v) in toks:
                k = id(s)
                if k not in need or need[k][1] < v:
                    need[k] = (s, v)
            for k, (s, v) in need.items():
                if waited.get(k, 0) >= v:
                    continue
                eng.wait_ge(s, v)
                waited[k] = v
            ins = op["emit"](eng)
            if op["sig"]:
                s, v = op["tok"]
                ins.then_inc(s, 16 if op["dma"] else 1)

    def emit_all(self, block, final_ops):
        P = self

        def mk(ename):
            def f(eng):
                P.run_engine(ename, eng)
                if ename == "sp":
                    for oid in final_ops:
                        s, v = P.ops[oid]["tok"]
                        eng.wait_ge(s, v)
            return f
        block.tensor(mk("pe"))
        block.scalar(mk("act"))
        block.vector(mk("dve"))
        block.gpsimd(mk("pool"))
        block.sync(mk("sp"))


class Arena:
    def __init__(self, nc, stack, nbytes):
        self.nbytes = nbytes
        self.t = stack.enter_context(nc.sbuf_tensor("arena", [128, nbytes // 2], BF16))

    def view(self, off, dtype, shape):
        n = 1
        for s_ in shape:
            n *= s_
        esz = 4 if dtype == F32 else 2
        assert off % 4 == 0 and off + n * esz <= self.nbytes, (off, n * esz, self.nbytes)
        a = self.t[:, off // 2: off // 2 + n * esz // 2]
        if dtype == F32:
            a = a.bitcast(F32)
        if len(shape) == 2:
            a = a.rearrange("p (a b) -> p a b", b=shape[1])
        elif len(shape) == 3:
            a = a.rearrange("p (a b c) -> p a b c", b=shape[1], c=shape[2])
        return a


class Lay:
    def __init__(self, arena, start):
        self.arena = arena
        self.off = start

    def get(self, dtype, shape):
        n = 1
        for s_ in shape:
            n *= s_
        esz = 4 if dtype == F32 else 2
        v = self.arena.view(self.off, dtype, shape)
        self.off += (n * esz + 3) // 4 * 4
        return v


def seg_tiles(br, seg):
    tiles = []
    if br in (0, 1):
        dil = 1 if br == 0 else 4
        rel = [(0, 128, 128), (0, 256, 0), (128, 384, 0), (256, 512, 0), (384, 512, 0)]
        sb = [(0, 128), (0, 256), (1, 0), (1, 256), (0, 0)]
        for m in range(5):
            qlo, qhi, bc0 = rel[m]
            if br == 0:
                k0 = 512 * seg - 64 + 128 * m
                kstart, qstart = k0, 512 * seg + qlo
            else:
                K0 = -64 + 128 * m
                kstart, qstart = seg + 4 * K0, seg + 4 * qlo
            tiles.append(dict(kstart=kstart, kstep=dil, qlo=qlo, n=qhi - qlo, bc0=bc0, slot=m,
                              boundary=(kstart < 0), qstart=qstart, qstep=dil, sbank=sb[m][0], soff=sb[m][1]))
    else:
        for rr in range(4):
            r = 4 * seg + rr
            for m in range(2):
                K0 = -64 + 128 * m
                tiles.append(dict(kstart=r + 16 * K0, kstep=16, qlo=rr * 128, n=128, bc0=128 if m == 0 else 0,
                                  slot=rr * 2 + m, boundary=(m == 0), qstart=r, qstep=16,
                                  sbank=rr // 2, soff=(rr % 2) * 256 + (128 if m == 0 else 0)))
    return tiles


def build_nc(debug=False):
    nc = bass.Bass("TRN2", target_bir_lowering=False)

    def din(name, shape):
        return nc.dram_tensor(name, list(shape), F32, kind="ExternalInput").ap()

    x_d = din("x", [NLOC, D])
    win_d = din("w_in", [D, 2560])
    wout_d = din("w_out", [D, D])
    w1_d = din("w_ff1", [D, 4096])
    w2_d = din("w_ff2", [4096, D])
    gbc_d = din("gbc", [4, 128, D])
    wsT_d = din("wsT", [128, 8, 128])
    sgf_d = din("sgf", [3, 128, 512])
    bias_d = din("biasT", [128, 24, 256])
    ident_d = din("ident", [128, 128])
    out_d = nc.dram_tensor("out", [NOWN, D], F32, kind="ExternalOutput").ap()
    x1s_d = nc.dram_tensor("x1s", [NOWN, D], F32, kind="Internal").ap()
    w1s_d = nc.dram_tensor("w1s", [D, 4096], BF16, kind="Internal").ap()
    w2s_d = nc.dram_tensor("w2s", [4096, D], BF16, kind="Internal").ap()
    wos_d = nc.dram_tensor("wos", [D, D], BF16, kind="Internal").ap()
    dbg = {}
    if debug:
        for nm, shp, dt_ in (("qT", [128, 4, NOWN], BF16), ("kT", [128, 4, PADK + NLOC], BF16), ("vT", [128, 4, PADK + NLOC], BF16),
                             ("catS", [128, 4, NOWN], BF16), ("catA", [128, 4, NOWN], BF16), ("x1", [128, 16, D], F32),
                             ("aT", [128, 32, 512], BF16), ("h2T", [128, 8, 512], BF16), ("w2", [128, 32, D], BF16),
                             ("w1", [128, 8, 4096], BF16), ("w1s", [D, 4096], BF16), ("w2s", [4096, D], BF16), ("wos", [D, D], BF16)):
            dbg[nm] = nc.dram_tensor("dbg_" + nm, shp, dt_, kind="ExternalOutput").ap()

    st = ExitStack()
    with st:
        ARENA = 206 * KB
        ar = Arena(nc, st, ARENA)
        ps_all = st.enter_context(nc.psum_tensor("ps", [128, 8 * 512], F32))
        PF = [ps_all[:, i * 512:(i + 1) * 512] for i in range(8)]
        PB = [PF[i].bitcast(BF16) for i in range(8)]
        P = Prog(nc)
        psk = lambda i: ("ps", i)

        L0 = Lay(ar, 0)
        ident = L0.get(BF16, [128])
        ss_all = L0.get(F32, [24])
        rstd_all = L0.get(F32, [24])
        stat = L0.get(F32, [64])
        epsr = L0.get(F32, [2])
        L0.off = 2 * KB
        catS = L0.get(BF16, [4, NOWN])
        BASE = L0.off

        P.add("pool", lambda e: e.dma_start(out=ident, in_=ident_d), writes=["ident"], dma=True)
        P.add("dve", lambda e: e.memset(ss_all, 0.0), writes=["ss_all"])
        P.add("dve", lambda e: e.memset(stat, 0.0), writes=["stat"])
        P.add("dve", lambda e: e.memset(epsr[:, 0:1], RMS_EPS), writes=["epsr"])
        P.add("dve", lambda e: e.memset(epsr[:, 1:2], LN_EPS), writes=["epsr"])

        L = Lay(ar, BASE)
        w_in = L.get(BF16, [8, 2560])
        qT = L.get(BF16, [4, NOWN])
        kT = L.get(BF16, [4, PADK + NLOC])
        vT = L.get(BF16, [4, PADK + NLOC])
        Q_END = L.off
        hT = [L.get(BF16, [8, 512]) for _ in range(2)]
        xt = [L.get(F32, [D]) for _ in range(2)]
        hb = [L.get(BF16, [D]) for _ in range(2)]
        uT = [L.get(BF16, [4, 512]) for _ in range(2)]
        GV_OFF = L.off
        gv = [L.get(F32, [512]) for _ in range(4)]
        nt = [L.get(BF16, [512]) for _ in range(4)]
        wsT = L.get(BF16, [8, 128])
        Cc = L.get(F32, [4, 128])
        lngF = L.get(F32, [4, 128])
        gA = L.get(F32, [D])
        xs = L.get(F32, [D])
        tmpS = [L.get(F32, [4, 128]) for _ in range(2)]
        lnbF, bsF = tmpS
        ones_bf = L.get(BF16, [64])
        bnst = L.get(F32, [4, 6])
        mv = L.get(F32, [4, 2])
        lnsd = L.get(F32, [4])
        lnr = L.get(F32, [4])
        assert L.off <= ARENA, L.off

        winv = win_d.rearrange("(c p) n -> p c n", p=128)
        for pc in range(5):
            P.add("pool", lambda e, pc=pc: e.dma_start(out=w_in[:, :, pc * 512:(pc + 1) * 512], in_=winv[:, :, pc * 512:(pc + 1) * 512]),
                  reads=([("w_in", pc - 1)] if pc > 0 else []), writes=[("w_in", pc)], dma=True, nobar=True)
            if pc == 0:
                P.add("pool", lambda e: e.memset(kT[:, :, 0:PADK], 0.0), writes=["kpad"])
                P.add("pool", lambda e: e.memset(vT[:, :, 0:PADK], 0.0), writes=["vpad"])
                P.add("pool", lambda e: e.dma_start(out=wsT, in_=wsT_d), writes=["wsT"], dma=True)
                P.add("pool", lambda e: e.memset(ones_bf, 1.0), writes=["ones_bf"])
        P.add("sp", lambda e: e.dma_start(out=gA, in_=gbc_d[0]), writes=["gA"], dma=True)

        xv = x_d.rearrange("(t p) d -> t p d", p=128)

        def stats_load(t):
            P.add("sp", lambda e: e.dma_start(out=xs, in_=xv[t]), writes=["xs"], dma=True)

        def stats_square(t):
            P.add("act", lambda e: e.activation(out=xs, in_=xs, func=AF.Square, accum_out=ss_all[:, t:t + 1]),
                  reads=["xs", "ss_all"], writes=["xs", ("ss", t)])

        def stats_finish(g):
            sl = slice(4 * g, 4 * g + 4)
            P.add("act", lambda e: e.activation(out=rstd_all[:, sl], in_=ss_all[:, sl], func=AF.Sqrt, bias=epsr[:, 0:1], scale=1.0 / D),
                  reads=[("ss", t) for t in range(4 * g, 4 * g + 4)] + ["epsr"], writes=[("sd", g)])
            P.add("dve", lambda e: e.reciprocal(out=rstd_all[:, sl], in_=rstd_all[:, sl]), reads=[("sd", g)], writes=[("rstd", g)])

        def stats_group(g):
            for t in range(4 * g, 4 * g + 4):
                stats_load(t)
                stats_square(t)
            stats_finish(g)

        gvbig = ar.view(GV_OFF, F32, [D])
        pre0 = [(xt[0], ("xt", 0)), (xt[1], ("xt", 1)), (xs, "xs"), (gvbig, "gvbig")]
        for t in range(4):
            q_ = "sp" if t % 2 == 0 else "act"
            P.add(q_, lambda e, t=t: e.dma_start(out=pre0[t][0], in_=xv[t]), writes=[pre0[t][1]], dma=True)
        for t in range(4):
            P.add("act", lambda e, t=t: e.activation(out=hb[t % 2], in_=pre0[t][0], func=AF.Square, accum_out=ss_all[:, t:t + 1]),
                  reads=[pre0[t][1], "ss_all"], writes=[("hb", t % 2), ("ss", t)])
        stats_finish(0)
        for i, tdst in enumerate((lngF, lnbF, bsF)):
            P.add("sp", lambda e, i=i, tdst=tdst: e.dma_start(
                out=tdst, in_=sgf_d[i].rearrange("p (a b) -> p a b", b=128)), writes=[("sgf", i)], dma=True)

        pf5v = PF[5].rearrange("p (a b) -> p a b", b=128)
        def sgu_const():
            for a in range(4):
                for e2 in range(2):
                    g_ = 2 * a + e2
                    P.add("pe", lambda e, a=a, e2=e2, g_=g_: e.matmul(
                        PF[5][e2 * 64:(e2 + 1) * 64, a * 128:(a + 1) * 128], lhsT=ones_bf, rhs=wsT[:, g_, :],
                        start=True, stop=True), reads=["ones_bf", "wsT"], writes=[psk(5)])
            P.add("dve", lambda e: e.tensor_tensor(out=Cc, in0=pf5v, in1=lnbF, op=ALU.mult),
                  reads=[psk(5), ("sgf", 1)], writes=["Cc0"])
            P.add("dve", lambda e: e.tensor_tensor(out=Cc, in0=Cc, in1=bsF, op=ALU.add),
                  reads=["Cc0", ("sgf", 2)], writes=["Cc"])

        casts = [lambda: P.add("pool", lambda e: e.dma_start(out=wos_d, in_=wout_d), writes=["wos"], dma=True, nobar=True)]
        for pc in range(8):
            casts.append(lambda pc=pc: P.add("pool", lambda e: e.dma_start(
                out=w2s_d[pc * 512:(pc + 1) * 512, :], in_=w2_d[pc * 512:(pc + 1) * 512, :]), writes=[("w2s", pc)], dma=True, nobar=True))
        cnt2 = dict(pb=0)

        def prep_tile(t, pre=None):
            if pre is None:
                P.add("sp", lambda e: e.dma_start(out=xt[t % 2], in_=xv[t]), writes=[("xt", t % 2)], dma=True)
                src, skey = xt[t % 2], ("xt", t % 2)
            else:
                src, skey = pre
            P.add("dve", lambda e: e.scalar_tensor_tensor(
                out=hb[t % 2], in0=src, scalar=rstd_all[:, t:t + 1], in1=gA, op0=ALU.mult, op1=ALU.mult),
                reads=[skey, ("rstd", t // 4), "gA"], writes=[("hb", t % 2)])

        def transpose_tile(t):
            g, j = t // 4, t % 4
            bank = 6 + (t % 2)
            for kc in range(8):
                P.add("pe", lambda e, kc=kc: e.transpose(
                    out=PB[bank][:, kc * 128:(kc + 1) * 128], in_=hb[t % 2][:, kc * 128:(kc + 1) * 128],
                    identity=ident), reads=[("hb", t % 2), "ident"], writes=[psk(bank)])
            src = PB[bank].rearrange("p (a b) -> p a b", b=128)
            dst = hT[g % 2][:, :, j * 128:(j + 1) * 128]
            if t % 2 == 0:
                P.add("act", lambda e: e.activation(out=dst, in_=src, func=AF.Copy), reads=[psk(bank)], writes=[("hT", g % 2, j)])
            else:
                P.add("dve", lambda e: e.tensor_copy(out=dst, in_=src), reads=[psk(bank)], writes=[("hT", g % 2, j)])

        def proj_chunk(g, kind, c):
            hTg = hT[g % 2]
            hreads = [("hT", g % 2, j) for j in range(4)]
            oc = {"q": 0, "k": 4, "v": 8, "u": 12}[kind] + c
            bank = cnt2["pb"] % 3
            cnt2["pb"] += 1
            for kc in range(8):
                P.add("pe", lambda e, kc=kc: e.matmul(
                    PF[bank], lhsT=w_in[:, kc, oc * 128:(oc + 1) * 128], rhs=hTg[:, kc, :],
                    start=(kc == 0), stop=(kc == 7)), reads=hreads + [("w_in", oc // 4)], writes=[psk(bank)])
            if kind == "q":
                dst = qT[:, c, g * 512:(g + 1) * 512]
                P.add("act", lambda e: e.activation(out=dst, in_=PF[bank], func=AF.Copy, scale=0.125),
                      reads=[psk(bank)], writes=[("qT", c, g)])
            elif kind == "k":
                dst = kT[:, c, PADK + g * 512:PADK + (g + 1) * 512]
                P.add("dve", lambda e: e.tensor_copy(out=dst, in_=PF[bank]), reads=[psk(bank)], writes=[("kT", c, g)])
            elif kind == "v":
                dst = vT[:, c, PADK + g * 512:PADK + (g + 1) * 512]
                vw = [("vT", c, g)] + (["vpad"] if g == 0 else [])
                if c % 2 == 0:
                    P.add("act", lambda e: e.activation(out=dst, in_=PF[bank], func=AF.Copy), reads=[psk(bank)], writes=vw)
                else:
                    P.add("dve", lambda e: e.tensor_copy(out=dst, in_=PF[bank]), reads=[psk(bank)], writes=vw)
            else:
                dst = uT[g % 2][:, c, :]
                P.add("act", lambda e: e.activation(out=dst, in_=PF[bank], func=AF.Gelu_apprx_tanh),
                      reads=[psk(bank)], writes=[("uT", g % 2, c)])

        def zv_tile(g, j):
            hTg = hT[g % 2]
            bank = 3 + (j % 2)
            for kc in range(8):
                P.add("pe", lambda e, kc=kc: e.matmul(
                    PF[bank], lhsT=hTg[:, kc, j * 128:(j + 1) * 128], rhs=w_in[:, kc, 2048:2560],
                    start=(kc == 0), stop=(kc == 7)), reads=[("hT", g % 2, j), ("w_in", 4)], writes=[psk(bank)])
            P.add("act", lambda e: e.activation(out=gv[j], in_=PF[bank], func=AF.Gelu_apprx_tanh),
                  reads=[psk(bank)], writes=[("gv", j)] + (["gvbig"] if j < 2 else []))
            P.add("dve", lambda e: e.bn_stats(out=bnst[:, j, :], in_=gv[j]), reads=[("gv", j)], writes=[("bnst", j)])
            P.add("dve", lambda e: e.bn_aggr(out=mv[:, j, :], in_=bnst[:, j, :]), reads=[("bnst", j)], writes=[("mv", j)])

        def ln_group(g):
            P.add("act", lambda e: e.activation(out=lnsd, in_=mv[:, :, 1], func=AF.Sqrt, bias=epsr[:, 1:2], scale=1.0),
                  reads=[("mv", j) for j in range(4)] + ["epsr"], writes=["lnsd"])
            P.add("dve", lambda e: e.reciprocal(out=lnr, in_=lnsd), reads=["lnsd"], writes=["lnr"])
            for j in range(4):
                P.add("dve", lambda e, j=j: e.tensor_scalar(out=nt[j], in0=gv[j], scalar1=mv[:, j, 0:1], scalar2=lnr[:, j:j + 1],
                                                            op0=ALU.subtract, op1=ALU.mult),
                      reads=[("gv", j), ("mv", j), "lnr"], writes=[("nt", j)])

        def sgu_tile(g, j):
            t = 4 * g + j
            for a in range(4):
                for e2 in range(2):
                    g_ = 2 * a + e2
                    P.add("pe", lambda e, a=a, e2=e2, g_=g_: e.matmul(
                        PF[5][e2 * 64:(e2 + 1) * 64, a * 128:(a + 1) * 128], lhsT=nt[j][:, g_ * 64:(g_ + 1) * 64],
                        rhs=wsT[:, g_, :], start=True, stop=True), reads=[("nt", j), "wsT"], writes=[psk(5)])
            tm = tmpS[j % 2]
            P.add("dve", lambda e: e.tensor_tensor(out=tm, in0=pf5v, in1=lngF, op=ALU.mult),
                  reads=[psk(5), ("sgf", 0)], writes=[("sgf", 1 + j % 2)])
            P.add("pool", lambda e: e.tensor_tensor(out=tm, in0=tm, in1=Cc, op=ALU.add),
                  reads=[("sgf", 1 + j % 2), "Cc"], writes=[("sgf", 1 + j % 2)])
            dst = catS[:, :, t * 128:(t + 1) * 128]
            usrc = uT[g % 2][:, :, j * 128:(j + 1) * 128]
            P.add("pool", lambda e: e.tensor_tensor(out=dst, in0=tm, in1=usrc, op=ALU.mult),
                  reads=[("sgf", 1 + j % 2)] + [("uT", g % 2, c) for c in range(4)], writes=[("catS", t)])

        for j in range(4):
            prep_tile(j, pre=pre0[j])
            transpose_tile(j)
        sgu_const()
        stats_group(1)
        for g in range(6):
            plan = []
            if g < 4:
                plan += [("q", c) for c in range(4)]
            plan += [("k", c) for c in range(4)] + [("v", c) for c in range(4)]
            if g < 4:
                plan += [("u", c) for c in range(4)]
            per = len(plan) // 4
            if g + 1 < 6:
                prep_tile(4 * (g + 1))
            for ci, (kind, c) in enumerate(plan):
                proj_chunk(g, kind, c)
                if g + 2 < 6:
                    k_, ph = divmod(ci, per)
                    if per >= 4:
                        if ph == 1:
                            stats_load(4 * (g + 2) + k_)
                        elif ph == 3:
                            stats_square(4 * (g + 2) + k_)
                    else:
                        if ph == 0:
                            stats_load(4 * (g + 2) + k_)
                        else:
                            stats_square(4 * (g + 2) + k_)
                    if ci == len(plan) - 1:
                        stats_finish(g + 2)
                if (ci + 1) % per == 0 and g + 1 < 6:
                    jn = (ci + 1) // per - 1
                    transpose_tile(4 * (g + 1) + jn)
                    if jn + 1 < 4:
                        prep_tile(4 * (g + 1) + jn + 1)
                    if 1 <= g <= 4:
                        sgu_tile(g - 1, jn)
            if g < 4:
                for j in range(4):
                    zv_tile(g, j)
                ln_group(g)
                for _ in range(4):
                    if casts:
                        casts.pop(0)()
            else:
                while casts:
                    casts.pop(0)()

        P.barrier()
        dbg_ops = []
        if debug:
            for nm, buf in (("qT", qT), ("kT", kT), ("vT", vT), ("catS", catS)):
                dbg_ops.append(P.add("sp", lambda e, nm=nm, buf=buf: e.dma_start(out=dbg[nm], in_=buf), dma=True))
        L = Lay(ar, BASE)
        catA = L.get(BF16, [4, NOWN])
        biasT = L.get(BF16, [24, 256])
        Vaug = [L.get(BF16, [8, 2, 128]) for _ in range(2)]
        assert L.off <= BASE + 40 * KB
        L = Lay(ar, Q_END)
        qmA = L.get(BF16, [4, NOWN])
        PTs = [L.get(BF16, [1024]) for _ in range(3)]
        rden = L.get(F32, [NOWN])
        ACC0_OFF = L.off
        accs = [L.get(F32, [2, NOWN]) for _ in range(2)]
        ACC1_OFF = ACC0_OFF + 16 * KB
        w_out = ar.view(ACC0_OFF, BF16, [8, D])
        assert L.off <= ARENA, L.off
        qm = [qmA, qT]
        for hb_ in range(0, 24, 6):
            P.add("pool", lambda e, hb_=hb_: e.dma_start(out=biasT[:, hb_:hb_ + 6, :], in_=bias_d[:, hb_:hb_ + 6, :]),
                  writes=[("biasT", hb_)], dma=True)
        def q_mask(c):
            P.add("pool", lambda e: e.memset(qmA[64:128, c, :], 0.0), writes=[("qm", 0, c)])
            if c % 2 == 0:
                P.add("act", lambda e: e.activation(out=qmA[0:64, c, :], in_=qT[0:64, c, :], func=AF.Copy),
                      reads=[("qm", 1, c)], writes=[("qm", 0, c)])
            else:
                P.add("dve", lambda e: e.tensor_copy(out=qmA[0:64, c, :], in_=qT[0:64, c, :]),
                      reads=[("qm", 1, c)], writes=[("qm", 0, c)])
            P.add("pool", lambda e: e.memset(qT[0:64, c, :], 0.0), writes=[("qm", 1, c)])

        q_mask(0)

        bias_reads = [("biasT", i) for i in range(0, 24, 6)]
        def bias_exp(hb_):
            P.add("act", lambda e: e.activation(out=biasT[:, hb_:hb_ + 6, :], in_=biasT[:, hb_:hb_ + 6, :], func=AF.Exp),
                  reads=[("biasT", hb_)], writes=[("biasT", hb_)])

        bias_exp(0)
        steps = [(c, br, seg, e2) for c in range(4) for br in range(3) for seg in range(4) for e2 in range(2)]
        NS = len(steps)
        pending_norm = []
        vones_state = [None, None]

        def seg_ctx(i):
            c, br, seg, e2 = steps[i]
            it_ = i // 2
            return c, br, seg, e2, it_, seg_tiles(br, seg), Vaug[it_ % 2], 6 + (it_ % 2), it_ % 2

        def emit_qk(i):
            c, br, seg, e2, it_, tiles, Va, vbank, vb = seg_ctx(i)
            ntile = len(tiles)
            if e2 == 0:
                for tl in tiles:
                    cs = PADK + tl["kstart"]
                    src = vT[:, c, cs: cs + 127 * tl["kstep"] + 1: tl["kstep"]]
                    P.add("pe", lambda e, src=src, sl=tl["slot"], vbank=vbank: e.transpose(
                        out=PB[vbank][:, sl * 128:(sl + 1) * 128], in_=src, identity=ident),
                        reads=["vT_all", "ident"], writes=[psk(vbank)])
                srcv = PB[vbank][:, 0:ntile * 128].rearrange("p (s h d) -> p s h d", h=2, d=64)
                dstv = Va[:, 0:ntile, :, 0:64]
                P.add("act", lambda e, srcv=srcv, dstv=dstv: e.activation(out=dstv, in_=srcv, func=AF.Copy),
                      reads=[psk(vbank)], writes=[("Vaug", vb)])
                want = frozenset(tl["slot"] for tl in tiles if tl["boundary"])
                have = vones_state[vb]
                if have is None:
                    P.add("pool", lambda e, Va=Va: e.memset(Va[:, :, :, 64:128], 1.0), writes=[("Vones", vb)])
                    have = frozenset()
                for sl in sorted(have - want):
                    P.add("pool", lambda e, Va=Va, sl=sl: e.memset(Va[0:64, sl, :, 64:128], 1.0), writes=[("Vones", vb)])
                for sl in sorted(want - have):
                    P.add("pool", lambda e, Va=Va, sl=sl: e.memset(Va[0:64, sl, :, 64:128], 0.0), writes=[("Vones", vb)])
                vones_state[vb] = want
            h = 2 * c + e2
            sbanks = [2 * (i % 2), 2 * (i % 2) + 1]
            pt = PTs[i % 3]
            ptk = ("PT", i % 3)
            hbi = h * 3 + br
            for tl in tiles:
                sb = sbanks[tl["sbank"]]
                n = tl["n"]
                so = tl["soff"]
                ks = PADK + tl["kstart"]
                kap = kT[:, c, ks: ks + 127 * tl["kstep"] + 1: tl["kstep"]]
                qap = qm[e2][:, c, tl["qstart"]: tl["qstart"] + (n - 1) * tl["qstep"] + 1: tl["qstep"]]
                P.add("pe", lambda e, sb=sb, so=so, n=n, kap=kap, qap=qap: e.matmul(
                    PF[sb][:, so:so + n], lhsT=kap, rhs=qap, start=True, stop=True, skip_group_check=True),
                    reads=["kT_all", ("qm", e2, c)], writes=[psk(sb)])
            for bi in range(2):
                P.add("act", lambda e, bi=bi, sbanks=sbanks, pt=pt: e.activation(
                    out=pt[:, bi * 512:(bi + 1) * 512], in_=PF[sbanks[bi]], func=AF.Exp),
                    reads=[psk(sbanks[bi])], writes=[(ptk, bi)])
            ebv = biasT[:, hbi:hbi + 1, :].broadcast_to([128, 2, 256])
            for bi, eng_ in ((0, "pool"), (1, "dve")):
                ptv = pt[:, bi * 512:(bi + 1) * 512].rearrange("p (a b) -> p a b", b=256)
                P.add(eng_, lambda e, ptv=ptv, ebv=ebv: e.tensor_tensor(out=ptv, in0=ptv, in1=ebv, op=ALU.mult),
                      reads=[(ptk, bi), ("biasT", hbi // 6 * 6)], writes=[(ptk, bi)])

        def emit_pv(i):
            c, br, seg, e2, it_, tiles, Va, vbank, vb = seg_ctx(i)
            ntile = len(tiles)
            acc = accs[c % 2]
            akey = ("acc", c % 2, e2)
            obank = 4 + (i % 2)
            pt = PTs[i % 3]
            ptk = ("PT", i % 3)
            for ti, tl in enumerate(tiles):
                n = tl["n"]
                po = tl["sbank"] * 512 + tl["soff"]
                P.add("pe", lambda e, ti=ti, tl=tl, n=n, po=po: e.matmul(
                    PF[obank][:, tl["qlo"]:tl["qlo"] + n], lhsT=Va[:, tl["slot"], e2, :], rhs=pt[:, po:po + n],
                    start=(ti == 0), stop=(ti == ntile - 1), skip_group_check=True),
                    reads=[("Vaug", vb), ("Vones", vb), (ptk, tl["sbank"])], writes=[psk(obank)])
            if br == 0:
                dst = acc[:, e2, seg * 512:(seg + 1) * 512]
                P.add("dve", lambda e: e.tensor_copy(out=dst, in_=PF[obank]), reads=[psk(obank)], writes=[akey])
            elif br == 1:
                dst = acc[:, e2, seg:NOWN:4]
                P.add("dve", lambda e: e.tensor_tensor(out=dst, in0=PF[obank], in1=dst, op=ALU.add),
                      reads=[psk(obank), akey], writes=[akey])
            else:
                dst = acc[:, e2, :].rearrange("p (i r) -> p r i", r=16)[:, 4 * seg:4 * seg + 4, :]
                srco = PF[obank].rearrange("p (r i) -> p r i", i=128)
                P.add("dve", lambda e: e.tensor_tensor(out=dst, in0=srco, in1=dst, op=ALU.add),
                      reads=[psk(obank), akey], writes=[akey])
            if (br, seg, e2) == (2, 3, 1):
                for ee in range(2):
                    for blk in range(4):
                        pending_norm.append((c, ee, blk))
            elif i % 2 == 1 and pending_norm:
                emit_norm(*pending_norm.pop(0))

        def emit_norm(c, ee, blk):
            acc = accs[c % 2]
            akey = ("acc", c % 2, ee)
            cols = slice(blk * 512, (blk + 1) * 512)
            P.add("act", lambda e: e.activation(out=rden[0:64, cols], in_=acc[64:128, ee, cols], func=AF.Ln),
                  reads=[akey], writes=[("rden", blk)])
            P.add("act", lambda e: e.activation(out=rden[0:64, cols], in_=rden[0:64, cols], func=AF.Exp, scale=-1.0),
                  reads=[("rden", blk)], writes=[("rden", blk)])
            dst = catA[ee * 64:(ee + 1) * 64, c, cols]
            P.add("pool", lambda e: e.tensor_tensor(out=dst, in0=acc[0:64, ee, cols], in1=rden[0:64, cols], op=ALU.mult),
                  reads=[akey, ("rden", blk)], writes=[("catA", c, ee, blk)])

        woutv = wos_d.rearrange("(c p) n -> p c n", p=128)
        for i in range(NS + 2):
            if i in (4, 8, 12):
                bias_exp(6 * (i // 4))
                q_mask(i // 4)
            if i < NS:
                emit_qk(i)
            if i >= 2:
                emit_pv(i - 2)
            if i == NS - 6:
                for c2 in range(0, 8, 2):
                    P.add("sp", lambda e, c2=c2: e.dma_start(out=w_out[:, c2:c2 + 2, :], in_=woutv[:, c2:c2 + 2, :]),
                          reads=[("acc", 0, 0), ("acc", 0, 1), "wos"], writes=[("acc", 0, 0), ("acc", 0, 1), ("w_out", c2)], dma=True)
        while pending_norm:
            emit_norm(*pending_norm.pop(0))

        P.barrier()
        if debug:
            dbg_ops.append(P.add("sp", lambda e: e.dma_start(out=dbg["catA"], in_=catA), dma=True))
        L = Lay(ar, BASE + 16 * KB)
        w1 = L.get(BF16, [8, 4096])
        w2 = L.get(BF16, [32, D])
        W2_END = L.off
        gB = L.get(F32, [D])
        sq4 = L.get(BF16, [D])
        assert L.off <= ACC0_OFF, (L.off, ACC0_OFF)
        L = Lay(ar, BASE + 16 * KB + 64 * KB)
        xt4 = [L.get(F32, [D]) for _ in range(2)]
        tmp4 = [L.get(F32, [D]) for _ in range(2)]
        L = Lay(ar, ACC1_OFF)
        h2T = [L.get(BF16, [8, 512])] * 2
        gC = L.get(F32, [D])
        hb5 = [L.get(BF16, [D])] * 2
        P5A_OFF = L.off
        assert L.off <= ARENA, L.off
        hb4 = [ar.view(BASE + 16 * KB + 64 * KB + 16 * KB + i * 2 * KB, BF16, [D]) for i in range(4)]
        pend_tr = []

        def p4_transposes(t):
            for kc in range(8):
                P.add("pe", lambda e, kc=kc: e.transpose(
                    out=PB[7][:, kc * 128:(kc + 1) * 128], in_=hb4[t][:, kc * 128:(kc + 1) * 128], identity=ident),
                    reads=[("hb4", t), "ident"], writes=[psk(7)])
            srcT = PB[7].rearrange("p (a b) -> p a b", b=128)
            dstT = h2T[0][:, :, t * 128:(t + 1) * 128]
            P.add("act", lambda e: e.activation(out=dstT, in_=srcT, func=AF.Copy), reads=[psk(7)], writes=[("h2T", 0, t)])

        P.add("sp", lambda e: e.dma_start(out=gB, in_=gbc_d[1]), writes=["gB"], dma=True)
        P.add("sp", lambda e: e.dma_start(out=gC, in_=gbc_d[2]), writes=["gC"], dma=True)
        w1f = w1_d.rearrange("(c p) n -> p c n", p=128)
        w2v = w2s_d.rearrange("(f p) n -> p f n", p=128)
        wst = [ar.view(BASE + 16 * KB + 64 * KB + 24 * KB + i * 16 * KB, F32, [8, 512]) for i in range(2)]
        wloads = []
        for pc in range(8):
            wloads.append(lambda pc=pc: P.add("sp", lambda e: e.dma_start(
                out=wst[pc % 2], in_=w1f[:, :, pc * 512:(pc + 1) * 512]), writes=[("wst", pc % 2)], dma=True))

        def w1_cast(pc):
            P.add("dve", lambda e: e.tensor_copy(out=w1[:, :, pc * 512:(pc + 1) * 512], in_=wst[pc % 2]),
                  reads=[("wst", pc % 2)], writes=[("w1", pc)])
        for f in range(0, 32, 4):
            wloads.append(lambda f=f: P.add("sp", lambda e: e.dma_start(out=w2[:, f:f + 4, :], in_=w2v[:, f:f + 4, :]),
                                            reads=[("w2s", f // 4)], writes=[("w2", f)], dma=True, nobar=True))
        x1v = x1s_d.rearrange("(t p) d -> t p d", p=128)
        def load_x4(t):
            P.add("sp", lambda e: e.dma_start(out=xt4[t % 2], in_=xv[t]), writes=[("xt4", t % 2)], dma=True)

        load_x4(0)
        for t in range(16):
            if t + 1 < 16:
                load_x4(t + 1)
            if t % 2 == 0:
                if t >= 2:
                    w1_cast(t // 2 - 1)
                wloads[t // 2]()
            for hh in range(2):
                bank = 2 * (t % 2) + hh
                for kc in range(8):
                    src = (catA[:, kc, t * 128:(t + 1) * 128] if kc < 4 else catS[:, kc - 4, t * 128:(t + 1) * 128])
                    P.add("pe", lambda e, src=src, kc=kc, hh=hh, bank=bank: e.matmul(
                        PF[bank], lhsT=src, rhs=w_out[:, kc, hh * 512:(hh + 1) * 512], start=(kc == 0), stop=(kc == 7)),
                        reads=["catA_all", "catS_all", ("w_out", kc // 2 * 2)], writes=[psk(bank)])
            if pend_tr and pend_tr[0] <= t - 2:
                p4_transposes(pend_tr.pop(0))
            b0 = 2 * (t % 2)
            ps2 = ps_all[:, b0 * 512:(b0 + 2) * 512]
            P.add("act", lambda e, t=t, ps2=ps2: e.activation(out=sq4, in_=ps2, func=AF.Square, accum_out=stat[:, 32 + t:33 + t]),
                  reads=[psk(b0), psk(b0 + 1), "stat"], writes=["sq4", ("sst", t)])
            P.add("act", lambda e, t=t: e.activation(out=stat[:, 48 + t:49 + t], in_=stat[:, 32 + t:33 + t], func=AF.Sqrt,
                                                     bias=epsr[:, 0:1], scale=1.0 / D),
                  reads=[("sst", t), "epsr"], writes=[("ssd", t)])
            P.add("dve", lambda e, t=t: e.reciprocal(out=stat[:, 48 + t:49 + t], in_=stat[:, 48 + t:49 + t]),
                  reads=[("ssd", t)], writes=[("rs4", t)])
            P.add("dve", lambda e, t=t, ps2=ps2: e.scalar_tensor_tensor(
                out=tmp4[t % 2], in0=ps2, scalar=stat[:, 48 + t:49 + t], in1=gB, op0=ALU.mult, op1=ALU.mult),
                reads=[psk(b0), psk(b0 + 1), ("rs4", t), "gB"], writes=[("tmp4", t % 2)])
            P.add("pool", lambda e, t=t: e.tensor_tensor(out=tmp4[t % 2], in0=tmp4[t % 2], in1=xt4[t % 2], op=ALU.add),
                  reads=[("tmp4", t % 2), ("xt4", t % 2)], writes=[("tmp4", t % 2)])
            P.add("sp", lambda e, t=t: e.dma_start(out=x1v[t], in_=tmp4[t % 2]), reads=[("tmp4", t % 2)],
                  writes=[("x1s", t)], dma=True)
            if t < 4:
                P.add("act", lambda e, t=t: e.activation(out=hb4[t], in_=tmp4[t % 2], func=AF.Square, accum_out=stat[:, t:t + 1]),
                      reads=[("tmp4", t % 2), "stat"], writes=[("hb4", t), ("s5", t)])
                P.add("act", lambda e, t=t: e.activation(out=stat[:, 8 + t:9 + t], in_=stat[:, t:t + 1], func=AF.Sqrt,
                                                         bias=epsr[:, 0:1], scale=1.0 / D),
                      reads=[("s5", t), "epsr"], writes=[("sd5", t)])
                P.add("dve", lambda e, t=t: e.reciprocal(out=stat[:, 8 + t:9 + t], in_=stat[:, 8 + t:9 + t]),
                      reads=[("sd5", t)], writes=[("r5", t)])
                P.add("dve", lambda e, t=t: e.scalar_tensor_tensor(
                    out=hb4[t], in0=tmp4[t % 2], scalar=stat[:, 8 + t:9 + t], in1=gC, op0=ALU.mult, op1=ALU.mult),
                    reads=[("tmp4", t % 2), ("r5", t), "gC"], writes=[("hb4", t)])
                pend_tr.append(t)

        w1_cast(7)
        P.barrier()
        if debug:
            x1dv = dbg["x1"]
            for t in range(16):
                dbg_ops.append(P.add("sp", lambda e, t=t: e.dma_start(out=x1dv[:, t, :], in_=x1v[t]), dma=True))
        L = Lay(ar, 2 * KB)
        aT = L.get(BF16, [32, 512])
        assert L.off <= BASE + 16 * KB, L.off
        L = Lay(ar, W2_END)
        gD = L.get(F32, [D])
        xa = [L.get(F32, [D]) for _ in range(2)]
        xb = [L.get(F32, [D]) for _ in range(2)]
        st5 = L.get(F32, [160])
        assert L.off <= ACC1_OFF, (L.off, ACC1_OFF)
        L = Lay(ar, P5A_OFF)
        RR_OFF = L.off
        rr_ = [L.get(F32, [512]) for _ in range(2)]
        ost = [L.get(F32, [D])] * 2
        sq5 = ar.view(RR_OFF, BF16, [D])
        assert L.off <= ARENA, L.off
        P.add("sp", lambda e: e.dma_start(out=gD, in_=gbc_d[3]), writes=["gD"], dma=True)
        P.add("dve", lambda e: e.memset(st5, 0.0), writes=["st5"])
        outv = out_d.rearrange("(t p) d -> t p d", p=128)
        final_ops = []
        cnt5 = dict(fb=0, rq=0)

        def prenorm_chain(t):
            P.add("act", lambda e: e.activation(out=hb5[0], in_=xa[t % 2], func=AF.Square, accum_out=st5[:, t:t + 1]),
                  reads=[("xa", t % 2), "st5"], writes=[("hb5", 0), ("s5", t)])
            P.add("act", lambda e: e.activation(out=st5[:, 16 + t:17 + t], in_=st5[:, t:t + 1], func=AF.Sqrt,
                                                bias=epsr[:, 0:1], scale=1.0 / D),
                  reads=[("s5", t), "epsr"], writes=[("sd5", t)])
            P.add("dve", lambda e: e.reciprocal(out=st5[:, 16 + t:17 + t], in_=st5[:, 16 + t:17 + t]),
                  reads=[("sd5", t)], writes=[("r5", t)])
            P.add("dve", lambda e: e.scalar_tensor_tensor(
                out=hb5[0], in0=xa[t % 2], scalar=st5[:, 16 + t:17 + t], in1=gC, op0=ALU.mult, op1=ALU.mult),
                reads=[("xa", t % 2), ("r5", t), "gC"], writes=[("hb5", 0)])

        def prenorm_transpose(t):
            j = t % 4
            for kc in range(8):
                P.add("pe", lambda e, kc=kc: e.transpose(
                    out=PB[7][:, kc * 128:(kc + 1) * 128], in_=hb5[0][:, kc * 128:(kc + 1) * 128], identity=ident),
                    reads=[("hb5", 0), "ident"], writes=[psk(7)])
            src = PB[7].rearrange("p (a b) -> p a b", b=128)
            dst = h2T[0][:, :, j * 128:(j + 1) * 128]
            P.add("act", lambda e: e.activation(out=dst, in_=src, func=AF.Copy), reads=[psk(7)], writes=[("h2T", 0, j)])

        def load_xa(t):
            P.add("sp", lambda e: e.dma_start(out=xa[t % 2], in_=x1v[t]), reads=[("x1s", t)], writes=[("xa", t % 2)], dma=True)

        def ff1_group(G):
            h2 = h2T[0]
            h2reads = [("h2T", 0, j) for j in range(4)]
            for F_ in range(32):
                if G == 0 and F_ % 4 == 0:
                    wloads[8 + F_ // 4]()
                bank = cnt5["fb"] % 3
                cnt5["fb"] += 1
                for kc in range(8):
                    P.add("pe", lambda e, F_=F_, kc=kc, bank=bank: e.matmul(
                        PF[bank], lhsT=w1[:, kc, F_ * 128:(F_ + 1) * 128], rhs=h2[:, kc, :], start=(kc == 0), stop=(kc == 7)),
                        reads=[("w1", F_ // 4)] + h2reads, writes=[psk(bank)])
                r_ = rr_[cnt5["rq"] % 2]
                rk = ("rr", cnt5["rq"] % 2)
                cnt5["rq"] += 1
                P.add("act", lambda e, r_=r_, bank=bank: e.activation(out=r_, in_=PF[bank], func=AF.Relu),
                      reads=[psk(bank)], writes=[rk])
                P.add("dve", lambda e, r_=r_, F_=F_: e.tensor_tensor(out=aT[:, F_, :], in0=r_, in1=r_, op=ALU.mult),
                      reads=[rk], writes=[("aT", F_)])

        def ff2_mm(t):
            j = t % 4
            for hh in range(2):
                bank = 3 + 2 * (t % 2) + hh
                for F_ in range(32):
                    P.add("pe", lambda e, F_=F_, hh=hh, bank=bank: e.matmul(
                        PF[bank], lhsT=aT[:, F_, j * 128:(j + 1) * 128], rhs=w2[:, F_, hh * 512:(hh + 1) * 512],
                        start=(F_ == 0), stop=(F_ == 31)), reads=[("aT", F_), ("w2", F_ // 4 * 4)], writes=[psk(bank)])

        def ff2_evac(t):
            b0 = 3 + 2 * (t % 2)
            ps2 = ps_all[:, b0 * 512:(b0 + 2) * 512]
            P.add("act", lambda e: e.activation(out=sq5, in_=ps2, func=AF.Square, accum_out=st5[:, 96 + t:97 + t]),
                  reads=[psk(b0), psk(b0 + 1), "st5"], writes=[("rr", 0), ("sft", t)])
            P.add("act", lambda e: e.activation(out=st5[:, 112 + t:113 + t], in_=st5[:, 96 + t:97 + t], func=AF.Sqrt,
                                                bias=epsr[:, 0:1], scale=1.0 / D),
                  reads=[("sft", t), "epsr"], writes=[("sfd", t)])
            P.add("dve", lambda e: e.reciprocal(out=st5[:, 112 + t:113 + t], in_=st5[:, 112 + t:113 + t]),
                  reads=[("sfd", t)], writes=[("rf", t)])
            P.add("dve", lambda e: e.scalar_tensor_tensor(
                out=ost[0], in0=ps2, scalar=st5[:, 112 + t:113 + t], in1=gD, op0=ALU.mult, op1=ALU.mult),
                reads=[psk(b0), psk(b0 + 1), ("rf", t), "gD"], writes=[("ost", 0)])
            P.add("pool", lambda e: e.tensor_tensor(out=ost[0], in0=ost[0], in1=xb[t % 2], op=ALU.add),
                  reads=[("ost", 0), ("xb", t % 2)], writes=[("ost", 0)])
            o = P.add("sp", lambda e: e.dma_start(out=outv[t], in_=ost[0]), reads=[("ost", 0)], writes=[("ost", 0)], dma=True)
            final_ops.append(o)

        def load_xb(t):
            P.add("sp", lambda e: e.dma_start(out=xb[t % 2], in_=x1v[t]), reads=[("x1s", t)], writes=[("xb", t % 2)], dma=True)

        for G in range(4):
            ff1_group(G)
            for j in range(4):
                t = 4 * G + j
                load_xb(t)
                if G < 3:
                    load_xa(t + 4)
                    prenorm_chain(t + 4)
                ff2_mm(t)
                if G < 3:
                    prenorm_transpose(t + 4)
                ff2_evac(t)

        if debug:
            P.barrier()
            for nm, buf in (("aT", aT), ("h2T", h2T[0]), ("w2", w2), ("w1", w1), ("w1s", w1s_d), ("w2s", w2s_d), ("wos", wos_d)):
                dbg_ops.append(P.add("sp", lambda e, nm=nm, buf=buf: e.dma_start(out=dbg[nm], in_=buf), dma=True))
        P.finalize(st)
        with nc.Block() as block:
            P.emit_all(block, final_ops + dbg_ops)
    return nc


def _t5_bucket(rel):
    half, max_exact = 16, 8
    ret = np.where(rel > 0, half, 0)
    n = np.abs(rel)
    nf = np.maximum(n, 1).astype(np.float32)
    large = max_exact + (np.log(nf / np.float32(max_exact)) / np.float32(math.log(1024 / max_exact))
                         * np.float32(half - max_exact)).astype(np.int32)
    large = np.minimum(large, half - 1)
    return ret + np.where(n < max_exact, n, large)


def _bias_tiles(rel_bias, sign):
    kl = np.arange(128)[:, None]
    ql = np.arange(256)[None, :]
    j = kl - ql + 64
    valid = np.abs(j) <= 64
    out = np.empty((128, 24, 256), np.float32)
    for br, dil in enumerate((1, 4, 16)):
        bidx = _t5_bucket((sign * dil * j).astype(np.int32))
        for h in range(8):
            tile = rel_bias[bidx, h]
            out[:, h * 3 + br, :] = np.where(valid, tile, np.float32(MASKV))
    return out


_NC_CACHE = {}


def kernel(x, g_pre_mix, w_in, sgu_ln_g, sgu_ln_b, sgu_w, sgu_b, w_out, g_post_mix,
           g_pre_ffn, w_ff1, w_ff2, g_post_ffn, rel_bias, _debug=None):
    f32 = np.float32
    x = np.asarray(x, f32)
    B = x.shape[0]
    key = "dbg" if _debug is not None else "nc"
    if key not in _NC_CACHE:
        _NC_CACHE[key] = build_nc(debug=_debug is not None)
    nc = _NC_CACHE[key]
    gbc = np.stack([np.broadcast_to(np.asarray(v, f32)[0][None, :], (128, D))
                    for v in (g_pre_mix, g_post_mix, g_pre_ffn, g_post_ffn)]).astype(f32)
    lng = np.asarray(sgu_ln_g, f32)[0]
    lnb = np.asarray(sgu_ln_b, f32)[0]
    Ws = np.asarray(sgu_w, f32)[0]
    bs = np.asarray(sgu_b, f32)[0]
    rb = np.asarray(rel_bias, f32)
    ident = np.eye(128, dtype=f32)
    pidx_g = (np.arange(4)[None, :] * 2 + (np.arange(128)[:, None] // 64))
    pidx_c = np.arange(128)[:, None] % 64
    feat = pidx_g * 64 + pidx_c
    lngF = np.broadcast_to(lng[feat][:, :, None], (128, 4, 128))
    lnbF = np.broadcast_to(lnb[feat][:, :, None], (128, 4, 128))
    in_maps = []
    for core in range(8):
        b, half = core // 2, core % 2
        if half == 0:
            idx = np.arange(NLOC)
            Wl, bl, sign = Ws, bs, 1
        else:
            idx = S - 1 - np.arange(NLOC)
            Wl, bl, sign = Ws[:, ::-1, ::-1], bs[:, ::-1], -1
        xl = np.ascontiguousarray(x[b][idx])
        wsT = np.ascontiguousarray(np.transpose(Wl, (2, 0, 1)))
        bsF = bl[pidx_g]
        sgf = np.stack([lngF.reshape(128, 512), lnbF.reshape(128, 512), bsF.reshape(128, 512)]).astype(f32)
        in_maps.append({
            "x": xl, "w_in": np.asarray(w_in, f32)[0], "w_out": np.asarray(w_out, f32)[0],
            "w_ff1": np.asarray(w_ff1, f32)[0], "w_ff2": np.asarray(w_ff2, f32)[0],
            "gbc": gbc, "wsT": wsT, "sgf": np.ascontiguousarray(sgf),
            "biasT": _bias_tiles(rb, sign), "ident": ident,
        })
    res = run_bass_kernel_spmd(nc, in_maps, core_ids=list(range(8)))
    if _debug is not None:
        _debug.extend(res.results)
    out = np.empty((B, S, D), f32)
    for core in range(8):
        b, half = core // 2, core % 2
        o = res.results[core]["out"]
        if half == 0:
            out[b, 0:NOWN] = o
        else:
            out[b, S - 1 - np.arange(NOWN)] = o
    return out
```
